# Optimizing a Trainium2 kernel written in Bass

```python
import jax
import jax.numpy as jnp
from jax import lax
import numpy as np

D_MODEL = 1024
BATCH = 8
SEQ = 4096
DEPTH = 1

GM_WIDTH = 1024
GM_GROUPS = 8
GM_GROUP_DIM = GM_WIDTH // GM_GROUPS
GM_CHUNK = 128

MLA_HEADS = 16
MLA_NOPE = 64
MLA_ROPE = 32
MLA_V = 64
MLA_Q_LORA = 384
MLA_KV_LORA = 256
ROPE_THETA = 10000.0
Q_BLOCK = 128

MEM_LEN = 256
MEM_HEADS = 4
MEM_HEAD_DIM = D_MODEL // MEM_HEADS

N_GROUPS = 8
EXPERTS_PER_GROUP = 8
N_EXPERTS = N_GROUPS * EXPERTS_PER_GROUP
TOP_K = 2
D_EXPERT = 256
MOE_BLOCK = 128

DEEPNORM_ALPHA = (2 * DEPTH) ** 0.25
DEEPNORM_BETA = (8 * DEPTH) ** -0.25
LN_EPS = 1e-5
RMS_EPS = 1e-6
MAX_POS_OFFSET = 2048

IN_SPLITS = (
    GM_WIDTH,
    2 * GM_WIDTH,
    2 * GM_WIDTH + MLA_Q_LORA,
    2 * GM_WIDTH + MLA_Q_LORA + MLA_KV_LORA,
    2 * GM_WIDTH + MLA_Q_LORA + MLA_KV_LORA + MLA_ROPE,
    2 * GM_WIDTH + MLA_Q_LORA + MLA_KV_LORA + MLA_ROPE + D_MODEL,
)
IN_COLS = IN_SPLITS[-1] + D_MODEL

kernel_name = "hybrid_gmlp_mla_memxattn_hmoe_deepnorm"


def layer_norm(x, g, b):
    xf = x.astype(jnp.float32)
    mu = jnp.mean(xf, axis=-1, keepdims=True)
    var = jnp.mean(jnp.square(xf - mu), axis=-1, keepdims=True)
    return ((xf - mu) * lax.rsqrt(var + LN_EPS) * g + b).astype(x.dtype)


def rms_norm(x, g):
    xf = x.astype(jnp.float32)
    return (xf * lax.rsqrt(jnp.mean(jnp.square(xf), axis=-1, keepdims=True) + RMS_EPS) * g).astype(x.dtype)


def rope(x, positions):
    half = MLA_ROPE // 2
    inv_freq = ROPE_THETA ** (-jnp.arange(half, dtype=jnp.float32) / half)
    ang = positions.astype(jnp.float32)[..., None] * inv_freq
    ang = ang.reshape(ang.shape[:2] + (1,) * (x.ndim - 3) + (half,))
    cos, sin = jnp.cos(ang), jnp.sin(ang)
    x1 = x[..., :half].astype(jnp.float32)
    x2 = x[..., half:].astype(jnp.float32)
    return jnp.concatenate([x1 * cos - x2 * sin, x2 * cos + x1 * sin], axis=-1).astype(x.dtype)


def gmlp_branch(u, v, ln_g, ln_b, w_s, b_s):
    B, S, _ = v.shape
    v = layer_norm(v, ln_g, ln_b)
    vc = v.reshape(B, S // GM_CHUNK, GM_CHUNK, GM_GROUPS, GM_GROUP_DIM)
    causal = jnp.tril(jnp.ones((GM_CHUNK, GM_CHUNK), dtype=bool))
    w = jnp.where(causal, w_s, 0).astype(v.dtype)
    mixed = jnp.einsum('gts,bcsgd->bctgd', w, vc) + b_s.T[None, None, :, :, None]
    return u * mixed.reshape(B, S, GM_WIDTH)


def mla_branch(c_q, c_kv, k_rope_raw, positions, q_norm_g, kv_norm_g, w_uq, w_uk, w_uv):
    B, S, _ = c_q.shape
    q = (rms_norm(c_q, q_norm_g) @ w_uq).reshape(B, S, MLA_HEADS, MLA_NOPE + MLA_ROPE)
    q_nope = q[..., :MLA_NOPE]
    q_rope = rope(q[..., MLA_NOPE:], positions)
    ckv = rms_norm(c_kv, kv_norm_g)
    k_nope = (ckv @ w_uk).reshape(B, S, MLA_HEADS, MLA_NOPE)
    v = (ckv @ w_uv).reshape(B, S, MLA_HEADS, MLA_V)
    k_rope = rope(k_rope_raw, positions)
    scale = (MLA_NOPE + MLA_ROPE) ** -0.5
    n_blocks = S // Q_BLOCK
    key_idx = jnp.arange(S)

    def to_blocks(t):
        return jnp.moveaxis(t.reshape((B, n_blocks, Q_BLOCK) + t.shape[2:]), 1, 0)

    def attend(args):
        qn, qr, blk = args
        s = (jnp.einsum('bqhd,bkhd->bhqk', qn, k_nope, preferred_element_type=jnp.float32)
             + jnp.einsum('bqhr,bkr->bhqk', qr, k_rope, preferred_element_type=jnp.float32)) * scale
        q_idx = blk * Q_BLOCK + jnp.arange(Q_BLOCK)
        s = jnp.where(key_idx[None, :] <= q_idx[:, None], s, -jnp.inf)
        p = jax.nn.softmax(s, axis=-1).astype(v.dtype)
        return jnp.einsum('bhqk,bkhd->bqhd', p, v)

    o = lax.map(attend, (to_blocks(q_nope), to_blocks(q_rope), jnp.arange(n_blocks)))
    return jnp.moveaxis(o, 0, 1).reshape(B, S, MLA_HEADS * MLA_V)


def memory_cross_attention(x, mem, w_mq, w_mk, w_mv, w_mo):
    B, S, _ = x.shape
    M = mem.shape[1]
    q = (x @ w_mq).reshape(B, S, MEM_HEADS, MEM_HEAD_DIM)
    k = (mem @ w_mk).reshape(B, M, MEM_HEADS, MEM_HEAD_DIM)
    v = (mem @ w_mv).reshape(B, M, MEM_HEADS, MEM_HEAD_DIM)
    s = jnp.einsum('bqhd,bmhd->bhqm', q, k, preferred_element_type=jnp.float32) * (MEM_HEAD_DIM ** -0.5)
    p = jax.nn.softmax(s, axis=-1).astype(v.dtype)
    o = jnp.einsum('bhqm,bmhd->bqhd', p, v).reshape(B, S, MEM_HEADS * MEM_HEAD_DIM)
    return o @ w_mo


def swiglu(xb, w_gate, w_up, w_down):
    return (jax.nn.silu(xb @ w_gate) * (xb @ w_up)) @ w_down


def hierarchical_moe(x, w_group_router, b_group_router, w_expert_router, b_expert_router,
                     w_exp_gate, w_exp_up, w_exp_down):
    B, S, D = x.shape
    T = B * S
    xf = x.reshape(T, D)
    g_logits = (xf @ w_group_router).astype(jnp.float32) + b_group_router
    g_prob = jax.nn.softmax(g_logits, axis=-1)
    g_sel = jnp.argmax(g_logits, axis=-1)
    g_w = jnp.take_along_axis(g_prob, g_sel[:, None], axis=-1)
    e_logits = ((xf @ w_expert_router).astype(jnp.float32) + b_expert_router).reshape(T, N_GROUPS, EXPERTS_PER_GROUP)
    e_logits = jnp.take_along_axis(e_logits, g_sel[:, None, None], axis=1)[:, 0]
    top_val, top_loc = lax.top_k(e_logits, TOP_K)
    top_w = jax.nn.softmax(top_val, axis=-1) * g_w
    top_e = g_sel[:, None] * EXPERTS_PER_GROUP + top_loc

    A = T * TOP_K
    flat_e = top_e.reshape(A)
    flat_tok = jnp.repeat(jnp.arange(T, dtype=jnp.int32), TOP_K)
    flat_w = top_w.reshape(A)
    order = jnp.argsort(flat_e)
    sorted_e, sorted_tok, sorted_w = flat_e[order], flat_tok[order], flat_w[order]
    counts = jnp.zeros((N_EXPERTS,), jnp.int32).at[flat_e].add(1)
    starts = jnp.cumsum(counts) - counts
    padded = (counts + MOE_BLOCK - 1) // MOE_BLOCK * MOE_BLOCK
    padded_ends = jnp.cumsum(padded)
    padded_starts = padded_ends - padded
    dest = padded_starts[sorted_e] + (jnp.arange(A, dtype=jnp.int32) - starts[sorted_e])
    P = A + N_EXPERTS * MOE_BLOCK
    n_blocks = P // MOE_BLOCK
    xd = jnp.zeros((P, D), x.dtype).at[dest].set(xf[sorted_tok])
    block_start = jnp.arange(n_blocks, dtype=jnp.int32) * MOE_BLOCK
    block_e = jnp.minimum(jnp.sum(padded_ends[None, :] <= block_start[:, None], axis=1), N_EXPERTS - 1)

    def run_block(args):
        xb, e = args
        return swiglu(xb, w_exp_gate[e], w_exp_up[e], w_exp_down[e])

    yd = lax.map(run_block, (xd.reshape(n_blocks, MOE_BLOCK, D), block_e)).reshape(P, D)
    contrib = yd[dest] * sorted_w[:, None].astype(yd.dtype)
    y = jnp.zeros((T, D), yd.dtype).at[sorted_tok].add(contrib)
    return y.reshape(B, S, D)


def hybrid_layer(x, mem, positions, w_in, b_in, gm_ln_g, gm_ln_b, gm_w_s, gm_b_s, w_gm_out,
                 mla_q_norm_g, mla_kv_norm_g, w_uq, w_uk, w_uv, w_mla_out, w_o, ln1_g, ln1_b,
                 w_mq, w_mk, w_mv, w_mo, ln2_g, ln2_b,
                 w_group_router, b_group_router, w_expert_router, b_expert_router,
                 w_exp_gate, w_exp_up, w_exp_down, ln3_g, ln3_b):
    proj = x @ w_in + b_in
    u, v, c_q, c_kv, k_rope_raw, gate_gm, gate_mla = jnp.split(proj, IN_SPLITS, axis=-1)
    y_gm = gmlp_branch(jax.nn.gelu(u, approximate=False), jax.nn.gelu(v, approximate=False),
                       gm_ln_g, gm_ln_b, gm_w_s, gm_b_s) @ w_gm_out
    y_mla = mla_branch(c_q, c_kv, k_rope_raw, positions, mla_q_norm_g, mla_kv_norm_g,
                       w_uq, w_uk, w_uv) @ w_mla_out
    merged = jax.nn.sigmoid(gate_gm) * y_gm + jax.nn.sigmoid(gate_mla) * y_mla
    x = layer_norm(DEEPNORM_ALPHA * x + merged @ w_o, ln1_g, ln1_b)
    x = layer_norm(DEEPNORM_ALPHA * x + memory_cross_attention(x, mem, w_mq, w_mk, w_mv, w_mo), ln2_g, ln2_b)
    y_moe = hierarchical_moe(x, w_group_router, b_group_router, w_expert_router, b_expert_router,
                             w_exp_gate, w_exp_up, w_exp_down)
    return layer_norm(DEEPNORM_ALPHA * x + y_moe, ln3_g, ln3_b)


def setup_inputs(seed: int = 0) -> dict:
    key = jax.random.key(seed)
    keys = jax.random.split(key, 40)
    counter = [0]

    def next_key():
        k = keys[counter[0]]
        counter[0] += 1
        return k

    def nrm(shape, scale):
        return jax.random.normal(next_key(), shape, jnp.float32) * scale

    def gain(shape):
        return 1.0 + nrm(shape, 0.02)

    L = DEPTH
    beta = DEEPNORM_BETA
    x = nrm((BATCH, SEQ, D_MODEL), 1.0)
    mem = nrm((BATCH, MEM_LEN, D_MODEL), 1.0)
    positions = (jnp.arange(SEQ, dtype=jnp.int32)[None, :]
                 + jax.random.randint(next_key(), (BATCH, 1), 0, MAX_POS_OFFSET, dtype=jnp.int32))
    return {
        "x": x,
        "mem": mem,
        "positions": positions,
        "w_in": nrm((L, D_MODEL, IN_COLS), D_MODEL ** -0.5),
        "b_in": nrm((L, IN_COLS), 0.02),
        "gm_ln_g": gain((L, GM_WIDTH)),
        "gm_ln_b": nrm((L, GM_WIDTH), 0.02),
        "gm_w_s": jnp.tril(nrm((L, GM_GROUPS, GM_CHUNK, GM_CHUNK), GM_CHUNK ** -0.5)),
        "gm_b_s": gain((L, GM_GROUPS, GM_CHUNK)),
        "w_gm_out": nrm((L, GM_WIDTH, D_MODEL), GM_WIDTH ** -0.5 * beta),
        "mla_q_norm_g": gain((L, MLA_Q_LORA)),
        "mla_kv_norm_g": gain((L, MLA_KV_LORA)),
        "w_uq": nrm((L, MLA_Q_LORA, MLA_HEADS * (MLA_NOPE + MLA_ROPE)), MLA_Q_LORA ** -0.5),
        "w_uk": nrm((L, MLA_KV_LORA, MLA_HEADS * MLA_NOPE), MLA_KV_LORA ** -0.5),
        "w_uv": nrm((L, MLA_KV_LORA, MLA_HEADS * MLA_V), MLA_KV_LORA ** -0.5 * beta),
        "w_mla_out": nrm((L, MLA_HEADS * MLA_V, D_MODEL), (MLA_HEADS * MLA_V) ** -0.5 * beta),
        "w_o": nrm((L, D_MODEL, D_MODEL), D_MODEL ** -0.5 * beta),
        "ln1_g": gain((L, D_MODEL)),
        "ln1_b": nrm((L, D_MODEL), 0.02),
        "w_mq": nrm((L, D_MODEL, MEM_HEADS * MEM_HEAD_DIM), D_MODEL ** -0.5),
        "w_mk": nrm((L, D_MODEL, MEM_HEADS * MEM_HEAD_DIM), D_MODEL ** -0.5),
        "w_mv": nrm((L, D_MODEL, MEM_HEADS * MEM_HEAD_DIM), D_MODEL ** -0.5 * beta),
        "w_mo": nrm((L, MEM_HEADS * MEM_HEAD_DIM, D_MODEL), D_MODEL ** -0.5 * beta),
        "ln2_g": gain((L, D_MODEL)),
        "ln2_b": nrm((L, D_MODEL), 0.02),
        "w_group_router": nrm((L, D_MODEL, N_GROUPS), D_MODEL ** -0.5),
        "b_group_router": nrm((L, N_GROUPS), 0.01),
        "w_expert_router": nrm((L, D_MODEL, N_EXPERTS), D_MODEL ** -0.5),
        "b_expert_router": nrm((L, N_EXPERTS), 0.01),
        "w_exp_gate": nrm((L, N_EXPERTS, D_MODEL, D_EXPERT), D_MODEL ** -0.5 * beta),
        "w_exp_up": nrm((L, N_EXPERTS, D_MODEL, D_EXPERT), D_MODEL ** -0.5 * beta),
        "w_exp_down": nrm((L, N_EXPERTS, D_EXPERT, D_MODEL), D_EXPERT ** -0.5 * beta),
        "ln3_g": gain((L, D_MODEL)),
        "ln3_b": nrm((L, D_MODEL), 0.02),
    }


def reference(x, mem, positions, w_in, b_in, gm_ln_g, gm_ln_b, gm_w_s, gm_b_s, w_gm_out,
              mla_q_norm_g, mla_kv_norm_g, w_uq, w_uk, w_uv, w_mla_out, w_o, ln1_g, ln1_b,
              w_mq, w_mk, w_mv, w_mo, ln2_g, ln2_b,
              w_group_router, b_group_router, w_expert_router, b_expert_router,
              w_exp_gate, w_exp_up, w_exp_down, ln3_g, ln3_b):
    h = x
    for l in range(DEPTH):
        h = hybrid_layer(h, mem, positions, w_in[l], b_in[l], gm_ln_g[l], gm_ln_b[l], gm_w_s[l], gm_b_s[l],
                         w_gm_out[l], mla_q_norm_g[l], mla_kv_norm_g[l], w_uq[l], w_uk[l], w_uv[l],
                         w_mla_out[l], w_o[l], ln1_g[l], ln1_b[l],
                         w_mq[l], w_mk[l], w_mv[l], w_mo[l], ln2_g[l], ln2_b[l],
                         w_group_router[l], b_group_router[l], w_expert_router[l], b_expert_router[l],
                         w_exp_gate[l], w_exp_up[l], w_exp_down[l], ln3_g[l], ln3_b[l])
    return h
```

```python
import numpy as np
import concourse.bass as bass
import concourse.mybir as mybir
from concourse.bass_utils import run_bass_kernel_spmd

F32 = mybir.dt.float32
BF16 = mybir.dt.bfloat16
I32 = mybir.dt.int32
AF = mybir.ActivationFunctionType
ALU = mybir.AluOpType
AX = mybir.AxisListType

NDMA = 24
ENGS = ("tensor", "vector", "scalar", "gpsimd", "sync")

S_TOK = 4096
D = 1024
NT = S_TOK // 128
IN_COLS = 4768
C_U, C_V, C_CQ, C_CKV, C_KR, C_GG, C_GM = 0, 1024, 2048, 2432, 2688, 2720, 3744
NA = 2720
ALPHA = 2.0 ** 0.25
LN_EPS = 1e-5
RMS_EPS = 1e-6
PI = float(np.pi)
TWO_PI = float(2 * np.pi)


class Buf:
    __slots__ = ("name", "w", "r")

    def __init__(self, name=""):
        self.name = name
        self.w = None
        self.r = {}


class Sched:
    def __init__(self, nc):
        self.nc = nc
        self.eng = {}
        for n in ENGS:
            self.eng[n] = dict(sem=nc.alloc_semaphore("s_" + n), count=0, last=None,
                               seen={}, prog=[], cur=None)
        self.dma_sems = [nc.alloc_semaphore(f"dsem{i}") for i in range(NDMA)]
        self.dma_cnt = [0] * NDMA
        self.dma_rr = 0
        self.n_ops = 0

    def _need(self, en, key, val, kind):
        E = self.eng[en]
        if key[0] == "e":
            X = self.eng[key[1]]
            if key[1] == en:
                if en == "tensor":
                    return
                if kind != "RAW":
                    return
            if X["count"] < val:
                assert X["count"] == val - 1 and X["last"] is not None and not X["last"]["signal"]
                X["last"]["signal"] = True
                X["count"] = val
        if E["seen"].get(key, 0) >= val:
            return
        E["seen"][key] = val
        E["cur"].append((key, val))

    def _deps(self, en, reads, writes):
        E = self.eng[en]
        E["cur"] = []
        for b in reads:
            if b.w is not None:
                self._need(en, b.w[0], b.w[1], "RAW")
        for b in writes:
            if b.w is not None:
                self._need(en, b.w[0], b.w[1], "WAW")
            for k, v in b.r.items():
                self._need(en, k, v, "WAR")
        return E["cur"]

    def op(self, en, fn, reads=(), writes=(), signal=False):
        E = self.eng[en]
        waits = self._deps(en, reads, writes)
        rec = dict(fn=fn, waits=waits, signal=False, dma=None)
        E["prog"].append(rec)
        val = E["count"] + 1
        key = ("e", en)
        for b in reads:
            if b.r.get(key, 0) < val:
                b.r[key] = val
        for b in writes:
            b.w = (key, val)
            b.r = {}
        E["last"] = rec
        if signal:
            rec["signal"] = True
            E["count"] = val
        self.n_ops += 1
        return rec

    def dma(self, qn, out, in_, reads=(), writes=(), fn=None, **kw):
        waits = self._deps(qn, reads, writes)
        E = self.eng[qn]
        i = self.dma_rr
        self.dma_rr = (i + 1) % NDMA
        key = ("d", i)
        prev = self.dma_cnt[i]
        if prev > 0:
            self._need(qn, key, prev, "RAW")
        val = prev + 16
        self.dma_cnt[i] = val
        if fn is None:
            fn = lambda e: e.dma_start(out=out, in_=in_, **kw)
        rec = dict(fn=fn, waits=waits, signal=False, dma=i)
        E["prog"].append(rec)
        for b in reads:
            b.r[key] = val
        for b in writes:
            b.w = (key, val)
            b.r = {}
        self.n_ops += 1
        return rec

    def barrier(self):
        targets = []
        for n in ENGS:
            X = self.eng[n]
            if X["last"] is not None and not X["last"]["signal"]:
                X["last"]["signal"] = True
                X["count"] += 1
            if X["count"] > 0:
                targets.append((("e", n), X["count"]))
        for i in range(NDMA):
            if self.dma_cnt[i] > 0:
                targets.append((("d", i), self.dma_cnt[i]))
        for n in ENGS:
            E = self.eng[n]
            waits = []
            for key, val in targets:
                if key == ("e", n):
                    continue
                if E["seen"].get(key, 0) >= val:
                    continue
                E["seen"][key] = val
                waits.append((key, val))
            if waits:
                E["prog"].append(dict(fn=None, waits=waits, signal=False, dma=None))

    def _sem(self, key):
        return self.eng[key[1]]["sem"] if key[0] == "e" else self.dma_sems[key[1]]

    def emit(self):
        nc = self.nc
        with nc.Block() as block:
            def mk(en):
                E = self.eng[en]

                def body(e):
                    for rec in E["prog"]:
                        for key, val in rec["waits"]:
                            e.wait_ge(self._sem(key), val)
                        if rec["fn"] is None:
                            continue
                        ins = rec["fn"](e)
                        if rec["dma"] is not None:
                            ins.then_inc(self.dma_sems[rec["dma"]], 16)
                        elif rec["signal"]:
                            ins.then_inc(E["sem"], 1)
                return body
            block.tensor(mk("tensor"))
            block.vector(mk("vector"))
            block.scalar(mk("scalar"))
            block.gpsimd(mk("gpsimd"))
            block.sync(mk("sync"))


class Arena:
    def __init__(self, nc, nbytes):
        self.t = nc.alloc_sbuf_tensor("arena", [128, nbytes // 2], BF16)
        self.A = self.t.ap()
        self.top = 0
        self.cap = nbytes

    def alloc(self, n_elem, dtype):
        sz = 2 if dtype == BF16 else 4
        nbytes = n_elem * sz
        off = (self.top + 63) // 64 * 64
        self.top = off + nbytes
        assert self.top <= self.cap, ("SBUF arena overflow", self.top, self.cap)
        v = self.A[:, off // 2:(off + nbytes) // 2]
        if dtype != BF16:
            v = v.bitcast(dtype)
        return v


class K:
    def __init__(self, nc):
        self.nc = nc
        self.S = Sched(nc)
        self.ar = Arena(nc, 206 * 1024)
        self.psbig = [nc.alloc_psum_tensor(f"psb{i}", [128, 1024], F32).ap() for i in range(4)]
        self.ps = [self.psbig[i // 2][:, (i % 2) * 512:(i % 2 + 1) * 512] for i in range(8)]
        self.psB = [Buf(f"ps{i}") for i in range(8)]

    def mm(self, out, lhsT, rhs, start, stop, reads, writes, signal=False):
        return self.S.op("tensor", lambda e: e.matmul(out, lhsT, rhs, start=start, stop=stop), reads, writes, signal)

    def tr(self, out, in_, ident, reads, writes, signal=False):
        return self.S.op("tensor", lambda e: e.transpose(out, in_, ident), reads, writes, signal)

    def act(self, out, in_, func, reads, writes, bias=None, scale=1.0, accum_out=None):
        kw = {}
        if bias is not None:
            kw["bias"] = bias
        if accum_out is not None:
            kw["accum_out"] = accum_out
        return self.S.op("scalar", lambda e: e.activation(out, in_, func, scale=scale, **kw), reads, writes)

    def tt(self, out, a, b, op, reads, writes, eng="vector"):
        return self.S.op(eng, lambda e: e.tensor_tensor(out, a, b, op), reads, writes)

    def ts(self, out, a, s1, s2, op0, op1, reads, writes, eng="vector"):
        if s2 is None:
            return self.S.op(eng, lambda e: e.tensor_scalar(out, a, s1, None, op0), reads, writes)
        return self.S.op(eng, lambda e: e.tensor_scalar(out, a, s1, s2, op0, op1), reads, writes)

    def stt(self, out, in0, scalar, in1, op0, op1, reads, writes, eng="vector"):
        return self.S.op(eng, lambda e: e.scalar_tensor_tensor(out, in0, scalar, in1, op0, op1), reads, writes)

    def cp(self, out, in_, reads, writes, eng="vector"):
        if eng == "scalar":
            return self.S.op(eng, lambda e: e.copy(out, in_), reads, writes)
        return self.S.op(eng, lambda e: e.tensor_copy(out, in_), reads, writes)

    def memset(self, ap, val, writes, eng="vector"):
        return self.S.op(eng, lambda e: e.memset(ap, val), (), writes)

    def dma(self, q, out, in_, reads=(), writes=(), **kw):
        return self.S.dma(q, out, in_, reads, writes, **kw)


def build_nc(stage=99, debug=False):
    nc = bass.Bass("TRN2", target_bir_lowering=False)

    in_names = []

    def din(name, shape, dt=F32):
        in_names.append(name)
        return nc.dram_tensor(name, list(shape), dt, kind="ExternalInput").ap()

    x = din("x", [S_TOK, D])
    mem = din("mem", [256, D])
    pos = din("positions", [1, S_TOK], I32)
    w_in = din("w_in", [D, IN_COLS])
    b_in = din("b_in", [1, IN_COLS])
    gm_ln_g = din("gm_ln_g", [1, 1024])
    gm_ln_b = din("gm_ln_b", [1, 1024])
    gm_w_s = din("gm_w_s", [8 * 128, 128])
    gm_b_s = din("gm_b_s", [1, 1024])
    w_gm_out = din("w_gm_out", [1024, 1024])
    q_norm_g = din("mla_q_norm_g", [1, 384])
    kv_norm_g = din("mla_kv_norm_g", [1, 256])
    w_uq = din("w_uq", [384, 1536])
    w_uk = din("w_uk", [256, 1024])
    w_uv = din("w_uv", [256, 1024])
    w_mla_out = din("w_mla_out", [1024, 1024])
    w_o = din("w_o", [1024, 1024])
    ln1_g = din("ln1_g", [1, 1024]); ln1_b = din("ln1_b", [1, 1024])
    w_mq = din("w_mq", [1024, 1024]); w_mk = din("w_mk", [1024, 1024])
    w_mv = din("w_mv", [1024, 1024]); w_mo = din("w_mo", [1024, 1024])
    ln2_g = din("ln2_g", [1, 1024]); ln2_b = din("ln2_b", [1, 1024])
    w_gr = din("w_group_router", [1024, 8]); b_gr = din("b_group_router", [1, 8])
    w_er = din("w_expert_router", [1024, 64]); b_er = din("b_expert_router", [1, 64])
    if stage >= 4:
        w_eg = din("w_exp_gate", [64 * 1024, 256]); w_eu = din("w_exp_up", [64 * 1024, 256])
        w_ed = din("w_exp_down", [64 * 256, 1024])
    ln3_g = din("ln3_g", [1, 1024]); ln3_b = din("ln3_b", [1, 1024])
    cst = din("consts", [128, 512])
    out = nc.dram_tensor("out", [S_TOK, D], F32, kind="ExternalOutput").ap()

    dk = dict(kind="ExternalOutput") if debug else {}
    ygm_d = nc.dram_tensor("ygm_d", [1024, S_TOK], BF16, **dk).ap()
    cqn_d = nc.dram_tensor("cqn_d", [384, S_TOK], BF16, **dk).ap()
    ckvn_d = nc.dram_tensor("ckvn_d", [256, S_TOK], BF16, **dk).ap()
    kr_d = nc.dram_tensor("kr_d", [32, S_TOK], BF16, **dk).ap()
    oT_d = nc.dram_tensor("oT_d", [1024, S_TOK], BF16).ap()

    dbg = {}
    if debug:
        for nm, shp in (("d_gated", [1024, S_TOK]),):
            dbg[nm] = nc.dram_tensor(nm, shp, F32, kind="ExternalOutput").ap()

    k = K(nc)
    S = k.S
    ar = k.ar
    ps, psB = k.ps, k.psB

    cst_sb = ar.alloc(512, F32); B_cst = Buf("cst")
    identf = cst_sb[:, 0:128]
    trif = cst_sb[:, 128:256]
    rc = cst_sb[:, 256:264]
    identb = ar.alloc(128, BF16); B_identb = Buf("identb")
    trib = ar.alloc(128, BF16); B_trib = Buf("trib")
    onesb = ar.alloc(128, BF16); B_onesb = Buf("onesb")
    cosT = ar.alloc(S_TOK, BF16); B_cos = Buf("cos")
    sinT = ar.alloc(S_TOK, BF16); B_sin = Buf("sin")
    persist_top = ar.top

    k.dma("sync", cst_sb, cst, writes=[B_cst])
    k.cp(identb, identf, [B_cst], [B_identb])
    k.cp(trib, trif, [B_cst], [B_trib])
    k.memset(onesb, 1.0, [B_onesb])

    p1_base = ar.top
    winA = ar.alloc(8 * NA, BF16).rearrange("p (c n) -> p c n", c=8); B_winA = Buf("winA")
    wkr = ar.alloc(8 * 96, BF16).rearrange("p (c n) -> p c n", c=8); B_wkr = Buf("wkr")
    wkrs = ar.alloc(8 * 96, BF16).rearrange("p (c n) -> p c n", c=8); B_wkrs = Buf("wkrs")
    wgo = ar.alloc(8 * 1024, BF16).rearrange("p (c n) -> p c n", c=8); B_wgo = Buf("wgo")
    bcol = ar.alloc(21, F32); B_bcol = Buf("bcol")
    bkr = ar.alloc(2, F32); B_bkr = Buf("bkr")
    bv_b = ar.alloc(1024, F32); B_bvb = Buf("bvb")
    lng_col = ar.alloc(8, F32); lnb_col = ar.alloc(8, F32); B_lncol = Buf("lncol")
    bs_b = ar.alloc(1024, F32); B_bsb = Buf("bsb")
    BT = ar.alloc(1024, F32); B_BT = Buf("BT")
    wsT = ar.alloc(1024, BF16); B_wsT = Buf("wsT")
    p1_work = ar.top
    wsf = ar.alloc(1024, F32); B_wsf = Buf("wsf")
    posi = ar.alloc(S_TOK, I32); B_posi = Buf("posi")
    ang = ar.alloc(S_TOK, F32); B_ang = Buf("ang")
    ang2 = ar.alloc(S_TOK, F32); B_ang2 = Buf("ang2")
    ang3 = ar.alloc(S_TOK, F32); B_ang3 = Buf("ang3")
    import os
    PARTS = os.environ.get("KPARTS", "ABCD")
    if "A" in PARTS:
        k.dma("sync", posi[64:96, :], pos.partition_broadcast(32), writes=[B_posi])
        R = slice(64, 96)
        angi = posi
        C1 = 6.28125
        C2 = TWO_PI - C1
        k.cp(ang[R, :], posi[R, :], [B_posi], [B_ang])
        k.ts(ang[R, :], ang[R, :], rc[R, 0:1], None, ALU.mult, None, [B_ang, B_cst], [B_ang])
        for which in range(2):
            if which == 0:
                k.ts(ang2[R, :], ang[R, :], PI / 2, None, ALU.add, None, [B_ang], [B_ang2])
                src = ang2
                Bsrc = B_ang2
            else:
                src = ang
                Bsrc = B_ang
            k.ts(ang3[R, :], src[R, :], 1.0 / TWO_PI, None, ALU.mult, None, [Bsrc], [B_ang3])
            k.cp(angi[R, :], ang3[R, :], [B_ang3], [B_posi])
            k.cp(ang3[R, :], angi[R, :], [B_posi], [B_ang3])
            k.stt(src[R, :], ang3[R, :], -C1, src[R, :], ALU.mult, ALU.add, [B_ang3, Bsrc], [Bsrc])
            k.stt(src[R, :], ang3[R, :], -C2, src[R, :], ALU.mult, ALU.add, [B_ang3, Bsrc], [Bsrc])
            if which == 0:
                k.act(cosT[R, :], src[R, :], AF.Sin, [Bsrc], [B_cos])
            else:
                k.act(sinT[R, :], src[R, :], AF.Sin, [Bsrc, B_cst], [B_sin], scale=rc[R, 1:2])
    with nc.allow_non_contiguous_dma(reason="one-time small parameter layout loads"):
        if "B" in PARTS:
            for c0 in range(0, NA, 680):
                k.dma("gpsimd", winA[:, :, c0:c0 + 680], w_in[:, c0:c0 + 680].rearrange("(c p) n -> p c n", p=128), writes=[B_winA])
            k.memset(wkr.rearrange("p c n -> p (c n)"), 0.0, [B_wkr])
            k.memset(wkrs.rearrange("p c n -> p (c n)"), 0.0, [B_wkrs])
            k.dma("gpsimd", wkr[:, :, 64:96], w_in[:, C_KR:C_KR + 32].rearrange("(c p) n -> p c n", p=128), writes=[B_wkr])
            k.dma("gpsimd", wkrs[:, :, 64:80], w_in[:, C_KR + 16:C_KR + 32].rearrange("(c p) n -> p c n", p=128), writes=[B_wkrs])
            k.dma("gpsimd", wkrs[:, :, 80:96], w_in[:, C_KR:C_KR + 16].rearrange("(c p) n -> p c n", p=128), writes=[B_wkrs])
            k.dma("gpsimd", wgo, w_gm_out.rearrange("(c p) n -> p c n", p=128), writes=[B_wgo])
        if "C" in PARTS:
            k.dma("sync", bcol, b_in[0, 0:2688].rearrange("(c p) -> p c", p=128), writes=[B_bcol], allow_slow_non_contiguous=True)
            k.dma("sync", bkr[64:96, 0:1], b_in[0, C_KR:C_KR + 32].rearrange("(p o) -> p o", o=1), writes=[B_bkr], allow_slow_non_contiguous=True)
            k.dma("sync", bkr[64:80, 1:2], b_in[0, C_KR + 16:C_KR + 32].rearrange("(p o) -> p o", o=1), writes=[B_bkr], allow_slow_non_contiguous=True)
            k.dma("sync", bkr[80:96, 1:2], b_in[0, C_KR:C_KR + 16].rearrange("(p o) -> p o", o=1), writes=[B_bkr], allow_slow_non_contiguous=True)
            k.dma("sync", bv_b, b_in[:, C_V:C_V + 1024].partition_broadcast(128), writes=[B_bvb])
            k.dma("sync", lng_col, gm_ln_g[0, :].rearrange("(c p) -> p c", p=128), writes=[B_lncol], allow_slow_non_contiguous=True)
            k.dma("sync", lnb_col, gm_ln_b[0, :].rearrange("(c p) -> p c", p=128), writes=[B_lncol], allow_slow_non_contiguous=True)
            k.dma("sync", bs_b, gm_b_s.partition_broadcast(128), writes=[B_bsb])
            k.dma("sync", wsf.rearrange("p (g s) -> p g s", g=8), gm_w_s.rearrange("(g t) s -> t g s", t=128), writes=[B_wsf])

    if "D" in PARTS:
        for g in range(8):
            bank = g // 4
            k.tr(ps[bank][:, (g % 4) * 128:(g % 4 + 1) * 128], wsf[:, g * 128:(g + 1) * 128], identf, [B_wsf, B_cst], [psB[bank]], signal=(g % 4 == 3))
        for bank in range(2):
            for j in range(4):
                g = bank * 4 + j
                k.tt(wsT[:, g * 128:(g + 1) * 128], ps[bank][:, j * 128:(j + 1) * 128], trif, ALU.mult, [psB[bank], B_cst], [B_wsT])
        for bank in range(2):
            k.mm(ps[2 + bank], onesb, wsT[:, bank * 512:(bank + 1) * 512], True, True, [B_onesb, B_wsT], [psB[2 + bank]], signal=True)
        for g in range(8):
            bank = 2 + g // 4
            k.stt(BT[:, g * 128:(g + 1) * 128], ps[bank][:, (g % 4) * 128:(g % 4 + 1) * 128], lnb_col[:, g:g + 1], bs_b[:, g * 128:(g + 1) * 128],
                  ALU.mult, ALU.add, [psB[bank], B_lncol, B_bsb], [B_BT])

    S.barrier()
    ar.top = p1_work
    TB = 512
    xbf = [ar.alloc(1024, BF16) for _ in range(4)]; B_xbf = [Buf() for _ in range(4)]
    xT = ar.alloc(8 * TB, BF16).rearrange("p (c n) -> p c n", c=8); B_xT = Buf("xT")
    uT = ar.alloc(8 * TB, BF16).rearrange("p (c n) -> p c n", c=8); B_uT = Buf("uT")
    gT = ar.alloc(8 * TB, BF16).rearrange("p (c n) -> p c n", c=8); B_gT = Buf("gT")
    ygT = ar.alloc(8 * TB, BF16).rearrange("p (c n) -> p c n", c=8); B_ygT = Buf("ygT")
    vf = [ar.alloc(1024, F32) for _ in range(2)]; B_vf = [Buf("vf0"), Buf("vf1")]
    vn = [ar.alloc(1024, BF16) for _ in range(2)]; B_vn = [Buf("vn0"), Buf("vn1")]
    stats = ar.alloc(2 * 6 + 8, F32); B_st = Buf("stats")
    latf = ar.alloc(3 * TB, F32).rearrange("p (c n) -> p c n", c=3); B_latf = Buf("latf")
    latsq = ar.alloc(3 * TB, BF16).rearrange("p (c n) -> p c n", c=3); B_latsq = Buf("latsq")
    rstd_b = ar.alloc(TB, F32); B_rstd = Buf("rstd")
    krt = ar.alloc(2 * TB, F32); B_krt = Buf("krt")
    gtmp = ar.alloc(1024, F32); B_gtmp = [Buf("gtmp0"), Buf("gtmp1")]
    cqs = ar.alloc(3 * TB, BF16).rearrange("p (c n) -> p c n", c=3); B_cqs = Buf("cqs")
    ckvs = ar.alloc(2 * TB, BF16).rearrange("p (c n) -> p c n", c=2); B_ckvs = Buf("ckvs")
    krs = ar.alloc(TB, BF16); B_krs = Buf("krs")
    dbgf = ar.alloc(8 * TB, F32).rearrange("p (c n) -> p c n", c=8) if debug else None; B_dbgf = Buf("dbgf")

    pr = [0]

    def nextps(lo, hi):
        i = lo + pr[0] % (hi - lo)
        pr[0] += 1
        return i

    nblk = S_TOK // TB if stage >= 1 else 0
    for tb in range(nblk):
        t0 = tb * TB
        if tb == 0:
            for j in range(4):
                k.dma("gpsimd", xbf[j], x[j * 128:(j + 1) * 128, :], writes=[B_xbf[j]])
        for j in range(4):
            xb = xbf[j]; Bx = B_xbf[j]
            bank = j % 2
            psb16 = ps[bank].bitcast(BF16)
            for c in range(8):
                k.tr(psb16[:, c * 128:(c + 1) * 128], xb[:, c * 128:(c + 1) * 128], identb, [Bx, B_identb], [psB[bank]], signal=(c == 7))
            k.cp(xT[:, :, j * 128:(j + 1) * 128], psb16.rearrange("p (c n) -> p c n", c=8), [psB[bank]], [B_xT],
                 eng=("vector" if j % 2 == 0 else "scalar"))
        if tb + 1 < nblk:
            for j in range(4):
                k.dma("gpsimd", xbf[j], x[t0 + TB + j * 128:t0 + TB + (j + 1) * 128, :], writes=[B_xbf[j]])
        for oc in range(8):
            bank = 2 + oc % 4
            for c in range(8):
                k.mm(ps[bank], winA[:, c, C_U + oc * 128:C_U + (oc + 1) * 128], xT[:, c, :], c == 0, c == 7,
                     [B_winA, B_xT], [psB[bank]], signal=(c == 7))
            k.act(uT[:, oc, :], ps[bank], AF.Gelu, [psB[bank], B_bcol], [B_uT], bias=bcol[:, oc:oc + 1])
        for (c_off, nch, dst, Bdst, eps_n, dst_d) in ((C_CQ, 3, cqs, B_cqs, 384, cqn_d), (C_CKV, 2, ckvs, B_ckvs, 256, ckvn_d)):
            for oc in range(nch):
                bank = 2 + oc % 4
                for c in range(8):
                    k.mm(ps[bank], winA[:, c, c_off + oc * 128:c_off + (oc + 1) * 128], xT[:, c, :], c == 0, c == 7,
                         [B_winA, B_xT], [psB[bank]], signal=(c == 7))
                k.act(latf[:, oc, :], ps[bank], AF.Identity, [psB[bank], B_bcol], [B_latf], bias=bcol[:, c_off // 128 + oc:c_off // 128 + oc + 1])
                k.act(latsq[:, oc, :], latf[:, oc, :], AF.Square, [B_latf], [B_latsq])
            bank = 6
            for oc in range(nch):
                k.mm(ps[bank], onesb, latsq[:, oc, :], oc == 0, oc == nch - 1, [B_onesb, B_latsq], [psB[bank]], signal=(oc == nch - 1))
            k.act(rstd_b, ps[bank], AF.Sqrt, [psB[bank], B_cst], [B_rstd], bias=(rc[:, 5:6] if eps_n == 384 else rc[:, 6:7]))
            S.op("vector", lambda e: e.reciprocal(rstd_b, rstd_b), [B_rstd], [B_rstd])
            for oc in range(nch):
                k.tt(dst[:, oc, :], latf[:, oc, :], rstd_b, ALU.mult, [B_latf, B_rstd], [Bdst])
            k.dma("sync", dst_d[:, t0:t0 + TB].rearrange("(c p) t -> p c t", p=128), dst, reads=[Bdst])
        for (wk, bank) in ((wkr, 6), (wkrs, 7)):
            for c in range(8):
                k.mm(ps[bank][0:96, :], wk[:, c, :], xT[:, c, :], c == 0, c == 7, [B_wkr, B_wkrs, B_xT], [psB[bank]], signal=(c == 7))
        k.stt(krt[R, 0:TB], ps[6][R, :], bkr[R, 0:1], cosT[R, t0:t0 + TB], ALU.add, ALU.mult, [psB[6], B_bkr, B_cos], [B_krt])
        k.stt(krt[R, TB:2 * TB], ps[7][R, :], bkr[R, 1:2], sinT[R, t0:t0 + TB], ALU.add, ALU.mult, [psB[7], B_bkr, B_sin], [B_krt])
        k.tt(krs[R, :], krt[R, 0:TB], krt[R, TB:2 * TB], ALU.add, [B_krt], [B_krs])
        k.dma("sync", kr_d[:, t0:t0 + TB], krs[R, :], reads=[B_krs])
        def v_mm(j):
            vt = vf[j % 2]; Bv = B_vf[j % 2]
            for half in range(2):
                bank = 2 + (2 * j + half) % 4
                for c in range(8):
                    k.mm(ps[bank], xT[:, c, j * 128:(j + 1) * 128], winA[:, c, C_V + half * 512:C_V + (half + 1) * 512], c == 0, c == 7,
                         [B_xT, B_winA], [psB[bank]], signal=(c == 7))
                k.tt(vt[:, half * 512:(half + 1) * 512], ps[bank], bv_b[:, half * 512:(half + 1) * 512], ALU.add, [psB[bank], B_bvb], [Bv])

        v_mm(0)
        for j in range(4):
            vt = vf[j % 2]; Bv = B_vf[j % 2]
            vb = vn[j % 2]; Bvn = B_vn[j % 2]
            k.act(vt, vt, AF.Gelu, [Bv], [Bv])
            for half in range(2):
                S.op("vector", lambda e, vt=vt, half=half: e.bn_stats(stats[:, half * 6:(half + 1) * 6], vt[:, half * 512:(half + 1) * 512]), [Bv], [B_st])
            S.op("vector", lambda e: e.bn_aggr(stats[:, 12:14], stats[:, 0:12]), [B_st], [B_st])
            k.act(stats[:, 14:15], stats[:, 13:14], AF.Sqrt, [B_st, B_cst], [B_st], bias=rc[:, 4:5])
            S.op("vector", lambda e: e.reciprocal(stats[:, 14:15], stats[:, 14:15]), [B_st], [B_st])
            k.ts(vb, vt, stats[:, 12:13], stats[:, 14:15], ALU.subtract, ALU.mult, [Bv, B_st], [Bvn])
            if j + 1 < 4:
                v_mm(j + 1)
            for gq in range(2):
                bank = 6 + gq
                for gg in range(4):
                    g = gq * 4 + gg
                    k.mm(ps[bank][:, gg * 128:(gg + 1) * 128], vb[:, g * 128:(g + 1) * 128], wsT[:, g * 128:(g + 1) * 128], True, True,
                         [Bvn, B_wsT], [psB[bank]], signal=(gg == 3))
                for gg in range(4):
                    g = gq * 4 + gg
                    k.stt(gtmp[:, gq * 512 + gg * 128:gq * 512 + (gg + 1) * 128], ps[bank][:, gg * 128:(gg + 1) * 128], lng_col[:, g:g + 1], BT[:, g * 128:(g + 1) * 128],
                          ALU.mult, ALU.add, [psB[bank], B_lncol, B_BT], [B_gtmp[gq]])
                k.tt(gT[:, gq * 4:(gq + 1) * 4, j * 128:(j + 1) * 128], gtmp[:, gq * 512:(gq + 1) * 512].rearrange("p (g t) -> p g t", g=4),
                     uT[:, gq * 4:(gq + 1) * 4, j * 128:(j + 1) * 128], ALU.mult, [B_gtmp[gq], B_uT], [B_gT], eng="gpsimd")
        for oc in range(8):
            bank = 2 + oc % 4
            for c in range(8):
                k.mm(ps[bank], wgo[:, c, oc * 128:(oc + 1) * 128], gT[:, c, :], c == 0, c == 7, [B_wgo, B_gT], [psB[bank]], signal=(c == 7))
            k.cp(ygT[:, oc, :], ps[bank], [psB[bank]], [B_ygT], eng=("vector" if oc % 2 == 0 else "scalar"))
        k.dma("sync", ygm_d[:, t0:t0 + TB].rearrange("(c p) t -> p c t", p=128), ygT, reads=[B_ygT])
        if debug:
            k.cp(dbgf.rearrange("p c n -> p (c n)"), gT.rearrange("p c n -> p (c n)"), [B_gT], [B_dbgf])
            k.dma("sync", dbg["d_gated"][:, t0:t0 + TB].rearrange("(c p) t -> p c t", p=128), dbgf, reads=[B_dbgf])


    def layer_norm_tile(h, Bh, outt, Bout, g_b, b_b, Bgb, st, Bst):
        for half in range(2):
            S.op("vector", lambda e, half=half: e.bn_stats(st[:, half * 6:(half + 1) * 6], h[:, half * 512:(half + 1) * 512]), [Bh], [Bst])
        S.op("vector", lambda e: e.bn_aggr(st[:, 12:14], st[:, 0:12]), [Bst], [Bst])
        k.act(st[:, 14:15], st[:, 13:14], AF.Sqrt, [Bst, B_cst], [Bst], bias=rc[:, 4:5])
        S.op("vector", lambda e: e.reciprocal(st[:, 14:15], st[:, 14:15]), [Bst], [Bst])
        k.ts(h, h, st[:, 12:13], st[:, 14:15], ALU.subtract, ALU.mult, [Bh, Bst], [Bh])
        k.tt(h, h, g_b, ALU.mult, [Bh, Bgb], [Bh])
        k.tt(outt, h, b_b, ALU.add, [Bh, Bgb], [Bout])

    _bc = {}

    def bcreg(e):
        if "r" not in _bc:
            _bc["r"] = e.to_reg(64 * 256 - 1)
        return _bc["r"]

    x1_d = nc.dram_tensor("x1_d", [S_TOK, D], F32, **dk).ap()
    x2_d = nc.dram_tensor("x2_d", [S_TOK, D], F32, **dk).ap()
    x2b_d = nc.dram_tensor("x2b_d", [S_TOK, D], BF16).ap()
    NSLOT = 64 * 256
    tokof_d = nc.dram_tensor("tokof_d", [NSLOT, 2], I32).ap()
    yd_d = nc.dram_tensor("yd_d", [NSLOT, D], BF16).ap()

    if stage >= 2:
        S.barrier()
        ar.top = persist_top
        wuq = ar.alloc(3 * 1536, BF16).rearrange("p (c n) -> p c n", c=3); B_wuq = Buf("wuq")
        wuqs = ar.alloc(3 * 1536, BF16).rearrange("p (c n) -> p c n", c=3); B_wuqs = Buf("wuqs")
        wuk = ar.alloc(2 * 1024, BF16).rearrange("p (c n) -> p c n", c=2); B_wuk = Buf("wuk")
        wuv = ar.alloc(2 * 1024, BF16).rearrange("p (c n) -> p c n", c=2); B_wuv = Buf("wuv")
        gcol = ar.alloc(8, F32); B_gcol = Buf("gcol")
        cqnT = ar.alloc(3 * S_TOK, BF16).rearrange("p (c n) -> p c n", c=3); B_cqn = Buf("cqn")
        ckvnT = ar.alloc(2 * S_TOK, BF16).rearrange("p (c n) -> p c n", c=2); B_ckvn = Buf("ckvn")
        KT = [ar.alloc(S_TOK, BF16) for _ in range(2)]; B_KT = [Buf("kt0"), Buf("kt1")]
        QT = [ar.alloc(S_TOK, BF16) for _ in range(2)]; B_QT = [Buf("qt0"), Buf("qt1")]
        VAf = [ar.alloc(NT * 65 + 64, BF16) for _ in range(2)]
        VA = [v_[:, 0:NT * 65].rearrange("p (t n) -> p t n", n=65) for v_ in VAf]; B_VA = [Buf("va0"), Buf("va1")]
        NP = 4
        pT = [ar.alloc(1024, BF16) for _ in range(NP)]; B_pT = [Buf(f"pT{i}") for i in range(NP)]
        rtmp = [ar.alloc(512, F32) for _ in range(2)]; B_rtmp = [Buf("rt0"), Buf("rt1")]
        rrec2 = [ar.alloc(512, F32) for _ in range(2)]; B_rrec2 = [Buf("rrec0"), Buf("rrec1")]
        bcs = ar.alloc(512, F32); B_bcs = Buf("bcs")
        onrm = [ar.alloc(512, BF16) for _ in range(2)]; B_onrm = [Buf("on0"), Buf("on1")]
        wst = ar.alloc(1536, F32); B_wst = Buf("wst")

        with nc.allow_non_contiguous_dma(reason="small parameter columns"):
            k.dma("sync", gcol[:, 0:3], q_norm_g[0, :].rearrange("(c p) -> p c", p=128), writes=[B_gcol], allow_slow_non_contiguous=True)
            k.dma("sync", gcol[:, 3:5], kv_norm_g[0, :].rearrange("(c p) -> p c", p=128), writes=[B_gcol], allow_slow_non_contiguous=True)
        k.ts(gcol[:, 0:3], gcol[:, 0:3], float(np.sqrt(384.0)), None, ALU.mult, None, [B_gcol], [B_gcol])
        k.ts(gcol[:, 3:5], gcol[:, 3:5], float(np.sqrt(256.0)), None, ALU.mult, None, [B_gcol], [B_gcol])
        for c in range(3):
            k.dma("sync", wst[:, 0:1536], w_uq[c * 128:(c + 1) * 128, :], writes=[B_wst])
            k.ts(wuq[:, c, :], wst[:, 0:1536], gcol[:, c:c + 1], None, ALU.mult, None, [B_wst, B_gcol], [B_wuq])
        for (wdst, Bw, wsrc) in ((wuk, B_wuk, w_uk), (wuv, B_wuv, w_uv)):
            for c in range(2):
                k.dma("sync", wst[:, 0:1024], wsrc[c * 128:(c + 1) * 128, :], writes=[B_wst])
                k.ts(wdst[:, c, :], wst[:, 0:1024], gcol[:, 3 + c:4 + c], None, ALU.mult, None, [B_wst, B_gcol], [Bw])
        k.memset(wuqs.rearrange("p c n -> p (c n)"), 0.0, [B_wuqs])
        for c in range(3):
            srcv = wuq[:, c, :].rearrange("p (h j) -> p h j", j=96)
            dstv = wuqs[:, c, :].rearrange("p (h j) -> p h j", j=96)
            k.cp(dstv[:, :, 64:80], srcv[:, :, 80:96], [B_wuq], [B_wuqs])
            k.cp(dstv[:, :, 80:96], srcv[:, :, 64:80], [B_wuq], [B_wuqs])
        k.dma("sync", cqnT, cqn_d.rearrange("(c p) t -> p c t", p=128), writes=[B_cqn])
        k.dma("sync", ckvnT, ckvn_d.rearrange("(c p) t -> p c t", p=128), writes=[B_ckvn])
        for b in range(2):
            k.memset(KT[b][64:128, :], 0.0, [B_KT[b]])
            k.memset(QT[b][64:128, :], 0.0, [B_QT[b]])
            k.dma("sync", KT[b][64:96, :], kr_d, writes=[B_KT[b]])
            k.memset(VAf[b][:, NT * 65:NT * 65 + 64], 0.0, [B_VA[b]])
            k.memset(VA[b][:, :, 64:65], 1.0, [B_VA[b]])
        maskD = ar.alloc(4 * 512, BF16); B_maskD = Buf("maskD")
        k.memset(maskD, 1.0, [B_maskD])
        for i_ in range(4):
            if i_ > 0:
                k.memset(maskD[:, i_ * 512:i_ * 512 + i_ * 128], 0.0, [B_maskD])
            k.cp(maskD[:, i_ * 512 + i_ * 128:i_ * 512 + (i_ + 1) * 128], trib, [B_trib, B_maskD], [B_maskD])
        SCALE = float(96.0 ** -0.5)
        NH = 16 if stage >= 2 else 0

        gcount = [0]
        SG = [k.psbig[1], k.psbig[2], k.psbig[3]]
        B_SG = [Buf("sg0"), Buf("sg1"), Buf("sg2")]

        busy = set()

        def next_group():
            for _ in range(3):
                g = gcount[0] % 3
                gcount[0] += 1
                if g not in busy:
                    return g
            raise AssertionError("no free PSUM group")

        def gen_chunks(h):
            b = h % 2
            chunks = []

            def q_chunk(qb):
                c0 = qb * 512
                g = next_group()
                G = SG[g]
                for c in range(3):
                    k.mm(G[0:96, 0:512], wuq[:, c, h * 96:(h + 1) * 96], cqnT[:, c, c0:c0 + 512], c == 0, c == 2, [B_wuq, B_cqn], [B_SG[g]])
                for c in range(3):
                    k.mm(G[0:96, 512:1024], wuqs[:, c, h * 96:(h + 1) * 96], cqnT[:, c, c0:c0 + 512], c == 0, c == 2, [B_wuqs, B_cqn], [B_SG[g]], signal=(c == 2))
                k.cp(QT[b][0:64, c0:c0 + 512], G[0:64, 0:512], [B_SG[g]], [B_QT[b]])
                k.tt(rtmp[0][R, :], G[R, 0:512], cosT[R, c0:c0 + 512], ALU.mult, [B_SG[g], B_cos], [B_rtmp[0]])
                k.tt(rtmp[1][R, :], G[R, 512:1024], sinT[R, c0:c0 + 512], ALU.mult, [B_SG[g], B_sin], [B_rtmp[1]])
                k.tt(QT[b][R, c0:c0 + 512], rtmp[0][R, :], rtmp[1][R, :], ALU.add, [B_rtmp[0], B_rtmp[1]], [B_QT[b]], eng="gpsimd")

            def k_chunk(qq):
                g = next_group()
                G = SG[g]
                for hf in range(2):
                    c0 = (qq * 2 + hf) * 512
                    for c in range(2):
                        k.mm(G[0:64, hf * 512:(hf + 1) * 512], wuk[:, c, h * 64:(h + 1) * 64], ckvnT[:, c, c0:c0 + 512], c == 0, c == 1, [B_wuk, B_ckvn], [B_SG[g]],
                             signal=(c == 1 and hf == 1))
                k.cp(KT[b][0:64, qq * 1024:(qq + 1) * 1024], G[0:64, :], [B_SG[g]], [B_KT[b]])

            def v_chunk(tg):
                g = next_group()
                G = SG[g]
                for i in range(16):
                    t = tg * 16 + i
                    for c in range(2):
                        k.mm(G[:, i * 64:(i + 1) * 64], ckvnT[:, c, t * 128:(t + 1) * 128], wuv[:, c, h * 64:(h + 1) * 64], c == 0, c == 1,
                             [B_ckvn, B_wuv], [B_SG[g]], signal=(c == 1 and i == 15))
                k.cp(VA[b][:, tg * 16:(tg + 1) * 16, 0:64], G.rearrange("p (t n) -> p t n", n=64), [B_SG[g]], [B_VA[b]])

            for qb in range(8):
                chunks.append(lambda qb=qb: q_chunk(qb))
            for qq in range(4):
                chunks.append(lambda qq=qq: k_chunk(qq))
            for tg in range(2):
                chunks.append(lambda tg=tg: v_chunk(tg))
            return chunks

        pcount = [0]
        ocount = [0]

        def attn_head(h, pending):
            b = h % 2
            pairs = [(qb, p) for qb in range(8) for p in range(2 * qb + 2)]
            ob_of = {}
            for qb in range(8):
                ob_of[qb] = ocount[0] % 2
                ocount[0] += 1
            grp = {}

            def offs(qb, kt):
                return 0

            def qk(i):
                qb, p = pairs[i]
                q0 = qb * 512
                g = next_group()
                busy.add(g)
                grp[i] = g
                for hf in range(2):
                    kt = 2 * p + hf
                    off = offs(qb, kt)
                    k.mm(SG[g][:, hf * 512 + off:hf * 512 + 512], KT[b][0:128, kt * 128:(kt + 1) * 128], QT[b][0:128, q0 + off:q0 + 512], True, True,
                         [B_KT[b], B_QT[b]], [B_SG[g]], signal=(hf == 1))

            def epi_a(qb):
                ob = ob_of[qb]
                rr = rrec2[qb % 2]
                S.op("vector", lambda e, ob=ob, rr=rr: e.reciprocal(rr[64:65, :], ps[ob][64:65, :]), [psB[ob]], [B_rrec2[qb % 2]])

            def epilogue(qb):
                ob = ob_of[qb]
                q0 = qb * 512
                rrec = rrec2[qb % 2]
                B_rrec = B_rrec2[qb % 2]
                g = next_group()
                k.mm(SG[g][0:64, 0:512], trif[64:65, 64:128], rrec[64:65, :], True, True, [B_cst, B_rrec], [B_SG[g]], signal=True)
                k.cp(bcs[0:64, :], SG[g][0:64, 0:512], [B_SG[g]], [B_bcs])
                oj = ocount[0] % 2
                ocount[0] += 1
                k.tt(onrm[oj][0:64, :], ps[ob][0:64, :], bcs[0:64, :], ALU.mult, [psB[ob], B_bcs], [B_onrm[oj]])
                k.dma("sync", oT_d[h * 64:(h + 1) * 64, q0:q0 + 512], onrm[oj][0:64, :], reads=[B_onrm[oj]])

            due = []
            since = 0
            qk(0)
            qk(1)
            for i, (qb, p) in enumerate(pairs):
                if i + 2 < len(pairs):
                    qk(i + 2)
                nkt = 4 * qb + 4
                g = grp[i]
                pj = pcount[0] % NP
                pcount[0] += 1
                diag = (2 * p >= 4 * qb)
                k.act(pT[pj], SG[g], AF.Exp, [B_SG[g]], [B_pT[pj]], scale=SCALE)
                if diag:
                    i0_ = 2 * p - 4 * qb
                    k.tt(pT[pj], pT[pj], maskD[:, i0_ * 512:i0_ * 512 + 1024], ALU.mult, [B_pT[pj], B_maskD], [B_pT[pj]], eng="gpsimd")
                ob = ob_of[qb]
                for hf in range(2):
                    kt = 2 * p + hf
                    off = offs(qb, kt)
                    k.mm(ps[ob][0:128, off:512], VAf[b][:, kt * 65:kt * 65 + 128], pT[pj][:, hf * 512 + off:hf * 512 + 512], kt == 0, kt == nkt - 1, [B_VA[b], B_pT[pj]], [psB[ob]],
                         signal=(kt == nkt - 1))
                busy.discard(g)
                if p == 2 * qb + 1:
                    epi_a(qb)
                    due.append((i + 4, qb))
                while due and due[0][0] <= i:
                    epilogue(due.pop(0)[1])
                since += 1
                if pending and since >= 5:
                    since = 0
                    pending.pop(0)()
            while due:
                epilogue(due.pop(0)[1])

        if NH:
            for ch in gen_chunks(0):
                ch()
        for h in range(NH):
            pending = gen_chunks(h + 1) if h + 1 < NH else []
            attn_head(h, pending)
            while pending:
                pending.pop(0)()

    if stage >= 3:
        S.barrier()
        ar.top = persist_top
        wing = ar.alloc(8 * 2048, BF16).rearrange("p (c n) -> p c n", c=8); B_wing = Buf("wing")
        wml = ar.alloc(8 * 1024, BF16).rearrange("p (c n) -> p c n", c=8); B_wml = Buf("wml")
        wo_sb = ar.alloc(8 * 1024, BF16).rearrange("p (c n) -> p c n", c=8); B_wo = Buf("wo")
        bgcol = ar.alloc(16, F32); B_bgcol = Buf("bgcol")
        l1g = ar.alloc(1024, F32); l1b = ar.alloc(1024, F32); B_l1 = Buf("l1")
        xbf = [ar.alloc(1024, BF16) for _ in range(4)]; B_xbf = [Buf() for _ in range(4)]
        xT = ar.alloc(8 * 512, BF16).rearrange("p (c n) -> p c n", c=8); B_xT = Buf("xT")
        xf = [ar.alloc(1024, F32) for _ in range(4)]; B_xf = [Buf() for _ in range(4)]
        sgT = ar.alloc(8 * 512, BF16).rearrange("p (c n) -> p c n", c=8); B_sgT = Buf("sgT")
        smT = ar.alloc(8 * 512, BF16).rearrange("p (c n) -> p c n", c=8); B_smT = Buf("smT")
        ygb2 = [ar.alloc(8 * 512, BF16).rearrange("p (c n) -> p c n", c=8) for _ in range(2)]; B_ygb2 = [Buf(), Buf()]
        oTb2 = [ar.alloc(8 * 512, BF16).rearrange("p (c n) -> p c n", c=8) for _ in range(2)]; B_oTb2 = [Buf(), Buf()]
        mT = ar.alloc(8 * 512, BF16).rearrange("p (c n) -> p c n", c=8); B_mT = Buf("mT")
        t1 = [ar.alloc(512, F32) for _ in range(2)]; B_t1 = [Buf("t1a"), Buf("t1b")]
        t2 = [ar.alloc(512, F32) for _ in range(2)]; B_t2 = [Buf("t2a"), Buf("t2b")]
        h1 = [ar.alloc(1024, F32) for _ in range(2)]; B_h1 = [Buf("h1a"), Buf("h1b")]
        st3 = ar.alloc(16, F32); B_st3 = Buf("st3")
        with nc.allow_non_contiguous_dma(reason="small parameter columns"):
            for c0 in range(0, 2048, 512):
                k.dma("gpsimd", wing[:, :, c0:c0 + 512], w_in[:, C_GG + c0:C_GG + c0 + 512].rearrange("(c p) n -> p c n", p=128), writes=[B_wing])
            k.dma("gpsimd", wml, w_mla_out.rearrange("(c p) n -> p c n", p=128), writes=[B_wml])
            k.dma("gpsimd", wo_sb, w_o.rearrange("(c p) n -> p c n", p=128), writes=[B_wo])
            k.dma("sync", bgcol, b_in[0, C_GG:C_GG + 2048].rearrange("(c p) -> p c", p=128), writes=[B_bgcol], allow_slow_non_contiguous=True)
        k.dma("sync", l1g, ln1_g.partition_broadcast(128), writes=[B_l1])
        k.dma("sync", l1b, ln1_b.partition_broadcast(128), writes=[B_l1])
        def p3a_big_loads(tb):
            t0_ = tb * 512
            k.dma("sync", ygb2[tb % 2], ygm_d[:, t0_:t0_ + 512].rearrange("(c p) t -> p c t", p=128), writes=[B_ygb2[tb % 2]])
            k.dma("sync", oTb2[tb % 2], oT_d[:, t0_:t0_ + 512].rearrange("(c p) t -> p c t", p=128), writes=[B_oTb2[tb % 2]])

        p3a_big_loads(0)
        for j in range(4):
            k.dma("gpsimd", xbf[j], x[j * 128:(j + 1) * 128, :], writes=[B_xbf[j]])
        for tb in range(8):
            t0 = tb * 512
            ygb = ygb2[tb % 2]; B_ygb = B_ygb2[tb % 2]
            oTb = oTb2[tb % 2]; B_oTb = B_oTb2[tb % 2]
            if tb + 1 < 8:
                p3a_big_loads(tb + 1)
            for j in range(4):
                k.dma("sync", xf[j], x[t0 + j * 128:t0 + (j + 1) * 128, :], writes=[B_xf[j]])
            for j in range(4):
                xb = xbf[j]; Bx = B_xbf[j]
                bank = j % 2
                psb16 = ps[bank].bitcast(BF16)
                for c in range(8):
                    k.tr(psb16[:, c * 128:(c + 1) * 128], xb[:, c * 128:(c + 1) * 128], identb, [Bx, B_identb], [psB[bank]], signal=(c == 7))
                k.cp(xT[:, :, j * 128:(j + 1) * 128], psb16.rearrange("p (c n) -> p c n", c=8), [psB[bank]], [B_xT])
            if tb + 1 < 8:
                for j in range(4):
                    k.dma("gpsimd", xbf[j], x[t0 + 512 + j * 128:t0 + 512 + (j + 1) * 128, :], writes=[B_xbf[j]])
            for (dstT, Bd, coff) in ((sgT, B_sgT, 0), (smT, B_smT, 1024)):
                for oc in range(8):
                    bank = 2 + oc % 2
                    for c in range(8):
                        k.mm(ps[bank], wing[:, c, coff + oc * 128:coff + (oc + 1) * 128], xT[:, c, :], c == 0, c == 7, [B_wing, B_xT], [psB[bank]], signal=(c == 7))
                    k.act(dstT[:, oc, :], ps[bank], AF.Sigmoid, [psB[bank], B_bgcol], [Bd], bias=bgcol[:, coff // 128 + oc:coff // 128 + oc + 1])
            for oc in range(8):
                bank = 4 + oc % 2
                for c in range(8):
                    k.mm(ps[bank], wml[:, c, oc * 128:(oc + 1) * 128], oTb[:, c, :], c == 0, c == 7, [B_wml, B_oTb], [psB[bank]], signal=(c == 7))
                k.tt(t1[oc % 2], ps[bank], smT[:, oc, :], ALU.mult, [psB[bank], B_smT], [B_t1[oc % 2]])
                k.tt(t2[oc % 2], sgT[:, oc, :], ygb[:, oc, :], ALU.mult, [B_sgT, B_ygb], [B_t2[oc % 2]], eng="gpsimd")
                k.tt(mT[:, oc, :], t1[oc % 2], t2[oc % 2], ALU.add, [B_t1[oc % 2], B_t2[oc % 2]], [B_mT])
            for j in range(4):
                xt = xf[j]; Bxf = B_xf[j]
                hh = h1[j % 2]; Bh = B_h1[j % 2]
                for half in range(2):
                    bank = 6 + half
                    for c in range(8):
                        k.mm(ps[bank], mT[:, c, j * 128:(j + 1) * 128], wo_sb[:, c, half * 512:(half + 1) * 512], c == 0, c == 7, [B_mT, B_wo], [psB[bank]], signal=(c == 7))
                    k.stt(hh[:, half * 512:(half + 1) * 512], xt[:, half * 512:(half + 1) * 512], ALPHA, ps[bank], ALU.mult, ALU.add, [Bxf, psB[bank]], [Bh])
                layer_norm_tile(hh, Bh, hh, Bh, l1g, l1b, B_l1, st3, B_st3)
                k.dma("sync", x1_d[t0 + j * 128:t0 + (j + 1) * 128, :], hh, reads=[Bh])

    if stage >= 4:
        S.barrier()
        ar.top = persist_top
        wmq_sb = ar.alloc(8 * 1024, BF16).rearrange("p (c n) -> p c n", c=8); B_wmq = Buf("wmq")
        wmo_sb = ar.alloc(8 * 1024, BF16).rearrange("p (c n) -> p c n", c=8); B_wmo = Buf("wmo")
        wkv = ar.alloc(8 * 1024, BF16).rearrange("p (c n) -> p c n", c=8); B_wkv = Buf("wkv")
        memb = ar.alloc(2 * 1024, BF16).rearrange("p (c n) -> p c n", c=2); B_memb = Buf("memb")
        memT = ar.alloc(8 * 256, BF16).rearrange("p (c n) -> p c n", c=8); B_memT = Buf("memT")
        mKT = ar.alloc(8 * 256, BF16).rearrange("p (c n) -> p c n", c=8); B_mKT = Buf("mKT")
        mV = ar.alloc(2 * 1024, BF16).rearrange("p (c n) -> p c n", c=2); B_mV = Buf("mV")
        wr = ar.alloc(8 * 72, F32).rearrange("p (c n) -> p c n", c=8); B_wr = Buf("wr")
        br_b = ar.alloc(72, F32); B_brb = Buf("brb")
        l2g = ar.alloc(1024, F32); l2b = ar.alloc(1024, F32); B_l2 = Buf("l2")
        eidx = ar.alloc(64, F32); B_eidx = Buf("eidx")
        tokid = ar.alloc(NT * 2, I32); B_tokid = Buf("tokid")
        dest_all = ar.alloc(NT * 2, I32); B_dest = Buf("dest")
        w_all = ar.alloc(NT * 2, F32); B_wall = Buf("wall")
        moe_persist = ar.top
        xbf = [ar.alloc(1024, BF16) for _ in range(4)]; B_xbf = [Buf() for _ in range(4)]
        xT = ar.alloc(8 * 512, BF16).rearrange("p (c n) -> p c n", c=8); B_xT = Buf("xT")
        xf = [ar.alloc(1024, F32) for _ in range(4)]; B_xf = [Buf() for _ in range(4)]
        q2T = ar.alloc(8 * 512, BF16).rearrange("p (c n) -> p c n", c=8); B_q2T = Buf("q2T")
        p2T = [ar.alloc(512, BF16) for _ in range(2)]; B_p2T = [Buf("p2a"), Buf("p2b")]
        rb = ar.alloc(512, F32); B_rb = Buf("rb")
        o2T = ar.alloc(8 * 512, BF16).rearrange("p (c n) -> p c n", c=8); B_o2T = Buf("o2T")
        h2 = [ar.alloc(1024, F32) for _ in range(2)]; B_h2 = [Buf("h2a"), Buf("h2b")]
        x2b = [ar.alloc(1024, BF16) for _ in range(2)]; B_x2b = [Buf("x2ba"), Buf("x2bb")]
        x2T = ar.alloc(8 * 128, F32).rearrange("p (c n) -> p c n", c=8); B_x2T = Buf("x2T")
        st3 = ar.alloc(16, F32); B_st3 = Buf("st3")
        lg = ar.alloc(72, F32); B_lg = Buf("lg")
        rt = ar.alloc(256, F32); B_rt = Buf("rt")
        Mt = ar.alloc(64, BF16); B_Mt = Buf("Mt")
        M12 = ar.alloc(128, F32); B_M12 = Buf("M12")
        base = ar.alloc(64, F32); B_base = Buf("base")
        rank = ar.alloc(64, F32); B_rank = Buf("rank")
        tris = ar.alloc(128, BF16); B_tris = Buf("tris")
        zt = ar.alloc(256, I32); B_zt = Buf("zt")

        with nc.allow_non_contiguous_dma(reason="small parameter columns"):
            k.dma("gpsimd", wmq_sb, w_mq.rearrange("(c p) n -> p c n", p=128), writes=[B_wmq])
            k.dma("gpsimd", wmo_sb, w_mo.rearrange("(c p) n -> p c n", p=128), writes=[B_wmo])
            k.dma("gpsimd", wkv, w_mk.rearrange("(c p) n -> p c n", p=128), writes=[B_wkv])
            k.dma("gpsimd", memb, mem.rearrange("(c p) n -> p c n", p=128), writes=[B_memb])
            k.dma("sync", wr[:, :, 0:8], w_gr.rearrange("(c p) n -> p c n", p=128), writes=[B_wr], allow_slow_non_contiguous=True)
            k.dma("sync", wr[:, :, 8:72], w_er.rearrange("(c p) n -> p c n", p=128), writes=[B_wr], allow_slow_non_contiguous=True)
            k.dma("sync", br_b[:, 0:8], b_gr.partition_broadcast(128), writes=[B_brb])
            k.dma("sync", br_b[:, 8:72], b_er.partition_broadcast(128), writes=[B_brb])
        k.dma("sync", l2g, ln2_g.partition_broadcast(128), writes=[B_l2])
        k.dma("sync", l2b, ln2_b.partition_broadcast(128), writes=[B_l2])
        k.cp(eidx, cst_sb[:, 264:328], [B_cst], [B_eidx])
        k.cp(tris, cst_sb[:, 328:456], [B_cst], [B_tris])
        k.cp(tokid.rearrange("p (t o) -> p t o", o=2), cst_sb[:, 456:488].unsqueeze(2).to_broadcast([128, NT, 2]), [B_cst], [B_tokid])
        k.memset(base, 0.0, [B_base])
        k.memset(zt, 0, [B_zt])
        k.dma("sync", tokof_d.rearrange("(p n) o -> p (n o)", p=128), zt, reads=[B_zt])
        B_tokof = Buf("tokof")
        for mt in range(2):
            psb16 = ps[mt].bitcast(BF16)
            for c in range(8):
                k.tr(psb16[:, c * 128:(c + 1) * 128], memb[:, mt, c * 128:(c + 1) * 128], identb, [B_memb, B_identb], [psB[mt]], signal=(c == 7))
            k.cp(memT[:, :, mt * 128:(mt + 1) * 128], psb16.rearrange("p (c n) -> p c n", c=8), [psB[mt]], [B_memT])
        for oc in range(8):
            bank = 2 + oc % 2
            for c in range(8):
                k.mm(ps[bank][:, 0:256], wkv[:, c, oc * 128:(oc + 1) * 128], memT[:, c, :], c == 0, c == 7, [B_wkv, B_memT], [psB[bank]], signal=(c == 7))
            k.cp(mKT[:, oc, :], ps[bank][:, 0:256], [psB[bank]], [B_mKT])
        k.dma("gpsimd", wkv, w_mv.rearrange("(c p) n -> p c n", p=128), reads=[], writes=[B_wkv])
        for mt in range(2):
            for half in range(2):
                bank = 4 + half
                for c in range(8):
                    k.mm(ps[bank], memT[:, c, mt * 128:(mt + 1) * 128], wkv[:, c, half * 512:(half + 1) * 512], c == 0, c == 7, [B_memT, B_wkv], [psB[bank]], signal=(c == 7))
                k.cp(mV[:, mt, half * 512:(half + 1) * 512], ps[bank], [psB[bank]], [B_mV])

        for j in range(4):
            k.dma("gpsimd", xbf[j], x1_d[j * 128:(j + 1) * 128, :], writes=[B_xbf[j]])
        for tb in range(8):
            t0 = tb * 512
            for j in range(4):
                k.dma("sync", xf[j], x1_d[t0 + j * 128:t0 + (j + 1) * 128, :], writes=[B_xf[j]])
            for j in range(4):
                xb = xbf[j]; Bx = B_xbf[j]
                bank = j % 2
                psb16 = ps[bank].bitcast(BF16)
                for c in range(8):
                    k.tr(psb16[:, c * 128:(c + 1) * 128], xb[:, c * 128:(c + 1) * 128], identb, [Bx, B_identb], [psB[bank]], signal=(c == 7))
                k.cp(xT[:, :, j * 128:(j + 1) * 128], psb16.rearrange("p (c n) -> p c n", c=8), [psB[bank]], [B_xT])
            if tb + 1 < 8:
                for j in range(4):
                    k.dma("gpsimd", xbf[j], x1_d[t0 + 512 + j * 128:t0 + 512 + (j + 1) * 128, :], writes=[B_xbf[j]])
            for oc in range(8):
                bank = 2 + oc % 2
                for c in range(8):
                    k.mm(ps[bank], wmq_sb[:, c, oc * 128:(oc + 1) * 128], xT[:, c, :], c == 0, c == 7, [B_wmq, B_xT], [psB[bank]], signal=(c == 7))
                k.cp(q2T[:, oc, :], ps[bank], [psB[bank]], [B_q2T], eng="scalar")
            for hm in range(4):
                for mt in range(2):
                    for dc in range(2):
                        k.mm(ps[4 + mt], mKT[:, hm * 2 + dc, mt * 128:(mt + 1) * 128], q2T[:, hm * 2 + dc, :], dc == 0, dc == 1, [B_mKT, B_q2T], [psB[4 + mt]], signal=(dc == 1))
                    k.act(p2T[mt], ps[4 + mt], AF.Exp, [psB[4 + mt]], [B_p2T[mt]], scale=1.0 / 16.0)
                for mt in range(2):
                    k.mm(ps[6], onesb, p2T[mt], mt == 0, mt == 1, [B_onesb, B_p2T[mt]], [psB[6]], signal=(mt == 1))
                S.op("vector", lambda e: e.reciprocal(rb, ps[6]), [psB[6]], [B_rb])
                for dvc in range(2):
                    bank = 2 + dvc
                    for mt in range(2):
                        k.mm(ps[bank], mV[:, mt, hm * 256 + dvc * 128:hm * 256 + (dvc + 1) * 128], p2T[mt], mt == 0, mt == 1, [B_mV, B_p2T[mt]], [psB[bank]], signal=(mt == 1))
                    k.tt(o2T[:, hm * 2 + dvc, :], ps[bank], rb, ALU.mult, [psB[bank], B_rb], [B_o2T])
            for j in range(4):
                ti = tb * 4 + j
                xt = xf[j]; Bxf = B_xf[j]
                hh = h2[j % 2]; Bh = B_h2[j % 2]
                xb2 = x2b[j % 2]; Bxb2 = B_x2b[j % 2]
                for half in range(2):
                    bank = 0 + half
                    for c in range(8):
                        k.mm(ps[bank], o2T[:, c, j * 128:(j + 1) * 128], wmo_sb[:, c, half * 512:(half + 1) * 512], c == 0, c == 7, [B_o2T, B_wmo], [psB[bank]], signal=(c == 7))
                    k.stt(hh[:, half * 512:(half + 1) * 512], xt[:, half * 512:(half + 1) * 512], ALPHA, ps[bank], ALU.mult, ALU.add, [Bxf, psB[bank]], [Bh])
                layer_norm_tile(hh, Bh, hh, Bh, l2g, l2b, B_l2, st3, B_st3)
                k.dma("sync", x2_d[t0 + j * 128:t0 + (j + 1) * 128, :], hh, reads=[Bh])
                k.cp(xb2, hh, [Bh], [Bxb2])
                k.dma("sync", x2b_d[t0 + j * 128:t0 + (j + 1) * 128, :], xb2, reads=[Bxb2])
                for c in range(8):
                    k.tr(ps[7][:, (c % 4) * 128:(c % 4 + 1) * 128], hh[:, c * 128:(c + 1) * 128], identf, [Bh, B_cst], [psB[7]], signal=(c % 4 == 3))
                    if c % 4 == 3:
                        k.cp(x2T[:, c - 3:c + 1, :], ps[7].rearrange("p (c n) -> p c n", c=4), [psB[7]], [B_x2T])
                for c in range(8):
                    k.mm(ps[6][:, 0:72], x2T[:, c, :], wr[:, c, :], c == 0, c == 7, [B_x2T, B_wr], [psB[6]], signal=(c == 7))
                k.tt(lg, ps[6][:, 0:72], br_b, ALU.add, [psB[6], B_brb], [B_lg])
                V = lambda a, bb: rt[:, a:bb]
                RW = ([B_lg, B_rt, B_eidx], [B_rt])
                S.op("vector", lambda e: e.reduce_max(V(0, 1), lg[:, 0:8], AX.X), [B_lg], [B_rt])
                k.ts(V(8, 16), lg[:, 0:8], V(0, 1), None, ALU.subtract, None, *RW)
                k.act(V(16, 24), V(8, 16), AF.Exp, [B_rt], [B_rt], accum_out=V(1, 2))
                S.op("vector", lambda e: e.reciprocal(V(2, 3), V(1, 2)), [B_rt], [B_rt])
                k.ts(V(24, 32), V(8, 16), 0.0, None, ALU.is_equal, None, *RW)
                k.tt(V(64, 128).rearrange("p (g e) -> p g e", g=8), lg[:, 8:72].rearrange("p (g e) -> p g e", g=8),
                     V(24, 32).unsqueeze(2).to_broadcast([128, 8, 8]), ALU.mult, *RW)
                S.op("vector", lambda e: e.tensor_reduce(V(32, 40), V(64, 128).rearrange("p (g e) -> p e g", g=8), AX.X, ALU.add), [B_rt], [B_rt])
                S.op("vector", lambda e: e.reduce_max(V(3, 4), V(32, 40), AX.X), [B_rt], [B_rt])
                k.ts(V(40, 48), V(32, 40), V(3, 4), None, ALU.is_equal, None, *RW)
                k.stt(V(48, 56), V(40, 48), -1e30, V(32, 40), ALU.mult, ALU.add, *RW)
                S.op("vector", lambda e: e.reduce_max(V(4, 5), V(48, 56), AX.X), [B_rt], [B_rt])
                k.ts(V(56, 64), V(48, 56), V(4, 5), None, ALU.is_equal, None, *RW)
                k.tt(V(5, 6), V(4, 5), V(3, 4), ALU.subtract, *RW)
                k.act(V(6, 7), V(5, 6), AF.Exp, [B_rt], [B_rt])
                k.ts(V(7, 8), V(6, 7), 1.0, None, ALU.add, None, *RW)
                S.op("vector", lambda e: e.reciprocal(V(7, 8), V(7, 8)), [B_rt], [B_rt])
                k.tt(V(128, 129), V(7, 8), V(2, 3), ALU.mult, *RW)
                k.tt(V(129, 130), V(128, 129), V(6, 7), ALU.mult, *RW)
                for kk in range(2):
                    k.tt(M12[:, kk * 64:(kk + 1) * 64].rearrange("p (g e) -> p g e", g=8), V(24, 32).unsqueeze(2).to_broadcast([128, 8, 8]),
                         V(40 + 16 * kk, 48 + 16 * kk).unsqueeze(1).to_broadcast([128, 8, 8]), ALU.mult, [B_rt], [B_M12])
                k.tt(Mt, M12[:, 0:64], M12[:, 64:128], ALU.add, [B_M12], [B_Mt])
                k.mm(ps[6][:, 128:192], tris, Mt, True, True, [B_tris, B_Mt], [psB[6]], signal=True)
                k.tt(rank, ps[6][:, 128:192], base, ALU.add, [psB[6], B_base], [B_rank])
                k.mm(ps[6][:, 256:320], onesb, Mt, True, True, [B_onesb, B_Mt], [psB[6]], signal=True)
                k.tt(base, ps[6][:, 256:320], base, ALU.add, [psB[6], B_base], [B_base])
                for kk in range(2):
                    k.tt(V(130, 194), M12[:, kk * 64:(kk + 1) * 64], rank, ALU.mult, [B_M12, B_rank, B_rt], [B_rt])
                    S.op("vector", lambda e, kk=kk: e.reduce_sum(V(200 + kk, 201 + kk), V(130, 194), AX.X), [B_rt], [B_rt])
                    k.tt(V(130, 194), M12[:, kk * 64:(kk + 1) * 64], eidx, ALU.mult, [B_M12, B_eidx, B_rt], [B_rt])
                    S.op("vector", lambda e, kk=kk: e.reduce_sum(V(202 + kk, 203 + kk), V(130, 194), AX.X), [B_rt], [B_rt])
                    k.ts(V(204 + kk, 205 + kk), V(200 + kk, 201 + kk), 256.0, None, ALU.is_lt, None, *RW)
                    k.stt(V(206 + kk, 207 + kk), V(202 + kk, 203 + kk), 256.0, V(200 + kk, 201 + kk), ALU.mult, ALU.add, *RW)
                    k.ts(V(208 + kk, 209 + kk), V(204 + kk, 205 + kk), -1.0e6, 1.0e6, ALU.mult, ALU.add, *RW)
                    k.tt(V(206 + kk, 207 + kk), V(206 + kk, 207 + kk), V(208 + kk, 209 + kk), ALU.add, *RW)
                    k.cp(dest_all[:, ti * 2 + kk:ti * 2 + kk + 1], V(206 + kk, 207 + kk), [B_rt], [B_dest])
                    k.tt(w_all[:, ti * 2 + kk:ti * 2 + kk + 1], V(128 + kk, 129 + kk), V(204 + kk, 205 + kk), ALU.mult, [B_rt], [B_wall])
                    S.dma("gpsimd", None, None, reads=[B_dest, B_tokid, B_zt], writes=[B_tokof],
                          fn=lambda e, col=ti * 2 + kk, ti=ti: e.indirect_dma_start(
                              out=tokof_d, out_offset=bass.IndirectOffsetOnAxis(ap=dest_all[:, col:col + 1], axis=0),
                              in_=tokid[:, 2 * ti:2 * ti + 2], in_offset=None, bounds_check=bcreg(e), oob_is_err=False))

        S.barrier()
        ar.top = moe_persist
        l3g = ar.alloc(1024, F32); l3b = ar.alloc(1024, F32); B_l3 = Buf("l3")
        k.dma("sync", l3g, ln3_g.partition_broadcast(128), writes=[B_l3])
        k.dma("sync", l3b, ln3_b.partition_broadcast(128), writes=[B_l3])
        moe_work = ar.top
        NB = 4
        idx = [ar.alloc(2, I32) for _ in range(NB)]; B_idx = [Buf(f"idx{i}") for i in range(NB)]
        Xe = [ar.alloc(2 * 1024, BF16).rearrange("p (s n) -> p s n", s=2) for _ in range(NB)]; B_Xe = [Buf(f"Xe{i}") for i in range(NB)]
        XeT = [ar.alloc(8 * 256, BF16).rearrange("p (c n) -> p c n", c=8) for _ in range(NB)]; B_XeT = [Buf(f"XeT{i}") for i in range(NB)]
        wg = [ar.alloc(8 * 256, BF16).rearrange("p (c n) -> p c n", c=8) for _ in range(NB)]; B_wg = [Buf(f"wg{i}") for i in range(NB)]
        wu = [ar.alloc(8 * 256, BF16).rearrange("p (c n) -> p c n", c=8) for _ in range(NB)]; B_wu = [Buf(f"wu{i}") for i in range(NB)]
        wd = [ar.alloc(2 * 1024, BF16).rearrange("p (c n) -> p c n", c=2) for _ in range(NB)]; B_wd = [Buf(f"wd{i}") for i in range(NB)]
        sg = [ar.alloc(256, F32) for _ in range(2)]; B_sg = [Buf("sga"), Buf("sgb")]
        aT = [ar.alloc(2 * 256, BF16).rearrange("p (c n) -> p c n", c=2) for _ in range(NB)]; B_aT = [Buf(f"aT{i}") for i in range(NB)]
        yb = [ar.alloc(1024, BF16) for _ in range(2)]; B_yb = [Buf("yba"), Buf("ybb")]
        B_yd = Buf("yd")
        def moe_loads(ex):
            b = ex % NB
            for s_ in range(2):
                k.dma("sync", idx[b][:, s_:s_ + 1], tokof_d[ex * 256 + s_ * 128:ex * 256 + (s_ + 1) * 128, 0:1], reads=[B_tokof], writes=[B_idx[b]], allow_slow_non_contiguous=True)
            for s_ in range(2):
                S.dma("gpsimd", None, None, reads=[B_idx[b]], writes=[B_Xe[b]],
                      fn=lambda e, b=b, s_=s_: e.indirect_dma_start(
                          out=Xe[b][:, s_, :], out_offset=None, in_=x2b_d,
                          in_offset=bass.IndirectOffsetOnAxis(ap=idx[b][:, s_:s_ + 1], axis=0)))
            k.dma("gpsimd", wg[b], w_eg[ex * 1024:(ex + 1) * 1024, :].rearrange("(c p) n -> p c n", p=128), writes=[B_wg[b]])
            k.dma("gpsimd", wu[b], w_eu[ex * 1024:(ex + 1) * 1024, :].rearrange("(c p) n -> p c n", p=128), writes=[B_wu[b]])
            k.dma("gpsimd", wd[b], w_ed[ex * 256:(ex + 1) * 256, :].rearrange("(c p) n -> p c n", p=128), writes=[B_wd[b]])

        def moe_compute(ex):
            b = ex % NB
            for s_ in range(2):
                bank = s_
                psb16 = ps[bank].bitcast(BF16)
                for c in range(8):
                    k.tr(psb16[:, c * 128:(c + 1) * 128], Xe[b][:, s_, c * 128:(c + 1) * 128], identb, [B_Xe[b], B_identb], [psB[bank]], signal=(c == 7))
                k.cp(XeT[b][:, :, s_ * 128:(s_ + 1) * 128], psb16.rearrange("p (c n) -> p c n", c=8), [psB[bank]], [B_XeT[b]])
            for fc in range(2):
                for c in range(8):
                    k.mm(ps[2 + fc][:, 0:256], wg[b][:, c, fc * 128:(fc + 1) * 128], XeT[b][:, c, :], c == 0, c == 7, [B_wg[b], B_XeT[b]], [psB[2 + fc]], signal=(c == 7))
                for c in range(8):
                    k.mm(ps[2 + fc][:, 256:512], wu[b][:, c, fc * 128:(fc + 1) * 128], XeT[b][:, c, :], c == 0, c == 7, [B_wu[b], B_XeT[b]], [psB[2 + fc]], signal=(c == 7))
                k.act(sg[fc], ps[2 + fc][:, 0:256], AF.Silu, [psB[2 + fc]], [B_sg[fc]])
                k.tt(aT[b][:, fc, :], sg[fc], ps[2 + fc][:, 256:512], ALU.mult, [B_sg[fc], psB[2 + fc]], [B_aT[b]])
            for s_ in range(2):
                yy = yb[s_]
                for half in range(2):
                    bank = 4 + s_ * 2 + half
                    for fc in range(2):
                        k.mm(ps[bank], aT[b][:, fc, s_ * 128:(s_ + 1) * 128], wd[b][:, fc, half * 512:(half + 1) * 512], fc == 0, fc == 1, [B_aT[b], B_wd[b]], [psB[bank]], signal=(fc == 1))
                    k.cp(yy[:, half * 512:(half + 1) * 512], ps[bank], [psB[bank]], [B_yb[s_]], eng=("vector" if half == 0 else "scalar"))
                k.dma("sync", yd_d[ex * 256 + s_ * 128:ex * 256 + (s_ + 1) * 128, :], yy, reads=[B_yb[s_]])

        PF = 2
        for ex in range(PF):
            moe_loads(ex)
        for ex in range(64):
            if ex + PF < 64:
                moe_loads(ex + PF)
            moe_compute(ex)
        S.barrier()
        ar.top = moe_work
        NC4 = 4
        yg = [[ar.alloc(1024, BF16) for _ in range(2)] for _ in range(NC4)]; B_yg = [[Buf(), Buf()] for _ in range(NC4)]
        xf = [ar.alloc(1024, F32) for _ in range(NC4)]; B_xf = [Buf() for _ in range(NC4)]
        h3 = [ar.alloc(1024, F32) for _ in range(NC4)]; B_h3 = [Buf() for _ in range(NC4)]
        st4 = ar.alloc(16, F32); B_st4 = Buf("st4")
        def cmb_loads(ti):
            pb = ti % NC4
            k.dma("sync", xf[pb], x2_d[ti * 128:(ti + 1) * 128, :], writes=[B_xf[pb]])
            for kk in range(2):
                k.memset(yg[pb][kk], 0.0, [B_yg[pb][kk]], eng="gpsimd")
                S.dma("gpsimd", None, None, reads=[B_dest, B_yd], writes=[B_yg[pb][kk]],
                      fn=lambda e, pb=pb, kk=kk, col=ti * 2 + kk: e.indirect_dma_start(
                          out=yg[pb][kk], out_offset=None, in_=yd_d,
                          in_offset=bass.IndirectOffsetOnAxis(ap=dest_all[:, col:col + 1], axis=0),
                          bounds_check=bcreg(e), oob_is_err=False))

        def cmb_compute(ti):
            pb = ti % NC4
            hh = h3[pb]; Bh = B_h3[pb]
            k.ts(hh, yg[pb][0], w_all[:, ti * 2:ti * 2 + 1], None, ALU.mult, None, [B_yg[pb][0], B_wall], [Bh])
            k.stt(hh, yg[pb][1], w_all[:, ti * 2 + 1:ti * 2 + 2], hh, ALU.mult, ALU.add, [B_yg[pb][1], B_wall, Bh], [Bh])
            k.stt(hh, xf[pb], ALPHA, hh, ALU.mult, ALU.add, [B_xf[pb], Bh], [Bh])
            layer_norm_tile(hh, Bh, hh, Bh, l3g, l3b, B_l3, st4, B_st4)
            k.dma("sync", out[ti * 128:(ti + 1) * 128, :], hh, reads=[Bh])

        for ti in range(2):
            cmb_loads(ti)
        for ti in range(NT):
            if ti + 2 < NT:
                cmb_loads(ti + 2)
            cmb_compute(ti)

    S.barrier()
    S.emit()
    return nc, in_names


def make_consts():
    c = np.zeros((128, 512), np.float32)
    c[:, 0:128] = np.eye(128, dtype=np.float32)
    c[:, 128:256] = np.triu(np.ones((128, 128), np.float32))
    half = 16
    inv_freq = (10000.0 ** (-np.arange(half, dtype=np.float32) / half)).astype(np.float32)
    for p in range(64, 96):
        j = (p - 64) % 16
        c[p, 256] = inv_freq[j]
        first = (p - 64) < 16
        c[p, 257] = -1.0 if first else 1.0
    c[:, 264:328] = np.arange(64, dtype=np.float32)[None, :]
    c[:, 328:456] = np.triu(np.ones((128, 128), np.float32), k=1)
    c[:, 456:488] = (np.arange(32, dtype=np.float32)[None, :] * 128 + np.arange(128, dtype=np.float32)[:, None])
    c[:, 260] = LN_EPS
    c[:, 261] = 384 * RMS_EPS
    c[:, 262] = 256 * RMS_EPS
    return c


_CACHE = {}


def kernel(**inputs):
    n = 8
    if "nc" not in _CACHE:
        _CACHE["nc"] = build_nc(stage=4)[0]
    nc = _CACHE["nc"]
    consts = make_consts()
    shared = {}
    for kname, v in inputs.items():
        if kname in ("x", "mem", "positions"):
            continue
        a = np.ascontiguousarray(np.asarray(v)[0])
        if a.ndim == 1 or kname == "gm_b_s":
            a = a.reshape(1, -1)
        elif a.ndim == 3:
            a = a.reshape(-1, a.shape[-1])
        shared[kname] = a
    shared["consts"] = consts
    in_maps = []
    for b in range(n):
        m = dict(shared)
        m["x"] = np.ascontiguousarray(np.asarray(inputs["x"])[b])
        m["mem"] = np.ascontiguousarray(np.asarray(inputs["mem"])[b])
        m["positions"] = np.ascontiguousarray(np.asarray(inputs["positions"])[b]).reshape(1, -1).astype(np.int32)
        in_maps.append(m)
    res = run_bass_kernel_spmd(nc, in_maps, core_ids=list(range(n)))
    return np.stack([np.asarray(r["out"]) for r in res.results], axis=0).astype(np.float32)
```

```python
import numpy as np
import concourse.bass as bass
import concourse.mybir as mybir
from concourse.bass_utils import run_bass_kernel_spmd

F32 = mybir.dt.float32
BF16 = mybir.dt.bfloat16
I32 = mybir.dt.int32
AF = mybir.ActivationFunctionType
ALU = mybir.AluOpType
AX = mybir.AxisListType

NDMA = 24
ENGS = ("tensor", "vector", "scalar", "gpsimd", "sync")

S_TOK = 4096
D = 1024
NT = S_TOK // 128
IN_COLS = 4768
C_U, C_V, C_CQ, C_CKV, C_KR, C_GG, C_GM = 0, 1024, 2048, 2432, 2688, 2720, 3744
NA = 2720
ALPHA = 2.0 ** 0.25
LN_EPS = 1e-5
RMS_EPS = 1e-6
PI = float(np.pi)
TWO_PI = float(2 * np.pi)


class Buf:
    __slots__ = ("name", "w", "r")

    def __init__(self, name=""):
        self.name = name
        self.w = None
        self.r = {}


class Sched:
    def __init__(self, nc):
        self.nc = nc
        self.eng = {}
        for n in ENGS:
            self.eng[n] = dict(sem=nc.alloc_semaphore("s_" + n), count=0, last=None,
                               seen={}, prog=[], cur=None)
        self.dma_sems = [nc.alloc_semaphore(f"dsem{i}") for i in range(NDMA)]
        self.dma_cnt = [0] * NDMA
        self.dma_rr = 0
        self.n_ops = 0

    def _need(self, en, key, val, kind):
        E = self.eng[en]
        if key[0] == "e":
            X = self.eng[key[1]]
            if key[1] == en:
                if en == "tensor":
                    return
                if kind != "RAW":
                    return
            if X["count"] < val:
                assert X["count"] == val - 1 and X["last"] is not None and not X["last"]["signal"]
                X["last"]["signal"] = True
                X["count"] = val
        if E["seen"].get(key, 0) >= val:
            return
        E["seen"][key] = val
        E["cur"].append((key, val))

    def _deps(self, en, reads, writes):
        E = self.eng[en]
        E["cur"] = []
        for b in reads:
            if b.w is not None:
                self._need(en, b.w[0], b.w[1], "RAW")
        for b in writes:
            if b.w is not None:
                self._need(en, b.w[0], b.w[1], "WAW")
            for k, v in b.r.items():
                self._need(en, k, v, "WAR")
        return E["cur"]

    def op(self, en, fn, reads=(), writes=(), signal=False):
        E = self.eng[en]
        waits = self._deps(en, reads, writes)
        rec = dict(fn=fn, waits=waits, signal=False, dma=None)
        E["prog"].append(rec)
        val = E["count"] + 1
        key = ("e", en)
        for b in reads:
            if b.r.get(key, 0) < val:
                b.r[key] = val
        for b in writes:
            b.w = (key, val)
            b.r = {}
        E["last"] = rec
        if signal:
            rec["signal"] = True
            E["count"] = val
        self.n_ops += 1
        return rec

    def dma(self, qn, out, in_, reads=(), writes=(), fn=None, **kw):
        waits = self._deps(qn, reads, writes)
        E = self.eng[qn]
        i = self.dma_rr
        self.dma_rr = (i + 1) % NDMA
        key = ("d", i)
        prev = self.dma_cnt[i]
        if prev > 0:
            self._need(qn, key, prev, "RAW")
        val = prev + 16
        self.dma_cnt[i] = val
        if fn is None:
            fn = lambda e: e.dma_start(out=out, in_=in_, **kw)
        rec = dict(fn=fn, waits=waits, signal=False, dma=i)
        E["prog"].append(rec)
        for b in reads:
            b.r[key] = val
        for b in writes:
            b.w = (key, val)
            b.r = {}
        self.n_ops += 1
        return rec

    def barrier(self):
        targets = []
        for n in ENGS:
            X = self.eng[n]
            if X["last"] is not None and not X["last"]["signal"]:
                X["last"]["signal"] = True
                X["count"] += 1
            if X["count"] > 0:
                targets.append((("e", n), X["count"]))
        for i in range(NDMA):
            if self.dma_cnt[i] > 0:
                targets.append((("d", i), self.dma_cnt[i]))
        for n in ENGS:
            E = self.eng[n]
            waits = []
            for key, val in targets:
                if key == ("e", n):
                    continue
                if E["seen"].get(key, 0) >= val:
                    continue
                E["seen"][key] = val
                waits.append((key, val))
            if waits:
                E["prog"].append(dict(fn=None, waits=waits, signal=False, dma=None))

    def _sem(self, key):
        return self.eng[key[1]]["sem"] if key[0] == "e" else self.dma_sems[key[1]]

    def emit(self):
        nc = self.nc
        with nc.Block() as block:
            def mk(en):
                E = self.eng[en]

                def body(e):
                    for rec in E["prog"]:
                        for key, val in rec["waits"]:
                            e.wait_ge(self._sem(key), val)
                        if rec["fn"] is None:
                            continue
                        ins = rec["fn"](e)
                        if rec["dma"] is not None:
                            ins.then_inc(self.dma_sems[rec["dma"]], 16)
                        elif rec["signal"]:
                            ins.then_inc(E["sem"], 1)
                return body
            block.tensor(mk("tensor"))
            block.vector(mk("vector"))
            block.scalar(mk("scalar"))
            block.gpsimd(mk("gpsimd"))
            block.sync(mk("sync"))


class Arena:
    def __init__(self, nc, nbytes):
        self.t = nc.alloc_sbuf_tensor("arena", [128, nbytes // 2], BF16)
        self.A = self.t.ap()
        self.top = 0
        self.cap = nbytes

    def alloc(self, n_elem, dtype):
        sz = 2 if dtype == BF16 else 4
        nbytes = n_elem * sz
        off = (self.top + 63) // 64 * 64
        self.top = off + nbytes
        assert self.top <= self.cap, ("SBUF arena overflow", self.top, self.cap)
        v = self.A[:, off // 2:(off + nbytes) // 2]
        if dtype != BF16:
            v = v.bitcast(dtype)
        return v


class K:
    def __init__(self, nc):
        self.nc = nc
        self.S = Sched(nc)
        self.ar = Arena(nc, 206 * 1024)
        self.psbig = [nc.alloc_psum_tensor(f"psb{i}", [128, 1024], F32).ap() for i in range(4)]
        self.ps = [self.psbig[i // 2][:, (i % 2) * 512:(i % 2 + 1) * 512] for i in range(8)]
        self.psB = [Buf(f"ps{i}") for i in range(8)]

    def mm(self, out, lhsT, rhs, start, stop, reads, writes, signal=False):
        return self.S.op("tensor", lambda e: e.matmul(out, lhsT, rhs, start=start, stop=stop), reads, writes, signal)

    def tr(self, out, in_, ident, reads, writes, signal=False):
        return self.S.op("tensor", lambda e: e.transpose(out, in_, ident), reads, writes, signal)

    def act(self, out, in_, func, reads, writes, bias=None, scale=1.0, accum_out=None):
        kw = {}
        if bias is not None:
            kw["bias"] = bias
        if accum_out is not None:
            kw["accum_out"] = accum_out
        return self.S.op("scalar", lambda e: e.activation(out, in_, func, scale=scale, **kw), reads, writes)

    def tt(self, out, a, b, op, reads, writes, eng="vector"):
        return self.S.op(eng, lambda e: e.tensor_tensor(out, a, b, op), reads, writes)

    def ts(self, out, a, s1, s2, op0, op1, reads, writes, eng="vector"):
        if s2 is None:
            return self.S.op(eng, lambda e: e.tensor_scalar(out, a, s1, None, op0), reads, writes)
        return self.S.op(eng, lambda e: e.tensor_scalar(out, a, s1, s2, op0, op1), reads, writes)

    def stt(self, out, in0, scalar, in1, op0, op1, reads, writes, eng="vector"):
        return self.S.op(eng, lambda e: e.scalar_tensor_tensor(out, in0, scalar, in1, op0, op1), reads, writes)

    def cp(self, out, in_, reads, writes, eng="vector"):
        if eng == "scalar":
            return self.S.op(eng, lambda e: e.copy(out, in_), reads, writes)
        return self.S.op(eng, lambda e: e.tensor_copy(out, in_), reads, writes)

    def memset(self, ap, val, writes, eng="vector"):
        return self.S.op(eng, lambda e: e.memset(ap, val), (), writes)

    def dma(self, q, out, in_, reads=(), writes=(), **kw):
        return self.S.dma(q, out, in_, reads, writes, **kw)


def build_nc(stage=99, debug=False):
    nc = bass.Bass("TRN2", target_bir_lowering=False)

    in_names = []

    def din(name, shape, dt=F32):
        in_names.append(name)
        return nc.dram_tensor(name, list(shape), dt, kind="ExternalInput").ap()

    x = din("x", [S_TOK, D])
    mem = din("mem", [256, D])
    pos = din("positions", [1, S_TOK], I32)
    w_in = din("w_in", [D, IN_COLS])
    b_in = din("b_in", [1, IN_COLS])
    gm_ln_g = din("gm_ln_g", [1, 1024])
    gm_ln_b = din("gm_ln_b", [1, 1024])
    gm_w_s = din("gm_w_s", [8 * 128, 128])
    gm_b_s = din("gm_b_s", [1, 1024])
    w_gm_out = din("w_gm_out", [1024, 1024])
    q_norm_g = din("mla_q_norm_g", [1, 384])
    kv_norm_g = din("mla_kv_norm_g", [1, 256])
    w_uq = din("w_uq", [384, 1536])
    w_uk = din("w_uk", [256, 1024])
    w_uv = din("w_uv", [256, 1024])
    w_mla_out = din("w_mla_out", [1024, 1024])
    w_o = din("w_o", [1024, 1024])
    ln1_g = din("ln1_g", [1, 1024]); ln1_b = din("ln1_b", [1, 1024])
    w_mq = din("w_mq", [1024, 1024]); w_mk = din("w_mk", [1024, 1024])
    w_mv = din("w_mv", [1024, 1024]); w_mo = din("w_mo", [1024, 1024])
    ln2_g = din("ln2_g", [1, 1024]); ln2_b = din("ln2_b", [1, 1024])
    w_gr = din("w_group_router", [1024, 8]); b_gr = din("b_group_router", [1, 8])
    w_er = din("w_expert_router", [1024, 64]); b_er = din("b_expert_router", [1, 64])
    if stage >= 4:
        w_eg = din("w_exp_gate", [64 * 1024, 256]); w_eu = din("w_exp_up", [64 * 1024, 256])
        w_ed = din("w_exp_down", [64 * 256, 1024])
    ln3_g = din("ln3_g", [1, 1024]); ln3_b = din("ln3_b", [1, 1024])
    cst = din("consts", [128, 512])
    out = nc.dram_tensor("out", [S_TOK, D], F32, kind="ExternalOutput").ap()

    dk = dict(kind="ExternalOutput") if debug else {}
    ygm_d = nc.dram_tensor("ygm_d", [1024, S_TOK], BF16, **dk).ap()
    cqn_d = nc.dram_tensor("cqn_d", [384, S_TOK], BF16, **dk).ap()
    ckvn_d = nc.dram_tensor("ckvn_d", [256, S_TOK], BF16, **dk).ap()
    kr_d = nc.dram_tensor("kr_d", [32, S_TOK], BF16, **dk).ap()
    oT_d = nc.dram_tensor("oT_d", [1024, S_TOK], BF16).ap()

    dbg = {}
    if debug:
        for nm, shp in (("d_gated", [1024, S_TOK]),):
            dbg[nm] = nc.dram_tensor(nm, shp, F32, kind="ExternalOutput").ap()

    k = K(nc)
    S = k.S
    ar = k.ar
    ps, psB = k.ps, k.psB

    cst_sb = ar.alloc(512, F32); B_cst = Buf("cst")
    identf = cst_sb[:, 0:128]
    trif = cst_sb[:, 128:256]
    rc = cst_sb[:, 256:264]
    identb = ar.alloc(128, BF16); B_identb = Buf("identb")
    trib = ar.alloc(128, BF16); B_trib = Buf("trib")
    onesb = ar.alloc(128, BF16); B_onesb = Buf("onesb")
    cosT = ar.alloc(S_TOK, BF16); B_cos = Buf("cos")
    sinT = ar.alloc(S_TOK, BF16); B_sin = Buf("sin")
    persist_top = ar.top

    k.dma("sync", cst_sb, cst, writes=[B_cst])
    k.cp(identb, identf, [B_cst], [B_identb])
    k.cp(trib, trif, [B_cst], [B_trib])
    k.memset(onesb, 1.0, [B_onesb])

    p1_base = ar.top
    winA = ar.alloc(8 * NA, BF16).rearrange("p (c n) -> p c n", c=8); B_winA = Buf("winA")
    wkr = ar.alloc(8 * 96, BF16).rearrange("p (c n) -> p c n", c=8); B_wkr = Buf("wkr")
    wkrs = ar.alloc(8 * 96, BF16).rearrange("p (c n) -> p c n", c=8); B_wkrs = Buf("wkrs")
    wgo = ar.alloc(8 * 1024, BF16).rearrange("p (c n) -> p c n", c=8); B_wgo = Buf("wgo")
    bcol = ar.alloc(21, F32); B_bcol = Buf("bcol")
    bkr = ar.alloc(2, F32); B_bkr = Buf("bkr")
    bv_b = ar.alloc(1024, F32); B_bvb = Buf("bvb")
    lng_col = ar.alloc(8, F32); lnb_col = ar.alloc(8, F32); B_lncol = Buf("lncol")
    bs_b = ar.alloc(1024, F32); B_bsb = Buf("bsb")
    BT = ar.alloc(1024, F32); B_BT = Buf("BT")
    wsT = ar.alloc(1024, BF16); B_wsT = Buf("wsT")
    p1_work = ar.top
    wsf = ar.alloc(1024, F32); B_wsf = Buf("wsf")
    posi = ar.alloc(S_TOK, I32); B_posi = Buf("posi")
    ang = ar.alloc(S_TOK, F32); B_ang = Buf("ang")
    ang2 = ar.alloc(S_TOK, F32); B_ang2 = Buf("ang2")
    ang3 = ar.alloc(S_TOK, F32); B_ang3 = Buf("ang3")
    import os
    PARTS = os.environ.get("KPARTS", "ABCD")
    if "A" in PARTS:
        k.dma("sync", posi[64:96, :], pos.partition_broadcast(32), writes=[B_posi])
        R = slice(64, 96)
        angi = posi
        C1 = 6.28125
        C2 = TWO_PI - C1
        k.cp(ang[R, :], posi[R, :], [B_posi], [B_ang])
        k.ts(ang[R, :], ang[R, :], rc[R, 0:1], None, ALU.mult, None, [B_ang, B_cst], [B_ang])
        for which in range(2):
            if which == 0:
                k.ts(ang2[R, :], ang[R, :], PI / 2, None, ALU.add, None, [B_ang], [B_ang2])
                src = ang2
                Bsrc = B_ang2
            else:
                src = ang
                Bsrc = B_ang
            k.ts(ang3[R, :], src[R, :], 1.0 / TWO_PI, None, ALU.mult, None, [Bsrc], [B_ang3])
            k.cp(angi[R, :], ang3[R, :], [B_ang3], [B_posi])
            k.cp(ang3[R, :], angi[R, :], [B_posi], [B_ang3])
            k.stt(src[R, :], ang3[R, :], -C1, src[R, :], ALU.mult, ALU.add, [B_ang3, Bsrc], [Bsrc])
            k.stt(src[R, :], ang3[R, :], -C2, src[R, :], ALU.mult, ALU.add, [B_ang3, Bsrc], [Bsrc])
            if which == 0:
                k.act(cosT[R, :], src[R, :], AF.Sin, [Bsrc], [B_cos])
            else:
                k.act(sinT[R, :], src[R, :], AF.Sin, [Bsrc, B_cst], [B_sin], scale=rc[R, 1:2])
    with nc.allow_non_contiguous_dma(reason="one-time small parameter layout loads"):
        if "B" in PARTS:
            for c0 in range(0, NA, 680):
                k.dma("gpsimd", winA[:, :, c0:c0 + 680], w_in[:, c0:c0 + 680].rearrange("(c p) n -> p c n", p=128), writes=[B_winA])
            k.memset(wkr.rearrange("p c n -> p (c n)"), 0.0, [B_wkr])
            k.memset(wkrs.rearrange("p c n -> p (c n)"), 0.0, [B_wkrs])
            k.dma("gpsimd", wkr[:, :, 64:96], w_in[:, C_KR:C_KR + 32].rearrange("(c p) n -> p c n", p=128), writes=[B_wkr])
            k.dma("gpsimd", wkrs[:, :, 64:80], w_in[:, C_KR + 16:C_KR + 32].rearrange("(c p) n -> p c n", p=128), writes=[B_wkrs])
            k.dma("gpsimd", wkrs[:, :, 80:96], w_in[:, C_KR:C_KR + 16].rearrange("(c p) n -> p c n", p=128), writes=[B_wkrs])
            k.dma("gpsimd", wgo, w_gm_out.rearrange("(c p) n -> p c n", p=128), writes=[B_wgo])
        if "C" in PARTS:
            k.dma("sync", bcol, b_in[0, 0:2688].rearrange("(c p) -> p c", p=128), writes=[B_bcol], allow_slow_non_contiguous=True)
            k.dma("sync", bkr[64:96, 0:1], b_in[0, C_KR:C_KR + 32].rearrange("(p o) -> p o", o=1), writes=[B_bkr], allow_slow_non_contiguous=True)
            k.dma("sync", bkr[64:80, 1:2], b_in[0, C_KR + 16:C_KR + 32].rearrange("(p o) -> p o", o=1), writes=[B_bkr], allow_slow_non_contiguous=True)
            k.dma("sync", bkr[80:96, 1:2], b_in[0, C_KR:C_KR + 16].rearrange("(p o) -> p o", o=1), writes=[B_bkr], allow_slow_non_contiguous=True)
            k.dma("sync", bv_b, b_in[:, C_V:C_V + 1024].partition_broadcast(128), writes=[B_bvb])
            k.dma("sync", lng_col, gm_ln_g[0, :].rearrange("(c p) -> p c", p=128), writes=[B_lncol], allow_slow_non_contiguous=True)
            k.dma("sync", lnb_col, gm_ln_b[0, :].rearrange("(c p) -> p c", p=128), writes=[B_lncol], allow_slow_non_contiguous=True)
            k.dma("sync", bs_b, gm_b_s.partition_broadcast(128), writes=[B_bsb])
            k.dma("sync", wsf.rearrange("p (g s) -> p g s", g=8), gm_w_s.rearrange("(g t) s -> t g s", t=128), writes=[B_wsf])

    if "D" in PARTS:
        for g in range(8):
            bank = g // 4
            k.tr(ps[bank][:, (g % 4) * 128:(g % 4 + 1) * 128], wsf[:, g * 128:(g + 1) * 128], identf, [B_wsf, B_cst], [psB[bank]], signal=(g % 4 == 3))
        for bank in range(2):
            for j in range(4):
                g = bank * 4 + j
                k.tt(wsT[:, g * 128:(g + 1) * 128], ps[bank][:, j * 128:(j + 1) * 128], trif, ALU.mult, [psB[bank], B_cst], [B_wsT])
        for bank in range(2):
            k.mm(ps[2 + bank], onesb, wsT[:, bank * 512:(bank + 1) * 512], True, True, [B_onesb, B_wsT], [psB[2 + bank]], signal=True)
        for g in range(8):
            bank = 2 + g // 4
            k.stt(BT[:, g * 128:(g + 1) * 128], ps[bank][:, (g % 4) * 128:(g % 4 + 1) * 128], lnb_col[:, g:g + 1], bs_b[:, g * 128:(g + 1) * 128],
                  ALU.mult, ALU.add, [psB[bank], B_lncol, B_bsb], [B_BT])

    S.barrier()
    ar.top = p1_work
    TB = 512
    xbf = [ar.alloc(1024, BF16) for _ in range(4)]; B_xbf = [Buf() for _ in range(4)]
    xT = ar.alloc(8 * TB, BF16).rearrange("p (c n) -> p c n", c=8); B_xT = Buf("xT")
    uT = ar.alloc(8 * TB, BF16).rearrange("p (c n) -> p c n", c=8); B_uT = Buf("uT")
    gT = ar.alloc(8 * TB, BF16).rearrange("p (c n) -> p c n", c=8); B_gT = Buf("gT")
    ygT = ar.alloc(8 * TB, BF16).rearrange("p (c n) -> p c n", c=8); B_ygT = Buf("ygT")
    vf = [ar.alloc(1024, F32) for _ in range(2)]; B_vf = [Buf("vf0"), Buf("vf1")]
    vn = [ar.alloc(1024, BF16) for _ in range(2)]; B_vn = [Buf("vn0"), Buf("vn1")]
    stats = ar.alloc(2 * 6 + 8, F32); B_st = Buf("stats")
    latf = ar.alloc(3 * TB, F32).rearrange("p (c n) -> p c n", c=3); B_latf = Buf("latf")
    latsq = ar.alloc(3 * TB, BF16).rearrange("p (c n) -> p c n", c=3); B_latsq = Buf("latsq")
    rstd_b = ar.alloc(TB, F32); B_rstd = Buf("rstd")
    krt = ar.alloc(2 * TB, F32); B_krt = Buf("krt")
    gtmp = ar.alloc(1024, F32); B_gtmp = [Buf("gtmp0"), Buf("gtmp1")]
    cqs = ar.alloc(3 * TB, BF16).rearrange("p (c n) -> p c n", c=3); B_cqs = Buf("cqs")
    ckvs = ar.alloc(2 * TB, BF16).rearrange("p (c n) -> p c n", c=2); B_ckvs = Buf("ckvs")
    krs = ar.alloc(TB, BF16); B_krs = Buf("krs")
    dbgf = ar.alloc(8 * TB, F32).rearrange("p (c n) -> p c n", c=8) if debug else None; B_dbgf = Buf("dbgf")

    pr = [0]

    def nextps(lo, hi):
        i = lo + pr[0] % (hi - lo)
        pr[0] += 1
        return i

    nblk = S_TOK // TB if stage >= 1 else 0
    for tb in range(nblk):
        t0 = tb * TB
        if tb == 0:
            for j in range(4):
                k.dma("gpsimd", xbf[j], x[j * 128:(j + 1) * 128, :], writes=[B_xbf[j]])
        for j in range(4):
            xb = xbf[j]; Bx = B_xbf[j]
            bank = j % 2
            psb16 = ps[bank].bitcast(BF16)
            for c in range(8):
                k.tr(psb16[:, c * 128:(c + 1) * 128], xb[:, c * 128:(c + 1) * 128], identb, [Bx, B_identb], [psB[bank]], signal=(c == 7))
            k.cp(xT[:, :, j * 128:(j + 1) * 128], psb16.rearrange("p (c n) -> p c n", c=8), [psB[bank]], [B_xT],
                 eng=("vector" if j % 2 == 0 else "scalar"))
        if tb + 1 < nblk:
            for j in range(4):
                k.dma("gpsimd", xbf[j], x[t0 + TB + j * 128:t0 + TB + (j + 1) * 128, :], writes=[B_xbf[j]])
        for oc in range(8):
            bank = 2 + oc % 4
            for c in range(8):
                k.mm(ps[bank], winA[:, c, C_U + oc * 128:C_U + (oc + 1) * 128], xT[:, c, :], c == 0, c == 7,
                     [B_winA, B_xT], [psB[bank]], signal=(c == 7))
            k.act(uT[:, oc, :], ps[bank], AF.Gelu, [psB[bank], B_bcol], [B_uT], bias=bcol[:, oc:oc + 1])
        for (c_off, nch, dst, Bdst, eps_n, dst_d) in ((C_CQ, 3, cqs, B_cqs, 384, cqn_d), (C_CKV, 2, ckvs, B_ckvs, 256, ckvn_d)):
            for oc in range(nch):
                bank = 2 + oc % 4
                for c in range(8):
                    k.mm(ps[bank], winA[:, c, c_off + oc * 128:c_off + (oc + 1) * 128], xT[:, c, :], c == 0, c == 7,
                         [B_winA, B_xT], [psB[bank]], signal=(c == 7))
                k.act(latf[:, oc, :], ps[bank], AF.Identity, [psB[bank], B_bcol], [B_latf], bias=bcol[:, c_off // 128 + oc:c_off // 128 + oc + 1])
                k.act(latsq[:, oc, :], latf[:, oc, :], AF.Square, [B_latf], [B_latsq])
            bank = 6
            for oc in range(nch):
                k.mm(ps[bank], onesb, latsq[:, oc, :], oc == 0, oc == nch - 1, [B_onesb, B_latsq], [psB[bank]], signal=(oc == nch - 1))
            k.act(rstd_b, ps[bank], AF.Sqrt, [psB[bank], B_cst], [B_rstd], bias=(rc[:, 5:6] if eps_n == 384 else rc[:, 6:7]))
            S.op("vector", lambda e: e.reciprocal(rstd_b, rstd_b), [B_rstd], [B_rstd])
            for oc in range(nch):
                k.tt(dst[:, oc, :], latf[:, oc, :], rstd_b, ALU.mult, [B_latf, B_rstd], [Bdst])
            k.dma("sync", dst_d[:, t0:t0 + TB].rearrange("(c p) t -> p c t", p=128), dst, reads=[Bdst])
        for (wk, bank) in ((wkr, 6), (wkrs, 7)):
            for c in range(8):
                k.mm(ps[bank][0:96, :], wk[:, c, :], xT[:, c, :], c == 0, c == 7, [B_wkr, B_wkrs, B_xT], [psB[bank]], signal=(c == 7))
        k.stt(krt[R, 0:TB], ps[6][R, :], bkr[R, 0:1], cosT[R, t0:t0 + TB], ALU.add, ALU.mult, [psB[6], B_bkr, B_cos], [B_krt])
        k.stt(krt[R, TB:2 * TB], ps[7][R, :], bkr[R, 1:2], sinT[R, t0:t0 + TB], ALU.add, ALU.mult, [psB[7], B_bkr, B_sin], [B_krt])
        k.tt(krs[R, :], krt[R, 0:TB], krt[R, TB:2 * TB], ALU.add, [B_krt], [B_krs])
        k.dma("sync", kr_d[:, t0:t0 + TB], krs[R, :], reads=[B_krs])
        def v_mm(j):
            vt = vf[j % 2]; Bv = B_vf[j % 2]
            for half in range(2):
                bank = 2 + (2 * j + half) % 4
                for c in range(8):
                    k.mm(ps[bank], xT[:, c, j * 128:(j + 1) * 128], winA[:, c, C_V + half * 512:C_V + (half + 1) * 512], c == 0, c == 7,
                         [B_xT, B_winA], [psB[bank]], signal=(c == 7))
                k.tt(vt[:, half * 512:(half + 1) * 512], ps[bank], bv_b[:, half * 512:(half + 1) * 512], ALU.add, [psB[bank], B_bvb], [Bv])

        v_mm(0)
        for j in range(4):
            vt = vf[j % 2]; Bv = B_vf[j % 2]
            vb = vn[j % 2]; Bvn = B_vn[j % 2]
            k.act(vt, vt, AF.Gelu, [Bv], [Bv])
            for half in range(2):
                S.op("vector", lambda e, vt=vt, half=half: e.bn_stats(stats[:, half * 6:(half + 1) * 6], vt[:, half * 512:(half + 1) * 512]), [Bv], [B_st])
            S.op("vector", lambda e: e.bn_aggr(stats[:, 12:14], stats[:, 0:12]), [B_st], [B_st])
            k.act(stats[:, 14:15], stats[:, 13:14], AF.Sqrt, [B_st, B_cst], [B_st], bias=rc[:, 4:5])
            S.op("vector", lambda e: e.reciprocal(stats[:, 14:15], stats[:, 14:15]), [B_st], [B_st])
            k.ts(vb, vt, stats[:, 12:13], stats[:, 14:15], ALU.subtract, ALU.mult, [Bv, B_st], [Bvn])
            if j + 1 < 4:
                v_mm(j + 1)
            for gq in range(2):
                bank = 6 + gq
                for gg in range(4):
                    g = gq * 4 + gg
                    k.mm(ps[bank][:, gg * 128:(gg + 1) * 128], vb[:, g * 128:(g + 1) * 128], wsT[:, g * 128:(g + 1) * 128], True, True,
                         [Bvn, B_wsT], [psB[bank]], signal=(gg == 3))
                for gg in range(4):
                    g = gq * 4 + gg
                    k.stt(gtmp[:, gq * 512 + gg * 128:gq * 512 + (gg + 1) * 128], ps[bank][:, gg * 128:(gg + 1) * 128], lng_col[:, g:g + 1], BT[:, g * 128:(g + 1) * 128],
                          ALU.mult, ALU.add, [psB[bank], B_lncol, B_BT], [B_gtmp[gq]])
                k.tt(gT[:, gq * 4:(gq + 1) * 4, j * 128:(j + 1) * 128], gtmp[:, gq * 512:(gq + 1) * 512].rearrange("p (g t) -> p g t", g=4),
                     uT[:, gq * 4:(gq + 1) * 4, j * 128:(j + 1) * 128], ALU.mult, [B_gtmp[gq], B_uT], [B_gT], eng="gpsimd")
        for oc in range(8):
            bank = 2 + oc % 4
            for c in range(8):
                k.mm(ps[bank], wgo[:, c, oc * 128:(oc + 1) * 128], gT[:, c, :], c == 0, c == 7, [B_wgo, B_gT], [psB[bank]], signal=(c == 7))
            k.cp(ygT[:, oc, :], ps[bank], [psB[bank]], [B_ygT], eng=("vector" if oc % 2 == 0 else "scalar"))
        k.dma("sync", ygm_d[:, t0:t0 + TB].rearrange("(c p) t -> p c t", p=128), ygT, reads=[B_ygT])
        if debug:
            k.cp(dbgf.rearrange("p c n -> p (c n)"), gT.rearrange("p c n -> p (c n)"), [B_gT], [B_dbgf])
            k.dma("sync", dbg["d_gated"][:, t0:t0 + TB].rearrange("(c p) t -> p c t", p=128), dbgf, reads=[B_dbgf])


    def layer_norm_tile(h, Bh, outt, Bout, g_b, b_b, Bgb, st, Bst):
        for half in range(2):
            S.op("vector", lambda e, half=half: e.bn_stats(st[:, half * 6:(half + 1) * 6], h[:, half * 512:(half + 1) * 512]), [Bh], [Bst])
        S.op("vector", lambda e: e.bn_aggr(st[:, 12:14], st[:, 0:12]), [Bst], [Bst])
        k.act(st[:, 14:15], st[:, 13:14], AF.Sqrt, [Bst, B_cst], [Bst], bias=rc[:, 4:5])
        S.op("vector", lambda e: e.reciprocal(st[:, 14:15], st[:, 14:15]), [Bst], [Bst])
        k.ts(h, h, st[:, 12:13], st[:, 14:15], ALU.subtract, ALU.mult, [Bh, Bst], [Bh])
        k.tt(h, h, g_b, ALU.mult, [Bh, Bgb], [Bh])
        k.tt(outt, h, b_b, ALU.add, [Bh, Bgb], [Bout])

    _bc = {}

    def bcreg(e):
        if "r" not in _bc:
            _bc["r"] = e.to_reg(64 * 256 - 1)
        return _bc["r"]

    x1_d = nc.dram_tensor("x1_d", [S_TOK, D], F32, **dk).ap()
    x2_d = nc.dram_tensor("x2_d", [S_TOK, D], F32, **dk).ap()
    x2b_d = nc.dram_tensor("x2b_d", [S_TOK, D], BF16).ap()
    NSLOT = 64 * 256
    tokof_d = nc.dram_tensor("tokof_d", [NSLOT, 2], I32).ap()
    yd_d = nc.dram_tensor("yd_d", [NSLOT, D], BF16).ap()

    if stage >= 2:
        S.barrier()
        ar.top = persist_top
        wuq = ar.alloc(3 * 1536, BF16).rearrange("p (c n) -> p c n", c=3); B_wuq = Buf("wuq")
        wuqs = ar.alloc(3 * 1536, BF16).rearrange("p (c n) -> p c n", c=3); B_wuqs = Buf("wuqs")
        wuk = ar.alloc(2 * 1024, BF16).rearrange("p (c n) -> p c n", c=2); B_wuk = Buf("wuk")
        wuv = ar.alloc(2 * 1024, BF16).rearrange("p (c n) -> p c n", c=2); B_wuv = Buf("wuv")
        gcol = ar.alloc(8, F32); B_gcol = Buf("gcol")
        cqnT = ar.alloc(3 * S_TOK, BF16).rearrange("p (c n) -> p c n", c=3); B_cqn = Buf("cqn")
        ckvnT = ar.alloc(2 * S_TOK, BF16).rearrange("p (c n) -> p c n", c=2); B_ckvn = Buf("ckvn")
        KT = [ar.alloc(S_TOK, BF16) for _ in range(2)]; B_KT = [Buf("kt0"), Buf("kt1")]
        QT = [ar.alloc(S_TOK, BF16) for _ in range(2)]; B_QT = [Buf("qt0"), Buf("qt1")]
        VAf = [ar.alloc(NT * 65 + 64, BF16) for _ in range(2)]
        VA = [v_[:, 0:NT * 65].rearrange("p (t n) -> p t n", n=65) for v_ in VAf]; B_VA = [Buf("va0"), Buf("va1")]
        NP = 4
        pT = [ar.alloc(1024, BF16) for _ in range(NP)]; B_pT = [Buf(f"pT{i}") for i in range(NP)]
        rtmp = [ar.alloc(512, F32) for _ in range(2)]; B_rtmp = [Buf("rt0"), Buf("rt1")]
        rrec2 = [ar.alloc(512, F32) for _ in range(2)]; B_rrec2 = [Buf("rrec0"), Buf("rrec1")]
        bcs = ar.alloc(512, F32); B_bcs = Buf("bcs")
        onrm = [ar.alloc(512, BF16) for _ in range(2)]; B_onrm = [Buf("on0"), Buf("on1")]
        wst = ar.alloc(1536, F32); B_wst = Buf("wst")

        with nc.allow_non_contiguous_dma(reason="small parameter columns"):
            k.dma("sync", gcol[:, 0:3], q_norm_g[0, :].rearrange("(c p) -> p c", p=128), writes=[B_gcol], allow_slow_non_contiguous=True)
            k.dma("sync", gcol[:, 3:5], kv_norm_g[0, :].rearrange("(c p) -> p c", p=128), writes=[B_gcol], allow_slow_non_contiguous=True)
        k.ts(gcol[:, 0:3], gcol[:, 0:3], float(np.sqrt(384.0)), None, ALU.mult, None, [B_gcol], [B_gcol])
        k.ts(gcol[:, 3:5], gcol[:, 3:5], float(np.sqrt(256.0)), None, ALU.mult, None, [B_gcol], [B_gcol])
        for c in range(3):
            k.dma("sync", wst[:, 0:1536], w_uq[c * 128:(c + 1) * 128, :], writes=[B_wst])
            k.ts(wuq[:, c, :], wst[:, 0:1536], gcol[:, c:c + 1], None, ALU.mult, None, [B_wst, B_gcol], [B_wuq])
        for (wdst, Bw, wsrc) in ((wuk, B_wuk, w_uk), (wuv, B_wuv, w_uv)):
            for c in range(2):
                k.dma("sync", wst[:, 0:1024], wsrc[c * 128:(c + 1) * 128, :], writes=[B_wst])
                k.ts(wdst[:, c, :], wst[:, 0:1024], gcol[:, 3 + c:4 + c], None, ALU.mult, None, [B_wst, B_gcol], [Bw])
        k.memset(wuqs.rearrange("p c n -> p (c n)"), 0.0, [B_wuqs])
        for c in range(3):
            srcv = wuq[:, c, :].rearrange("p (h j) -> p h j", j=96)
            dstv = wuqs[:, c, :].rearrange("p (h j) -> p h j", j=96)
            k.cp(dstv[:, :, 64:80], srcv[:, :, 80:96], [B_wuq], [B_wuqs])
            k.cp(dstv[:, :, 80:96], srcv[:, :, 64:80], [B_wuq], [B_wuqs])
        k.dma("sync", cqnT, cqn_d.rearrange("(c p) t -> p c t", p=128), writes=[B_cqn])
        k.dma("sync", ckvnT, ckvn_d.rearrange("(c p) t -> p c t", p=128), writes=[B_ckvn])
        for b in range(2):
            k.memset(KT[b][64:128, :], 0.0, [B_KT[b]])
            k.memset(QT[b][64:128, :], 0.0, [B_QT[b]])
            k.dma("sync", KT[b][64:96, :], kr_d, writes=[B_KT[b]])
            k.memset(VAf[b][:, NT * 65:NT * 65 + 64], 0.0, [B_VA[b]])
            k.memset(VA[b][:, :, 64:65], 1.0, [B_VA[b]])
        maskD = ar.alloc(4 * 512, BF16); B_maskD = Buf("maskD")
        k.memset(maskD, 1.0, [B_maskD])
        for i_ in range(4):
            if i_ > 0:
                k.memset(maskD[:, i_ * 512:i_ * 512 + i_ * 128], 0.0, [B_maskD])
            k.cp(maskD[:, i_ * 512 + i_ * 128:i_ * 512 + (i_ + 1) * 128], trib, [B_trib, B_maskD], [B_maskD])
        SCALE = float(96.0 ** -0.5)
        NH = 16 if stage >= 2 else 0

        gcount = [0]
        SG = [k.psbig[1], k.psbig[2], k.psbig[3]]
        B_SG = [Buf("sg0"), Buf("sg1"), Buf("sg2")]

        busy = set()

        def next_group():
            for _ in range(3):
                g = gcount[0] % 3
                gcount[0] += 1
                if g not in busy:
                    return g
            raise AssertionError("no free PSUM group")

        def gen_chunks(h):
            b = h % 2
            chunks = []

            def q_chunk(qb):
                c0 = qb * 512
                g = next_group()
                G = SG[g]
                for c in range(3):
                    k.mm(G[0:96, 0:512], wuq[:, c, h * 96:(h + 1) * 96], cqnT[:, c, c0:c0 + 512], c == 0, c == 2, [B_wuq, B_cqn], [B_SG[g]])
                for c in range(3):
                    k.mm(G[0:96, 512:1024], wuqs[:, c, h * 96:(h + 1) * 96], cqnT[:, c, c0:c0 + 512], c == 0, c == 2, [B_wuqs, B_cqn], [B_SG[g]], signal=(c == 2))
                k.cp(QT[b][0:64, c0:c0 + 512], G[0:64, 0:512], [B_SG[g]], [B_QT[b]])
                k.tt(rtmp[0][R, :], G[R, 0:512], cosT[R, c0:c0 + 512], ALU.mult, [B_SG[g], B_cos], [B_rtmp[0]])
                k.tt(rtmp[1][R, :], G[R, 512:1024], sinT[R, c0:c0 + 512], ALU.mult, [B_SG[g], B_sin], [B_rtmp[1]])
                k.tt(QT[b][R, c0:c0 + 512], rtmp[0][R, :], rtmp[1][R, :], ALU.add, [B_rtmp[0], B_rtmp[1]], [B_QT[b]], eng="gpsimd")

            def k_chunk(qq):
                g = next_group()
                G = SG[g]
                for hf in range(2):
                    c0 = (qq * 2 + hf) * 512
                    for c in range(2):
                        k.mm(G[0:64, hf * 512:(hf + 1) * 512], wuk[:, c, h * 64:(h + 1) * 64], ckvnT[:, c, c0:c0 + 512], c == 0, c == 1, [B_wuk, B_ckvn], [B_SG[g]],
                             signal=(c == 1 and hf == 1))
                k.cp(KT[b][0:64, qq * 1024:(qq + 1) * 1024], G[0:64, :], [B_SG[g]], [B_KT[b]], eng="scalar")

            def v_chunk(tg):
                g = next_group()
                G = SG[g]
                for i in range(16):
                    t = tg * 16 + i
                    for c in range(2):
                        k.mm(G[:, i * 64:(i + 1) * 64], ckvnT[:, c, t * 128:(t + 1) * 128], wuv[:, c, h * 64:(h + 1) * 64], c == 0, c == 1,
                             [B_ckvn, B_wuv], [B_SG[g]], signal=(c == 1 and i == 15))
                k.cp(VA[b][:, tg * 16:(tg + 1) * 16, 0:64], G.rearrange("p (t n) -> p t n", n=64), [B_SG[g]], [B_VA[b]])

            for qb in range(8):
                chunks.append(lambda qb=qb: q_chunk(qb))
            for qq in range(4):
                chunks.append(lambda qq=qq: k_chunk(qq))
            for tg in range(2):
                chunks.append(lambda tg=tg: v_chunk(tg))
            return chunks

        pcount = [0]
        ocount = [0]

        def attn_head(h, pending):
            b = h % 2
            pairs = [(qb, p) for qb in range(8) for p in range(2 * qb + 2)]
            ob_of = {}
            for qb in range(8):
                ob_of[qb] = ocount[0] % 2
                ocount[0] += 1
            grp = {}

            def offs(qb, kt):
                return max(0, kt - 4 * qb) * 128

            def qk(i):
                qb, p = pairs[i]
                q0 = qb * 512
                g = next_group()
                busy.add(g)
                grp[i] = g
                for hf in range(2):
                    kt = 2 * p + hf
                    off = offs(qb, kt)
                    k.mm(SG[g][:, hf * 512 + off:hf * 512 + 512], KT[b][0:128, kt * 128:(kt + 1) * 128], QT[b][0:128, q0 + off:q0 + 512], True, True,
                         [B_KT[b], B_QT[b]], [B_SG[g]], signal=(hf == 1))

            def epi_a(qb):
                ob = ob_of[qb]
                rr = rrec2[qb % 2]
                S.op("vector", lambda e, ob=ob, rr=rr: e.reciprocal(rr[64:65, :], ps[ob][64:65, :]), [psB[ob]], [B_rrec2[qb % 2]])

            def epilogue(qb):
                ob = ob_of[qb]
                q0 = qb * 512
                rrec = rrec2[qb % 2]
                B_rrec = B_rrec2[qb % 2]
                g = next_group()
                k.mm(SG[g][0:64, 0:512], trif[64:65, 64:128], rrec[64:65, :], True, True, [B_cst, B_rrec], [B_SG[g]], signal=True)
                k.cp(bcs[0:64, :], SG[g][0:64, 0:512], [B_SG[g]], [B_bcs], eng="scalar")
                oj = ocount[0] % 2
                ocount[0] += 1
                k.tt(onrm[oj][0:64, :], ps[ob][0:64, :], bcs[0:64, :], ALU.mult, [psB[ob], B_bcs], [B_onrm[oj]])
                k.dma("sync", oT_d[h * 64:(h + 1) * 64, q0:q0 + 512], onrm[oj][0:64, :], reads=[B_onrm[oj]])

            due = []
            since = 0
            qk(0)
            qk(1)
            for i, (qb, p) in enumerate(pairs):
                if i + 2 < len(pairs):
                    qk(i + 2)
                nkt = 4 * qb + 4
                g = grp[i]
                pj = pcount[0] % NP
                pcount[0] += 1
                diag = (2 * p >= 4 * qb)
                if not diag:
                    k.act(pT[pj], SG[g], AF.Exp, [B_SG[g]], [B_pT[pj]], scale=SCALE)
                else:
                    for hf in range(2):
                        off = offs(qb, 2 * p + hf)
                        k.act(pT[pj][:, hf * 512 + off:hf * 512 + 512], SG[g][:, hf * 512 + off:hf * 512 + 512], AF.Exp, [B_SG[g]], [B_pT[pj]], scale=SCALE)
                        k.tt(pT[pj][:, hf * 512 + off:hf * 512 + off + 128], pT[pj][:, hf * 512 + off:hf * 512 + off + 128], trib, ALU.mult,
                             [B_pT[pj], B_trib], [B_pT[pj]], eng="gpsimd")
                ob = ob_of[qb]
                for hf in range(2):
                    kt = 2 * p + hf
                    off = offs(qb, kt)
                    k.mm(ps[ob][0:128, off:512], VAf[b][:, kt * 65:kt * 65 + 128], pT[pj][:, hf * 512 + off:hf * 512 + 512], kt == 0, kt == nkt - 1, [B_VA[b], B_pT[pj]], [psB[ob]],
                         signal=(kt == nkt - 1))
                busy.discard(g)
                if p == 2 * qb + 1:
                    epi_a(qb)
                    due.append((i + 4, qb))
                while due and due[0][0] <= i:
                    epilogue(due.pop(0)[1])
                since += 1
                if pending and since >= 5:
                    since = 0
                    pending.pop(0)()
            while due:
                epilogue(due.pop(0)[1])

        if NH:
            for ch in gen_chunks(0):
                ch()
        for h in range(NH):
            pending = gen_chunks(h + 1) if h + 1 < NH else []
            attn_head(h, pending)
            while pending:
                pending.pop(0)()

    if stage >= 3:
        S.barrier()
        ar.top = persist_top
        wing = ar.alloc(8 * 2048, BF16).rearrange("p (c n) -> p c n", c=8); B_wing = Buf("wing")
        wml = ar.alloc(8 * 1024, BF16).rearrange("p (c n) -> p c n", c=8); B_wml = Buf("wml")
        wo_sb = ar.alloc(8 * 1024, BF16).rearrange("p (c n) -> p c n", c=8); B_wo = Buf("wo")
        bgcol = ar.alloc(16, F32); B_bgcol = Buf("bgcol")
        l1g = ar.alloc(1024, F32); l1b = ar.alloc(1024, F32); B_l1 = Buf("l1")
        xbf = [ar.alloc(1024, BF16) for _ in range(4)]; B_xbf = [Buf() for _ in range(4)]
        xT = ar.alloc(8 * 512, BF16).rearrange("p (c n) -> p c n", c=8); B_xT = Buf("xT")
        xf = [ar.alloc(1024, F32) for _ in range(4)]; B_xf = [Buf() for _ in range(4)]
        sgT = ar.alloc(8 * 512, BF16).rearrange("p (c n) -> p c n", c=8); B_sgT = Buf("sgT")
        smT = ar.alloc(8 * 512, BF16).rearrange("p (c n) -> p c n", c=8); B_smT = Buf("smT")
        ygb2 = [ar.alloc(8 * 512, BF16).rearrange("p (c n) -> p c n", c=8) for _ in range(2)]; B_ygb2 = [Buf(), Buf()]
        oTb2 = [ar.alloc(8 * 512, BF16).rearrange("p (c n) -> p c n", c=8) for _ in range(2)]; B_oTb2 = [Buf(), Buf()]
        mT = ar.alloc(8 * 512, BF16).rearrange("p (c n) -> p c n", c=8); B_mT = Buf("mT")
        t1 = [ar.alloc(512, F32) for _ in range(2)]; B_t1 = [Buf("t1a"), Buf("t1b")]
        t2 = [ar.alloc(512, F32) for _ in range(2)]; B_t2 = [Buf("t2a"), Buf("t2b")]
        h1 = [ar.alloc(1024, F32) for _ in range(2)]; B_h1 = [Buf("h1a"), Buf("h1b")]
        st3 = ar.alloc(16, F32); B_st3 = Buf("st3")
        with nc.allow_non_contiguous_dma(reason="small parameter columns"):
            for c0 in range(0, 2048, 512):
                k.dma("gpsimd", wing[:, :, c0:c0 + 512], w_in[:, C_GG + c0:C_GG + c0 + 512].rearrange("(c p) n -> p c n", p=128), writes=[B_wing])
            k.dma("gpsimd", wml, w_mla_out.rearrange("(c p) n -> p c n", p=128), writes=[B_wml])
            k.dma("gpsimd", wo_sb, w_o.rearrange("(c p) n -> p c n", p=128), writes=[B_wo])
            k.dma("sync", bgcol, b_in[0, C_GG:C_GG + 2048].rearrange("(c p) -> p c", p=128), writes=[B_bgcol], allow_slow_non_contiguous=True)
        k.dma("sync", l1g, ln1_g.partition_broadcast(128), writes=[B_l1])
        k.dma("sync", l1b, ln1_b.partition_broadcast(128), writes=[B_l1])
        def p3a_big_loads(tb):
            t0_ = tb * 512
            k.dma("sync", ygb2[tb % 2], ygm_d[:, t0_:t0_ + 512].rearrange("(c p) t -> p c t", p=128), writes=[B_ygb2[tb % 2]])
            k.dma("sync", oTb2[tb % 2], oT_d[:, t0_:t0_ + 512].rearrange("(c p) t -> p c t", p=128), writes=[B_oTb2[tb % 2]])

        p3a_big_loads(0)
        for j in range(4):
            k.dma("gpsimd", xbf[j], x[j * 128:(j + 1) * 128, :], writes=[B_xbf[j]])
        for tb in range(8):
            t0 = tb * 512
            ygb = ygb2[tb % 2]; B_ygb = B_ygb2[tb % 2]
            oTb = oTb2[tb % 2]; B_oTb = B_oTb2[tb % 2]
            if tb + 1 < 8:
                p3a_big_loads(tb + 1)
            for j in range(4):
                k.dma("sync", xf[j], x[t0 + j * 128:t0 + (j + 1) * 128, :], writes=[B_xf[j]])
            for j in range(4):
                xb = xbf[j]; Bx = B_xbf[j]
                bank = j % 2
                psb16 = ps[bank].bitcast(BF16)
                for c in range(8):
                    k.tr(psb16[:, c * 128:(c + 1) * 128], xb[:, c * 128:(c + 1) * 128], identb, [Bx, B_identb], [psB[bank]], signal=(c == 7))
                k.cp(xT[:, :, j * 128:(j + 1) * 128], psb16.rearrange("p (c n) -> p c n", c=8), [psB[bank]], [B_xT])
            if tb + 1 < 8:
                for j in range(4):
                    k.dma("gpsimd", xbf[j], x[t0 + 512 + j * 128:t0 + 512 + (j + 1) * 128, :], writes=[B_xbf[j]])
            for (dstT, Bd, coff) in ((sgT, B_sgT, 0), (smT, B_smT, 1024)):
                for oc in range(8):
                    bank = 2 + oc % 2
                    for c in range(8):
                        k.mm(ps[bank], wing[:, c, coff + oc * 128:coff + (oc + 1) * 128], xT[:, c, :], c == 0, c == 7, [B_wing, B_xT], [psB[bank]], signal=(c == 7))
                    k.act(dstT[:, oc, :], ps[bank], AF.Sigmoid, [psB[bank], B_bgcol], [Bd], bias=bgcol[:, coff // 128 + oc:coff // 128 + oc + 1])
            for oc in range(8):
                bank = 4 + oc % 2
                for c in range(8):
                    k.mm(ps[bank], wml[:, c, oc * 128:(oc + 1) * 128], oTb[:, c, :], c == 0, c == 7, [B_wml, B_oTb], [psB[bank]], signal=(c == 7))
                k.tt(t1[oc % 2], ps[bank], smT[:, oc, :], ALU.mult, [psB[bank], B_smT], [B_t1[oc % 2]])
                k.tt(t2[oc % 2], sgT[:, oc, :], ygb[:, oc, :], ALU.mult, [B_sgT, B_ygb], [B_t2[oc % 2]], eng="gpsimd")
                k.tt(mT[:, oc, :], t1[oc % 2], t2[oc % 2], ALU.add, [B_t1[oc % 2], B_t2[oc % 2]], [B_mT])
            for j in range(4):
                xt = xf[j]; Bxf = B_xf[j]
                hh = h1[j % 2]; Bh = B_h1[j % 2]
                for half in range(2):
                    bank = 6 + half
                    for c in range(8):
                        k.mm(ps[bank], mT[:, c, j * 128:(j + 1) * 128], wo_sb[:, c, half * 512:(half + 1) * 512], c == 0, c == 7, [B_mT, B_wo], [psB[bank]], signal=(c == 7))
                    k.stt(hh[:, half * 512:(half + 1) * 512], xt[:, half * 512:(half + 1) * 512], ALPHA, ps[bank], ALU.mult, ALU.add, [Bxf, psB[bank]], [Bh])
                layer_norm_tile(hh, Bh, hh, Bh, l1g, l1b, B_l1, st3, B_st3)
                k.dma("sync", x1_d[t0 + j * 128:t0 + (j + 1) * 128, :], hh, reads=[Bh])

    if stage >= 4:
        S.barrier()
        ar.top = persist_top
        wmq_sb = ar.alloc(8 * 1024, BF16).rearrange("p (c n) -> p c n", c=8); B_wmq = Buf("wmq")
        wmo_sb = ar.alloc(8 * 1024, BF16).rearrange("p (c n) -> p c n", c=8); B_wmo = Buf("wmo")
        wkv = ar.alloc(8 * 1024, BF16).rearrange("p (c n) -> p c n", c=8); B_wkv = Buf("wkv")
        memb = ar.alloc(2 * 1024, BF16).rearrange("p (c n) -> p c n", c=2); B_memb = Buf("memb")
        memT = ar.alloc(8 * 256, BF16).rearrange("p (c n) -> p c n", c=8); B_memT = Buf("memT")
        mKT = ar.alloc(8 * 256, BF16).rearrange("p (c n) -> p c n", c=8); B_mKT = Buf("mKT")
        mV = ar.alloc(2 * 1024, BF16).rearrange("p (c n) -> p c n", c=2); B_mV = Buf("mV")
        wr = ar.alloc(8 * 72, F32).rearrange("p (c n) -> p c n", c=8); B_wr = Buf("wr")
        br_b = ar.alloc(72, F32); B_brb = Buf("brb")
        l2g = ar.alloc(1024, F32); l2b = ar.alloc(1024, F32); B_l2 = Buf("l2")
        eidx = ar.alloc(64, F32); B_eidx = Buf("eidx")
        tokid = ar.alloc(NT * 2, I32); B_tokid = Buf("tokid")
        dest_all = ar.alloc(NT * 2, I32); B_dest = Buf("dest")
        w_all = ar.alloc(NT * 2, F32); B_wall = Buf("wall")
        moe_persist = ar.top
        xbf = [ar.alloc(1024, BF16) for _ in range(4)]; B_xbf = [Buf() for _ in range(4)]
        xT = ar.alloc(8 * 512, BF16).rearrange("p (c n) -> p c n", c=8); B_xT = Buf("xT")
        xf = [ar.alloc(1024, F32) for _ in range(4)]; B_xf = [Buf() for _ in range(4)]
        q2T = ar.alloc(8 * 512, BF16).rearrange("p (c n) -> p c n", c=8); B_q2T = Buf("q2T")
        p2T = [ar.alloc(512, BF16) for _ in range(2)]; B_p2T = [Buf("p2a"), Buf("p2b")]
        rb = ar.alloc(512, F32); B_rb = Buf("rb")
        o2T = ar.alloc(8 * 512, BF16).rearrange("p (c n) -> p c n", c=8); B_o2T = Buf("o2T")
        h2 = [ar.alloc(1024, F32) for _ in range(2)]; B_h2 = [Buf("h2a"), Buf("h2b")]
        x2b = [ar.alloc(1024, BF16) for _ in range(2)]; B_x2b = [Buf("x2ba"), Buf("x2bb")]
        x2T = ar.alloc(8 * 128, F32).rearrange("p (c n) -> p c n", c=8); B_x2T = Buf("x2T")
        st3 = ar.alloc(16, F32); B_st3 = Buf("st3")
        lg = ar.alloc(72, F32); B_lg = Buf("lg")
        rt = ar.alloc(256, F32); B_rt = Buf("rt")
        Mt = ar.alloc(64, BF16); B_Mt = Buf("Mt")
        M12 = ar.alloc(128, F32); B_M12 = Buf("M12")
        base = ar.alloc(64, F32); B_base = Buf("base")
        rank = ar.alloc(64, F32); B_rank = Buf("rank")
        tris = ar.alloc(128, BF16); B_tris = Buf("tris")
        zt = ar.alloc(256, I32); B_zt = Buf("zt")

        with nc.allow_non_contiguous_dma(reason="small parameter columns"):
            k.dma("gpsimd", wmq_sb, w_mq.rearrange("(c p) n -> p c n", p=128), writes=[B_wmq])
            k.dma("gpsimd", wmo_sb, w_mo.rearrange("(c p) n -> p c n", p=128), writes=[B_wmo])
            k.dma("gpsimd", wkv, w_mk.rearrange("(c p) n -> p c n", p=128), writes=[B_wkv])
            k.dma("gpsimd", memb, mem.rearrange("(c p) n -> p c n", p=128), writes=[B_memb])
            k.dma("sync", wr[:, :, 0:8], w_gr.rearrange("(c p) n -> p c n", p=128), writes=[B_wr], allow_slow_non_contiguous=True)
            k.dma("sync", wr[:, :, 8:72], w_er.rearrange("(c p) n -> p c n", p=128), writes=[B_wr], allow_slow_non_contiguous=True)
            k.dma("sync", br_b[:, 0:8], b_gr.partition_broadcast(128), writes=[B_brb])
            k.dma("sync", br_b[:, 8:72], b_er.partition_broadcast(128), writes=[B_brb])
        k.dma("sync", l2g, ln2_g.partition_broadcast(128), writes=[B_l2])
        k.dma("sync", l2b, ln2_b.partition_broadcast(128), writes=[B_l2])
        k.cp(eidx, cst_sb[:, 264:328], [B_cst], [B_eidx])
        k.cp(tris, cst_sb[:, 328:456], [B_cst], [B_tris])
        k.cp(tokid.rearrange("p (t o) -> p t o", o=2), cst_sb[:, 456:488].unsqueeze(2).to_broadcast([128, NT, 2]), [B_cst], [B_tokid])
        k.memset(base, 0.0, [B_base])
        k.memset(zt, 0, [B_zt])
        k.dma("sync", tokof_d.rearrange("(p n) o -> p (n o)", p=128), zt, reads=[B_zt])
        B_tokof = Buf("tokof")
        for mt in range(2):
            psb16 = ps[mt].bitcast(BF16)
            for c in range(8):
                k.tr(psb16[:, c * 128:(c + 1) * 128], memb[:, mt, c * 128:(c + 1) * 128], identb, [B_memb, B_identb], [psB[mt]], signal=(c == 7))
            k.cp(memT[:, :, mt * 128:(mt + 1) * 128], psb16.rearrange("p (c n) -> p c n", c=8), [psB[mt]], [B_memT])
        for oc in range(8):
            bank = 2 + oc % 2
            for c in range(8):
                k.mm(ps[bank][:, 0:256], wkv[:, c, oc * 128:(oc + 1) * 128], memT[:, c, :], c == 0, c == 7, [B_wkv, B_memT], [psB[bank]], signal=(c == 7))
            k.cp(mKT[:, oc, :], ps[bank][:, 0:256], [psB[bank]], [B_mKT])
        k.dma("gpsimd", wkv, w_mv.rearrange("(c p) n -> p c n", p=128), reads=[], writes=[B_wkv])
        for mt in range(2):
            for half in range(2):
                bank = 4 + half
                for c in range(8):
                    k.mm(ps[bank], memT[:, c, mt * 128:(mt + 1) * 128], wkv[:, c, half * 512:(half + 1) * 512], c == 0, c == 7, [B_memT, B_wkv], [psB[bank]], signal=(c == 7))
                k.cp(mV[:, mt, half * 512:(half + 1) * 512], ps[bank], [psB[bank]], [B_mV])

        for j in range(4):
            k.dma("gpsimd", xbf[j], x1_d[j * 128:(j + 1) * 128, :], writes=[B_xbf[j]])
        for tb in range(8):
            t0 = tb * 512
            for j in range(4):
                k.dma("sync", xf[j], x1_d[t0 + j * 128:t0 + (j + 1) * 128, :], writes=[B_xf[j]])
            for j in range(4):
                xb = xbf[j]; Bx = B_xbf[j]
                bank = j % 2
                psb16 = ps[bank].bitcast(BF16)
                for c in range(8):
                    k.tr(psb16[:, c * 128:(c + 1) * 128], xb[:, c * 128:(c + 1) * 128], identb, [Bx, B_identb], [psB[bank]], signal=(c == 7))
                k.cp(xT[:, :, j * 128:(j + 1) * 128], psb16.rearrange("p (c n) -> p c n", c=8), [psB[bank]], [B_xT])
            if tb + 1 < 8:
                for j in range(4):
                    k.dma("gpsimd", xbf[j], x1_d[t0 + 512 + j * 128:t0 + 512 + (j + 1) * 128, :], writes=[B_xbf[j]])
            for oc in range(8):
                bank = 2 + oc % 2
                for c in range(8):
                    k.mm(ps[bank], wmq_sb[:, c, oc * 128:(oc + 1) * 128], xT[:, c, :], c == 0, c == 7, [B_wmq, B_xT], [psB[bank]], signal=(c == 7))
                k.cp(q2T[:, oc, :], ps[bank], [psB[bank]], [B_q2T], eng="scalar")
            for hm in range(4):
                for mt in range(2):
                    for dc in range(2):
                        k.mm(ps[4 + mt], mKT[:, hm * 2 + dc, mt * 128:(mt + 1) * 128], q2T[:, hm * 2 + dc, :], dc == 0, dc == 1, [B_mKT, B_q2T], [psB[4 + mt]], signal=(dc == 1))
                    k.act(p2T[mt], ps[4 + mt], AF.Exp, [psB[4 + mt]], [B_p2T[mt]], scale=1.0 / 16.0)
                for mt in range(2):
                    k.mm(ps[6], onesb, p2T[mt], mt == 0, mt == 1, [B_onesb, B_p2T[mt]], [psB[6]], signal=(mt == 1))
                S.op("vector", lambda e: e.reciprocal(rb, ps[6]), [psB[6]], [B_rb])
                for dvc in range(2):
                    bank = 2 + dvc
                    for mt in range(2):
                        k.mm(ps[bank], mV[:, mt, hm * 256 + dvc * 128:hm * 256 + (dvc + 1) * 128], p2T[mt], mt == 0, mt == 1, [B_mV, B_p2T[mt]], [psB[bank]], signal=(mt == 1))
                    k.tt(o2T[:, hm * 2 + dvc, :], ps[bank], rb, ALU.mult, [psB[bank], B_rb], [B_o2T])
            for j in range(4):
                ti = tb * 4 + j
                xt = xf[j]; Bxf = B_xf[j]
                hh = h2[j % 2]; Bh = B_h2[j % 2]
                xb2 = x2b[j % 2]; Bxb2 = B_x2b[j % 2]
                for half in range(2):
                    bank = 0 + half
                    for c in range(8):
                        k.mm(ps[bank], o2T[:, c, j * 128:(j + 1) * 128], wmo_sb[:, c, half * 512:(half + 1) * 512], c == 0, c == 7, [B_o2T, B_wmo], [psB[bank]], signal=(c == 7))
                    k.stt(hh[:, half * 512:(half + 1) * 512], xt[:, half * 512:(half + 1) * 512], ALPHA, ps[bank], ALU.mult, ALU.add, [Bxf, psB[bank]], [Bh])
                layer_norm_tile(hh, Bh, hh, Bh, l2g, l2b, B_l2, st3, B_st3)
                k.dma("sync", x2_d[t0 + j * 128:t0 + (j + 1) * 128, :], hh, reads=[Bh])
                k.cp(xb2, hh, [Bh], [Bxb2])
                k.dma("sync", x2b_d[t0 + j * 128:t0 + (j + 1) * 128, :], xb2, reads=[Bxb2])
                for c in range(8):
                    k.tr(ps[7][:, (c % 4) * 128:(c % 4 + 1) * 128], hh[:, c * 128:(c + 1) * 128], identf, [Bh, B_cst], [psB[7]], signal=(c % 4 == 3))
                    if c % 4 == 3:
                        k.cp(x2T[:, c - 3:c + 1, :], ps[7].rearrange("p (c n) -> p c n", c=4), [psB[7]], [B_x2T])
                for c in range(8):
                    k.mm(ps[6][:, 0:72], x2T[:, c, :], wr[:, c, :], c == 0, c == 7, [B_x2T, B_wr], [psB[6]], signal=(c == 7))
                k.tt(lg, ps[6][:, 0:72], br_b, ALU.add, [psB[6], B_brb], [B_lg])
                V = lambda a, bb: rt[:, a:bb]
                RW = ([B_lg, B_rt, B_eidx], [B_rt])
                S.op("vector", lambda e: e.reduce_max(V(0, 1), lg[:, 0:8], AX.X), [B_lg], [B_rt])
                k.ts(V(8, 16), lg[:, 0:8], V(0, 1), None, ALU.subtract, None, *RW)
                k.act(V(16, 24), V(8, 16), AF.Exp, [B_rt], [B_rt], accum_out=V(1, 2))
                S.op("vector", lambda e: e.reciprocal(V(2, 3), V(1, 2)), [B_rt], [B_rt])
                k.ts(V(24, 32), V(8, 16), 0.0, None, ALU.is_equal, None, *RW)
                k.tt(V(64, 128).rearrange("p (g e) -> p g e", g=8), lg[:, 8:72].rearrange("p (g e) -> p g e", g=8),
                     V(24, 32).unsqueeze(2).to_broadcast([128, 8, 8]), ALU.mult, *RW)
                S.op("vector", lambda e: e.tensor_reduce(V(32, 40), V(64, 128).rearrange("p (g e) -> p e g", g=8), AX.X, ALU.add), [B_rt], [B_rt])
                S.op("vector", lambda e: e.reduce_max(V(3, 4), V(32, 40), AX.X), [B_rt], [B_rt])
                k.ts(V(40, 48), V(32, 40), V(3, 4), None, ALU.is_equal, None, *RW)
                k.stt(V(48, 56), V(40, 48), -1e30, V(32, 40), ALU.mult, ALU.add, *RW)
                S.op("vector", lambda e: e.reduce_max(V(4, 5), V(48, 56), AX.X), [B_rt], [B_rt])
                k.ts(V(56, 64), V(48, 56), V(4, 5), None, ALU.is_equal, None, *RW)
                k.tt(V(5, 6), V(4, 5), V(3, 4), ALU.subtract, *RW)
                k.act(V(6, 7), V(5, 6), AF.Exp, [B_rt], [B_rt])
                k.ts(V(7, 8), V(6, 7), 1.0, None, ALU.add, None, *RW)
                S.op("vector", lambda e: e.reciprocal(V(7, 8), V(7, 8)), [B_rt], [B_rt])
                k.tt(V(128, 129), V(7, 8), V(2, 3), ALU.mult, *RW)
                k.tt(V(129, 130), V(128, 129), V(6, 7), ALU.mult, *RW)
                for kk in range(2):
                    k.tt(M12[:, kk * 64:(kk + 1) * 64].rearrange("p (g e) -> p g e", g=8), V(24, 32).unsqueeze(2).to_broadcast([128, 8, 8]),
                         V(40 + 16 * kk, 48 + 16 * kk).unsqueeze(1).to_broadcast([128, 8, 8]), ALU.mult, [B_rt], [B_M12])
                k.tt(Mt, M12[:, 0:64], M12[:, 64:128], ALU.add, [B_M12], [B_Mt])
                k.mm(ps[6][:, 128:192], tris, Mt, True, True, [B_tris, B_Mt], [psB[6]], signal=True)
                k.tt(rank, ps[6][:, 128:192], base, ALU.add, [psB[6], B_base], [B_rank])
                k.mm(ps[6][:, 256:320], onesb, Mt, True, True, [B_onesb, B_Mt], [psB[6]], signal=True)
                k.tt(base, ps[6][:, 256:320], base, ALU.add, [psB[6], B_base], [B_base])
                for kk in range(2):
                    k.tt(V(130, 194), M12[:, kk * 64:(kk + 1) * 64], rank, ALU.mult, [B_M12, B_rank, B_rt], [B_rt])
                    S.op("vector", lambda e, kk=kk: e.reduce_sum(V(200 + kk, 201 + kk), V(130, 194), AX.X), [B_rt], [B_rt])
                    k.tt(V(130, 194), M12[:, kk * 64:(kk + 1) * 64], eidx, ALU.mult, [B_M12, B_eidx, B_rt], [B_rt])
                    S.op("vector", lambda e, kk=kk: e.reduce_sum(V(202 + kk, 203 + kk), V(130, 194), AX.X), [B_rt], [B_rt])
                    k.ts(V(204 + kk, 205 + kk), V(200 + kk, 201 + kk), 256.0, None, ALU.is_lt, None, *RW)
                    k.stt(V(206 + kk, 207 + kk), V(202 + kk, 203 + kk), 256.0, V(200 + kk, 201 + kk), ALU.mult, ALU.add, *RW)
                    k.ts(V(208 + kk, 209 + kk), V(204 + kk, 205 + kk), -1.0e6, 1.0e6, ALU.mult, ALU.add, *RW)
                    k.tt(V(206 + kk, 207 + kk), V(206 + kk, 207 + kk), V(208 + kk, 209 + kk), ALU.add, *RW)
                    k.cp(dest_all[:, ti * 2 + kk:ti * 2 + kk + 1], V(206 + kk, 207 + kk), [B_rt], [B_dest])
                    k.tt(w_all[:, ti * 2 + kk:ti * 2 + kk + 1], V(128 + kk, 129 + kk), V(204 + kk, 205 + kk), ALU.mult, [B_rt], [B_wall])
                    S.dma("gpsimd", None, None, reads=[B_dest, B_tokid, B_zt], writes=[B_tokof],
                          fn=lambda e, col=ti * 2 + kk, ti=ti: e.indirect_dma_start(
                              out=tokof_d, out_offset=bass.IndirectOffsetOnAxis(ap=dest_all[:, col:col + 1], axis=0),
                              in_=tokid[:, 2 * ti:2 * ti + 2], in_offset=None, bounds_check=bcreg(e), oob_is_err=False))

        S.barrier()
        ar.top = moe_persist
        l3g = ar.alloc(1024, F32); l3b = ar.alloc(1024, F32); B_l3 = Buf("l3")
        k.dma("sync", l3g, ln3_g.partition_broadcast(128), writes=[B_l3])
        k.dma("sync", l3b, ln3_b.partition_broadcast(128), writes=[B_l3])
        moe_work = ar.top
        NB = 4
        idx = [ar.alloc(2, I32) for _ in range(NB)]; B_idx = [Buf(f"idx{i}") for i in range(NB)]
        Xe = [ar.alloc(2 * 1024, BF16).rearrange("p (s n) -> p s n", s=2) for _ in range(NB)]; B_Xe = [Buf(f"Xe{i}") for i in range(NB)]
        XeT = [ar.alloc(8 * 256, BF16).rearrange("p (c n) -> p c n", c=8) for _ in range(NB)]; B_XeT = [Buf(f"XeT{i}") for i in range(NB)]
        wg = [ar.alloc(8 * 256, BF16).rearrange("p (c n) -> p c n", c=8) for _ in range(NB)]; B_wg = [Buf(f"wg{i}") for i in range(NB)]
        wu = [ar.alloc(8 * 256, BF16).rearrange("p (c n) -> p c n", c=8) for _ in range(NB)]; B_wu = [Buf(f"wu{i}") for i in range(NB)]
        wd = [ar.alloc(2 * 1024, BF16).rearrange("p (c n) -> p c n", c=2) for _ in range(NB)]; B_wd = [Buf(f"wd{i}") for i in range(NB)]
        sg = [ar.alloc(256, F32) for _ in range(2)]; B_sg = [Buf("sga"), Buf("sgb")]
        aT = [ar.alloc(2 * 256, BF16).rearrange("p (c n) -> p c n", c=2) for _ in range(NB)]; B_aT = [Buf(f"aT{i}") for i in range(NB)]
        yb = [ar.alloc(1024, BF16) for _ in range(2)]; B_yb = [Buf("yba"), Buf("ybb")]
        B_yd = Buf("yd")
        def moe_loads(ex):
            b = ex % NB
            for s_ in range(2):
                k.dma("sync", idx[b][:, s_:s_ + 1], tokof_d[ex * 256 + s_ * 128:ex * 256 + (s_ + 1) * 128, 0:1], reads=[B_tokof], writes=[B_idx[b]], allow_slow_non_contiguous=True)
            for s_ in range(2):
                S.dma("gpsimd", None, None, reads=[B_idx[b]], writes=[B_Xe[b]],
                      fn=lambda e, b=b, s_=s_: e.indirect_dma_start(
                          out=Xe[b][:, s_, :], out_offset=None, in_=x2b_d,
                          in_offset=bass.IndirectOffsetOnAxis(ap=idx[b][:, s_:s_ + 1], axis=0)))
            k.dma("gpsimd", wg[b], w_eg[ex * 1024:(ex + 1) * 1024, :].rearrange("(c p) n -> p c n", p=128), writes=[B_wg[b]])
            k.dma("gpsimd", wu[b], w_eu[ex * 1024:(ex + 1) * 1024, :].rearrange("(c p) n -> p c n", p=128), writes=[B_wu[b]])
            k.dma("gpsimd", wd[b], w_ed[ex * 256:(ex + 1) * 256, :].rearrange("(c p) n -> p c n", p=128), writes=[B_wd[b]])

        def moe_compute(ex):
            b = ex % NB
            for s_ in range(2):
                bank = s_
                psb16 = ps[bank].bitcast(BF16)
                for c in range(8):
                    k.tr(psb16[:, c * 128:(c + 1) * 128], Xe[b][:, s_, c * 128:(c + 1) * 128], identb, [B_Xe[b], B_identb], [psB[bank]], signal=(c == 7))
                k.cp(XeT[b][:, :, s_ * 128:(s_ + 1) * 128], psb16.rearrange("p (c n) -> p c n", c=8), [psB[bank]], [B_XeT[b]])
            for fc in range(2):
                for c in range(8):
                    k.mm(ps[2 + fc][:, 0:256], wg[b][:, c, fc * 128:(fc + 1) * 128], XeT[b][:, c, :], c == 0, c == 7, [B_wg[b], B_XeT[b]], [psB[2 + fc]], signal=(c == 7))
                for c in range(8):
                    k.mm(ps[2 + fc][:, 256:512], wu[b][:, c, fc * 128:(fc + 1) * 128], XeT[b][:, c, :], c == 0, c == 7, [B_wu[b], B_XeT[b]], [psB[2 + fc]], signal=(c == 7))
                k.act(sg[fc], ps[2 + fc][:, 0:256], AF.Silu, [psB[2 + fc]], [B_sg[fc]])
                k.tt(aT[b][:, fc, :], sg[fc], ps[2 + fc][:, 256:512], ALU.mult, [B_sg[fc], psB[2 + fc]], [B_aT[b]])
            for s_ in range(2):
                yy = yb[s_]
                for half in range(2):
                    bank = 4 + s_ * 2 + half
                    for fc in range(2):
                        k.mm(ps[bank], aT[b][:, fc, s_ * 128:(s_ + 1) * 128], wd[b][:, fc, half * 512:(half + 1) * 512], fc == 0, fc == 1, [B_aT[b], B_wd[b]], [psB[bank]], signal=(fc == 1))
                    k.cp(yy[:, half * 512:(half + 1) * 512], ps[bank], [psB[bank]], [B_yb[s_]], eng=("vector" if half == 0 else "scalar"))
                k.dma("sync", yd_d[ex * 256 + s_ * 128:ex * 256 + (s_ + 1) * 128, :], yy, reads=[B_yb[s_]])

        PF = 2
        for ex in range(PF):
            moe_loads(ex)
        for ex in range(64):
            if ex + PF < 64:
                moe_loads(ex + PF)
            moe_compute(ex)
        S.barrier()
        ar.top = moe_work
        NC4 = 4
        yg = [[ar.alloc(1024, BF16) for _ in range(2)] for _ in range(NC4)]; B_yg = [[Buf(), Buf()] for _ in range(NC4)]
        xf = [ar.alloc(1024, F32) for _ in range(NC4)]; B_xf = [Buf() for _ in range(NC4)]
        h3 = [ar.alloc(1024, F32) for _ in range(NC4)]; B_h3 = [Buf() for _ in range(NC4)]
        st4 = ar.alloc(16, F32); B_st4 = Buf("st4")
        def cmb_loads(ti):
            pb = ti % NC4
            k.dma("sync", xf[pb], x2_d[ti * 128:(ti + 1) * 128, :], writes=[B_xf[pb]])
            for kk in range(2):
                k.memset(yg[pb][kk], 0.0, [B_yg[pb][kk]], eng="gpsimd")
                S.dma("gpsimd", None, None, reads=[B_dest, B_yd], writes=[B_yg[pb][kk]],
                      fn=lambda e, pb=pb, kk=kk, col=ti * 2 + kk: e.indirect_dma_start(
                          out=yg[pb][kk], out_offset=None, in_=yd_d,
                          in_offset=bass.IndirectOffsetOnAxis(ap=dest_all[:, col:col + 1], axis=0),
                          bounds_check=bcreg(e), oob_is_err=False))

        def cmb_compute(ti):
            pb = ti % NC4
            hh = h3[pb]; Bh = B_h3[pb]
            k.ts(hh, yg[pb][0], w_all[:, ti * 2:ti * 2 + 1], None, ALU.mult, None, [B_yg[pb][0], B_wall], [Bh])
            k.stt(hh, yg[pb][1], w_all[:, ti * 2 + 1:ti * 2 + 2], hh, ALU.mult, ALU.add, [B_yg[pb][1], B_wall, Bh], [Bh])
            k.stt(hh, xf[pb], ALPHA, hh, ALU.mult, ALU.add, [B_xf[pb], Bh], [Bh])
            layer_norm_tile(hh, Bh, hh, Bh, l3g, l3b, B_l3, st4, B_st4)
            k.dma("sync", out[ti * 128:(ti + 1) * 128, :], hh, reads=[Bh])

        for ti in range(2):
            cmb_loads(ti)
        for ti in range(NT):
            if ti + 2 < NT:
                cmb_loads(ti + 2)
            cmb_compute(ti)

    S.barrier()
    S.emit()
    return nc, in_names


def make_consts():
    c = np.zeros((128, 512), np.float32)
    c[:, 0:128] = np.eye(128, dtype=np.float32)
    c[:, 128:256] = np.triu(np.ones((128, 128), np.float32))
    half = 16
    inv_freq = (10000.0 ** (-np.arange(half, dtype=np.float32) / half)).astype(np.float32)
    for p in range(64, 96):
        j = (p - 64) % 16
        c[p, 256] = inv_freq[j]
        first = (p - 64) < 16
        c[p, 257] = -1.0 if first else 1.0
    c[:, 264:328] = np.arange(64, dtype=np.float32)[None, :]
    c[:, 328:456] = np.triu(np.ones((128, 128), np.float32), k=1)
    c[:, 456:488] = (np.arange(32, dtype=np.float32)[None, :] * 128 + np.arange(128, dtype=np.float32)[:, None])
    c[:, 260] = LN_EPS
    c[:, 261] = 384 * RMS_EPS
    c[:, 262] = 256 * RMS_EPS
    return c


_CACHE = {}


def kernel(**inputs):
    n = 8
    if "nc" not in _CACHE:
        _CACHE["nc"] = build_nc(stage=4)[0]
    nc = _CACHE["nc"]
    consts = make_consts()
    shared = {}
    for kname, v in inputs.items():
        if kname in ("x", "mem", "positions"):
            continue
        a = np.ascontiguousarray(np.asarray(v)[0])
        if a.ndim == 1 or kname == "gm_b_s":
            a = a.reshape(1, -1)
        elif a.ndim == 3:
            a = a.reshape(-1, a.shape[-1])
        shared[kname] = a
    shared["consts"] = consts
    in_maps = []
    for b in range(n):
        m = dict(shared)
        m["x"] = np.ascontiguousarray(np.asarray(inputs["x"])[b])
        m["mem"] = np.ascontiguousarray(np.asarray(inputs["mem"])[b])
        m["positions"] = np.ascontiguousarray(np.asarray(inputs["positions"])[b]).reshape(1, -1).astype(np.int32)
        in_maps.append(m)
    res = run_bass_kernel_spmd(nc, in_maps, core_ids=list(range(n)))
    return np.stack([np.asarray(r["out"]) for r in res.results], axis=0).astype(np.float32)
```

```python
import numpy as np
import concourse.bass as bass
import concourse.mybir as mybir
from concourse.bass_utils import run_bass_kernel_spmd

F32 = mybir.dt.float32
BF16 = mybir.dt.bfloat16
I32 = mybir.dt.int32
AF = mybir.ActivationFunctionType
ALU = mybir.AluOpType
AX = mybir.AxisListType

NDMA = 24
ENGS = ("tensor", "vector", "scalar", "gpsimd", "sync")

S_TOK = 4096
D = 1024
NT = S_TOK // 128
IN_COLS = 4768
C_U, C_V, C_CQ, C_CKV, C_KR, C_GG, C_GM = 0, 1024, 2048, 2432, 2688, 2720, 3744
NA = 2720
ALPHA = 2.0 ** 0.25
LN_EPS = 1e-5
RMS_EPS = 1e-6
PI = float(np.pi)
TWO_PI = float(2 * np.pi)


class Buf:
    __slots__ = ("name", "w", "r")

    def __init__(self, name=""):
        self.name = name
        self.w = None
        self.r = {}


class Sched:
    def __init__(self, nc):
        self.nc = nc
        self.eng = {}
        for n in ENGS:
            self.eng[n] = dict(sem=nc.alloc_semaphore("s_" + n), count=0, last=None,
                               seen={}, prog=[], cur=None)
        self.dma_sems = [nc.alloc_semaphore(f"dsem{i}") for i in range(NDMA)]
        self.dma_cnt = [0] * NDMA
        self.dma_rr = 0
        self.n_ops = 0

    def _need(self, en, key, val, kind):
        E = self.eng[en]
        if key[0] == "e":
            X = self.eng[key[1]]
            if key[1] == en:
                if en == "tensor":
                    return
                if kind != "RAW":
                    return
            if X["count"] < val:
                assert X["count"] == val - 1 and X["last"] is not None and not X["last"]["signal"]
                X["last"]["signal"] = True
                X["count"] = val
        if E["seen"].get(key, 0) >= val:
            return
        E["seen"][key] = val
        E["cur"].append((key, val))

    def _deps(self, en, reads, writes):
        E = self.eng[en]
        E["cur"] = []
        for b in reads:
            if b.w is not None:
                self._need(en, b.w[0], b.w[1], "RAW")
        for b in writes:
            if b.w is not None:
                self._need(en, b.w[0], b.w[1], "WAW")
            for k, v in b.r.items():
                self._need(en, k, v, "WAR")
        return E["cur"]

    def op(self, en, fn, reads=(), writes=(), signal=False):
        E = self.eng[en]
        waits = self._deps(en, reads, writes)
        rec = dict(fn=fn, waits=waits, signal=False, dma=None)
        E["prog"].append(rec)
        val = E["count"] + 1
        key = ("e", en)
        for b in reads:
            if b.r.get(key, 0) < val:
                b.r[key] = val
        for b in writes:
            b.w = (key, val)
            b.r = {}
        E["last"] = rec
        if signal:
            rec["signal"] = True
            E["count"] = val
        self.n_ops += 1
        return rec

    def dma(self, qn, out, in_, reads=(), writes=(), fn=None, **kw):
        waits = self._deps(qn, reads, writes)
        E = self.eng[qn]
        i = self.dma_rr
        self.dma_rr = (i + 1) % NDMA
        key = ("d", i)
        prev = self.dma_cnt[i]
        if prev > 0:
            self._need(qn, key, prev, "RAW")
        val = prev + 16
        self.dma_cnt[i] = val
        if fn is None:
            fn = lambda e: e.dma_start(out=out, in_=in_, **kw)
        rec = dict(fn=fn, waits=waits, signal=False, dma=i)
        E["prog"].append(rec)
        for b in reads:
            b.r[key] = val
        for b in writes:
            b.w = (key, val)
            b.r = {}
        self.n_ops += 1
        return rec

    def barrier(self):
        targets = []
        for n in ENGS:
            X = self.eng[n]
            if X["last"] is not None and not X["last"]["signal"]:
                X["last"]["signal"] = True
                X["count"] += 1
            if X["count"] > 0:
                targets.append((("e", n), X["count"]))
        for i in range(NDMA):
            if self.dma_cnt[i] > 0:
                targets.append((("d", i), self.dma_cnt[i]))
        for n in ENGS:
            E = self.eng[n]
            waits = []
            for key, val in targets:
                if key == ("e", n):
                    continue
                if E["seen"].get(key, 0) >= val:
                    continue
                E["seen"][key] = val
                waits.append((key, val))
            if waits:
                E["prog"].append(dict(fn=None, waits=waits, signal=False, dma=None))

    def _sem(self, key):
        return self.eng[key[1]]["sem"] if key[0] == "e" else self.dma_sems[key[1]]

    def emit(self):
        nc = self.nc
        with nc.Block() as block:
            def mk(en):
                E = self.eng[en]

                def body(e):
                    for rec in E["prog"]:
                        for key, val in rec["waits"]:
                            e.wait_ge(self._sem(key), val)
                        if rec["fn"] is None:
                            continue
                        ins = rec["fn"](e)
                        if rec["dma"] is not None:
                            ins.then_inc(self.dma_sems[rec["dma"]], 16)
                        elif rec["signal"]:
                            ins.then_inc(E["sem"], 1)
                return body
            block.tensor(mk("tensor"))
            block.vector(mk("vector"))
            block.scalar(mk("scalar"))
            block.gpsimd(mk("gpsimd"))
            block.sync(mk("sync"))


class Arena:
    def __init__(self, nc, nbytes):
        self.t = nc.alloc_sbuf_tensor("arena", [128, nbytes // 2], BF16)
        self.A = self.t.ap()
        self.top = 0
        self.cap = nbytes

    def alloc(self, n_elem, dtype):
        sz = 2 if dtype == BF16 else 4
        nbytes = n_elem * sz
        off = (self.top + 63) // 64 * 64
        self.top = off + nbytes
        assert self.top <= self.cap, ("SBUF arena overflow", self.top, self.cap)
        v = self.A[:, off // 2:(off + nbytes) // 2]
        if dtype != BF16:
            v = v.bitcast(dtype)
        return v


class K:
    def __init__(self, nc):
        self.nc = nc
        self.S = Sched(nc)
        self.ar = Arena(nc, 206 * 1024)
        self.psbig = [nc.alloc_psum_tensor(f"psb{i}", [128, 1024], F32).ap() for i in range(4)]
        self.ps = [self.psbig[i // 2][:, (i % 2) * 512:(i % 2 + 1) * 512] for i in range(8)]
        self.psB = [Buf(f"ps{i}") for i in range(8)]

    def mm(self, out, lhsT, rhs, start, stop, reads, writes, signal=False):
        return self.S.op("tensor", lambda e: e.matmul(out, lhsT, rhs, start=start, stop=stop), reads, writes, signal)

    def tr(self, out, in_, ident, reads, writes, signal=False):
        return self.S.op("tensor", lambda e: e.transpose(out, in_, ident), reads, writes, signal)

    def act(self, out, in_, func, reads, writes, bias=None, scale=1.0, accum_out=None):
        kw = {}
        if bias is not None:
            kw["bias"] = bias
        if accum_out is not None:
            kw["accum_out"] = accum_out
        return self.S.op("scalar", lambda e: e.activation(out, in_, func, scale=scale, **kw), reads, writes)

    def tt(self, out, a, b, op, reads, writes, eng="vector"):
        return self.S.op(eng, lambda e: e.tensor_tensor(out, a, b, op), reads, writes)

    def ts(self, out, a, s1, s2, op0, op1, reads, writes, eng="vector"):
        if s2 is None:
            return self.S.op(eng, lambda e: e.tensor_scalar(out, a, s1, None, op0), reads, writes)
        return self.S.op(eng, lambda e: e.tensor_scalar(out, a, s1, s2, op0, op1), reads, writes)

    def stt(self, out, in0, scalar, in1, op0, op1, reads, writes, eng="vector"):
        return self.S.op(eng, lambda e: e.scalar_tensor_tensor(out, in0, scalar, in1, op0, op1), reads, writes)

    def cp(self, out, in_, reads, writes, eng="vector"):
        if eng == "scalar":
            return self.S.op(eng, lambda e: e.copy(out, in_), reads, writes)
        return self.S.op(eng, lambda e: e.tensor_copy(out, in_), reads, writes)

    def memset(self, ap, val, writes, eng="vector"):
        return self.S.op(eng, lambda e: e.memset(ap, val), (), writes)

    def dma(self, q, out, in_, reads=(), writes=(), **kw):
        return self.S.dma(q, out, in_, reads, writes, **kw)


def build_nc(stage=99, debug=False):
    nc = bass.Bass("TRN2", target_bir_lowering=False)

    in_names = []

    def din(name, shape, dt=F32):
        in_names.append(name)
        return nc.dram_tensor(name, list(shape), dt, kind="ExternalInput").ap()

    x = din("x", [S_TOK, D])
    mem = din("mem", [256, D])
    pos = din("positions", [1, S_TOK], I32)
    w_in = din("w_in", [D, IN_COLS])
    b_in = din("b_in", [1, IN_COLS])
    gm_ln_g = din("gm_ln_g", [1, 1024])
    gm_ln_b = din("gm_ln_b", [1, 1024])
    gm_w_s = din("gm_w_s", [8 * 128, 128])
    gm_b_s = din("gm_b_s", [1, 1024])
    w_gm_out = din("w_gm_out", [1024, 1024])
    q_norm_g = din("mla_q_norm_g", [1, 384])
    kv_norm_g = din("mla_kv_norm_g", [1, 256])
    w_uq = din("w_uq", [384, 1536])
    w_uk = din("w_uk", [256, 1024])
    w_uv = din("w_uv", [256, 1024])
    w_mla_out = din("w_mla_out", [1024, 1024])
    w_o = din("w_o", [1024, 1024])
    ln1_g = din("ln1_g", [1, 1024]); ln1_b = din("ln1_b", [1, 1024])
    w_mq = din("w_mq", [1024, 1024]); w_mk = din("w_mk", [1024, 1024])
    w_mv = din("w_mv", [1024, 1024]); w_mo = din("w_mo", [1024, 1024])
    ln2_g = din("ln2_g", [1, 1024]); ln2_b = din("ln2_b", [1, 1024])
    w_gr = din("w_group_router", [1024, 8]); b_gr = din("b_group_router", [1, 8])
    w_er = din("w_expert_router", [1024, 64]); b_er = din("b_expert_router", [1, 64])
    if stage >= 4:
        w_eg = din("w_exp_gate", [64 * 1024, 256]); w_eu = din("w_exp_up", [64 * 1024, 256])
        w_ed = din("w_exp_down", [64 * 256, 1024])
    ln3_g = din("ln3_g", [1, 1024]); ln3_b = din("ln3_b", [1, 1024])
    cst = din("consts", [128, 512])
    out = nc.dram_tensor("out", [S_TOK, D], F32, kind="ExternalOutput").ap()

    dk = dict(kind="ExternalOutput") if debug else {}
    ygm_d = nc.dram_tensor("ygm_d", [1024, S_TOK], BF16, **dk).ap()
    cqn_d = nc.dram_tensor("cqn_d", [384, S_TOK], BF16, **dk).ap()
    ckvn_d = nc.dram_tensor("ckvn_d", [256, S_TOK], BF16, **dk).ap()
    kr_d = nc.dram_tensor("kr_d", [32, S_TOK], BF16, **dk).ap()
    oT_d = nc.dram_tensor("oT_d", [1024, S_TOK], BF16).ap()

    dbg = {}
    if debug:
        for nm, shp in (("d_gated", [1024, S_TOK]),):
            dbg[nm] = nc.dram_tensor(nm, shp, F32, kind="ExternalOutput").ap()

    k = K(nc)
    S = k.S
    ar = k.ar
    ps, psB = k.ps, k.psB

    cst_sb = ar.alloc(512, F32); B_cst = Buf("cst")
    identf = cst_sb[:, 0:128]
    trif = cst_sb[:, 128:256]
    rc = cst_sb[:, 256:264]
    identb = ar.alloc(128, BF16); B_identb = Buf("identb")
    trib = ar.alloc(128, BF16); B_trib = Buf("trib")
    onesb = ar.alloc(128, BF16); B_onesb = Buf("onesb")
    cosT = ar.alloc(S_TOK, BF16); B_cos = Buf("cos")
    sinT = ar.alloc(S_TOK, BF16); B_sin = Buf("sin")
    persist_top = ar.top

    k.dma("sync", cst_sb, cst, writes=[B_cst])
    k.cp(identb, identf, [B_cst], [B_identb])
    k.cp(trib, trif, [B_cst], [B_trib])
    k.memset(onesb, 1.0, [B_onesb])

    p1_base = ar.top
    winA = ar.alloc(8 * NA, BF16).rearrange("p (c n) -> p c n", c=8); B_winA = Buf("winA")
    wkr = ar.alloc(8 * 96, BF16).rearrange("p (c n) -> p c n", c=8); B_wkr = Buf("wkr")
    wkrs = ar.alloc(8 * 96, BF16).rearrange("p (c n) -> p c n", c=8); B_wkrs = Buf("wkrs")
    wgo = ar.alloc(8 * 1024, BF16).rearrange("p (c n) -> p c n", c=8); B_wgo = Buf("wgo")
    bcol = ar.alloc(21, F32); B_bcol = Buf("bcol")
    bkr = ar.alloc(2, F32); B_bkr = Buf("bkr")
    bv_b = ar.alloc(1024, F32); B_bvb = Buf("bvb")
    lng_col = ar.alloc(8, F32); lnb_col = ar.alloc(8, F32); B_lncol = Buf("lncol")
    bs_b = ar.alloc(1024, F32); B_bsb = Buf("bsb")
    BT = ar.alloc(1024, F32); B_BT = Buf("BT")
    wsT = ar.alloc(1024, BF16); B_wsT = Buf("wsT")
    p1_work = ar.top
    wsf = ar.alloc(1024, F32); B_wsf = Buf("wsf")
    posi = ar.alloc(S_TOK, I32); B_posi = Buf("posi")
    ang = ar.alloc(S_TOK, F32); B_ang = Buf("ang")
    ang2 = ar.alloc(S_TOK, F32); B_ang2 = Buf("ang2")
    ang3 = ar.alloc(S_TOK, F32); B_ang3 = Buf("ang3")
    import os
    PARTS = os.environ.get("KPARTS", "ABCD")
    if "A" in PARTS:
        k.dma("sync", posi[64:96, :], pos.partition_broadcast(32), writes=[B_posi])
        R = slice(64, 96)
        angi = posi
        C1 = 6.28125
        C2 = TWO_PI - C1
        k.cp(ang[R, :], posi[R, :], [B_posi], [B_ang])
        k.ts(ang[R, :], ang[R, :], rc[R, 0:1], None, ALU.mult, None, [B_ang, B_cst], [B_ang])
        for which in range(2):
            if which == 0:
                k.ts(ang2[R, :], ang[R, :], PI / 2, None, ALU.add, None, [B_ang], [B_ang2])
                src = ang2
                Bsrc = B_ang2
            else:
                src = ang
                Bsrc = B_ang
            k.ts(ang3[R, :], src[R, :], 1.0 / TWO_PI, None, ALU.mult, None, [Bsrc], [B_ang3])
            k.cp(angi[R, :], ang3[R, :], [B_ang3], [B_posi])
            k.cp(ang3[R, :], angi[R, :], [B_posi], [B_ang3])
            k.stt(src[R, :], ang3[R, :], -C1, src[R, :], ALU.mult, ALU.add, [B_ang3, Bsrc], [Bsrc])
            k.stt(src[R, :], ang3[R, :], -C2, src[R, :], ALU.mult, ALU.add, [B_ang3, Bsrc], [Bsrc])
            if which == 0:
                k.act(cosT[R, :], src[R, :], AF.Sin, [Bsrc], [B_cos])
            else:
                k.act(sinT[R, :], src[R, :], AF.Sin, [Bsrc, B_cst], [B_sin], scale=rc[R, 1:2])
    with nc.allow_non_contiguous_dma(reason="one-time small parameter layout loads"):
        if "B" in PARTS:
            for c0 in range(0, NA, 680):
                k.dma("gpsimd", winA[:, :, c0:c0 + 680], w_in[:, c0:c0 + 680].rearrange("(c p) n -> p c n", p=128), writes=[B_winA])
            k.memset(wkr.rearrange("p c n -> p (c n)"), 0.0, [B_wkr])
            k.memset(wkrs.rearrange("p c n -> p (c n)"), 0.0, [B_wkrs])
            k.dma("gpsimd", wkr[:, :, 64:96], w_in[:, C_KR:C_KR + 32].rearrange("(c p) n -> p c n", p=128), writes=[B_wkr])
            k.dma("gpsimd", wkrs[:, :, 64:80], w_in[:, C_KR + 16:C_KR + 32].rearrange("(c p) n -> p c n", p=128), writes=[B_wkrs])
            k.dma("gpsimd", wkrs[:, :, 80:96], w_in[:, C_KR:C_KR + 16].rearrange("(c p) n -> p c n", p=128), writes=[B_wkrs])
            k.dma("gpsimd", wgo, w_gm_out.rearrange("(c p) n -> p c n", p=128), writes=[B_wgo])
        if "C" in PARTS:
            k.dma("sync", bcol, b_in[0, 0:2688].rearrange("(c p) -> p c", p=128), writes=[B_bcol], allow_slow_non_contiguous=True)
            k.dma("sync", bkr[64:96, 0:1], b_in[0, C_KR:C_KR + 32].rearrange("(p o) -> p o", o=1), writes=[B_bkr], allow_slow_non_contiguous=True)
            k.dma("sync", bkr[64:80, 1:2], b_in[0, C_KR + 16:C_KR + 32].rearrange("(p o) -> p o", o=1), writes=[B_bkr], allow_slow_non_contiguous=True)
            k.dma("sync", bkr[80:96, 1:2], b_in[0, C_KR:C_KR + 16].rearrange("(p o) -> p o", o=1), writes=[B_bkr], allow_slow_non_contiguous=True)
            k.dma("sync", bv_b, b_in[:, C_V:C_V + 1024].partition_broadcast(128), writes=[B_bvb])
            k.dma("sync", lng_col, gm_ln_g[0, :].rearrange("(c p) -> p c", p=128), writes=[B_lncol], allow_slow_non_contiguous=True)
            k.dma("sync", lnb_col, gm_ln_b[0, :].rearrange("(c p) -> p c", p=128), writes=[B_lncol], allow_slow_non_contiguous=True)
            k.dma("sync", bs_b, gm_b_s.partition_broadcast(128), writes=[B_bsb])
            k.dma("sync", wsf.rearrange("p (g s) -> p g s", g=8), gm_w_s.rearrange("(g t) s -> t g s", t=128), writes=[B_wsf])

    if "D" in PARTS:
        for g in range(8):
            bank = g // 4
            k.tr(ps[bank][:, (g % 4) * 128:(g % 4 + 1) * 128], wsf[:, g * 128:(g + 1) * 128], identf, [B_wsf, B_cst], [psB[bank]], signal=(g % 4 == 3))
        for bank in range(2):
            for j in range(4):
                g = bank * 4 + j
                k.tt(wsT[:, g * 128:(g + 1) * 128], ps[bank][:, j * 128:(j + 1) * 128], trif, ALU.mult, [psB[bank], B_cst], [B_wsT])
        for bank in range(2):
            k.mm(ps[2 + bank], onesb, wsT[:, bank * 512:(bank + 1) * 512], True, True, [B_onesb, B_wsT], [psB[2 + bank]], signal=True)
        for g in range(8):
            bank = 2 + g // 4
            k.stt(BT[:, g * 128:(g + 1) * 128], ps[bank][:, (g % 4) * 128:(g % 4 + 1) * 128], lnb_col[:, g:g + 1], bs_b[:, g * 128:(g + 1) * 128],
                  ALU.mult, ALU.add, [psB[bank], B_lncol, B_bsb], [B_BT])

    S.barrier()
    ar.top = p1_work
    TB = 512
    xbf = [ar.alloc(1024, BF16) for _ in range(4)]; B_xbf = [Buf() for _ in range(4)]
    xT = ar.alloc(8 * TB, BF16).rearrange("p (c n) -> p c n", c=8); B_xT = Buf("xT")
    uT = ar.alloc(8 * TB, BF16).rearrange("p (c n) -> p c n", c=8); B_uT = Buf("uT")
    gT = ar.alloc(8 * TB, BF16).rearrange("p (c n) -> p c n", c=8); B_gT = Buf("gT")
    ygT = ar.alloc(8 * TB, BF16).rearrange("p (c n) -> p c n", c=8); B_ygT = Buf("ygT")
    vf = [ar.alloc(1024, F32) for _ in range(2)]; B_vf = [Buf("vf0"), Buf("vf1")]
    vn = [ar.alloc(1024, BF16) for _ in range(2)]; B_vn = [Buf("vn0"), Buf("vn1")]
    stats = ar.alloc(2 * 6 + 8, F32); B_st = Buf("stats")
    latf = ar.alloc(3 * TB, F32).rearrange("p (c n) -> p c n", c=3); B_latf = Buf("latf")
    latsq = ar.alloc(3 * TB, BF16).rearrange("p (c n) -> p c n", c=3); B_latsq = Buf("latsq")
    rstd_b = ar.alloc(TB, F32); B_rstd = Buf("rstd")
    krt = ar.alloc(2 * TB, F32); B_krt = Buf("krt")
    gtmp = ar.alloc(1024, F32); B_gtmp = [Buf("gtmp0"), Buf("gtmp1")]
    cqs = ar.alloc(3 * TB, BF16).rearrange("p (c n) -> p c n", c=3); B_cqs = Buf("cqs")
    ckvs = ar.alloc(2 * TB, BF16).rearrange("p (c n) -> p c n", c=2); B_ckvs = Buf("ckvs")
    krs = ar.alloc(TB, BF16); B_krs = Buf("krs")
    dbgf = ar.alloc(8 * TB, F32).rearrange("p (c n) -> p c n", c=8) if debug else None; B_dbgf = Buf("dbgf")

    pr = [0]

    def nextps(lo, hi):
        i = lo + pr[0] % (hi - lo)
        pr[0] += 1
        return i

    nblk = S_TOK // TB if stage >= 1 else 0
    for tb in range(nblk):
        t0 = tb * TB
        if tb == 0:
            for j in range(4):
                k.dma("gpsimd", xbf[j], x[j * 128:(j + 1) * 128, :], writes=[B_xbf[j]])
        for j in range(4):
            xb = xbf[j]; Bx = B_xbf[j]
            bank = j % 2
            psb16 = ps[bank].bitcast(BF16)
            for c in range(8):
                k.tr(psb16[:, c * 128:(c + 1) * 128], xb[:, c * 128:(c + 1) * 128], identb, [Bx, B_identb], [psB[bank]], signal=(c == 7))
            k.cp(xT[:, :, j * 128:(j + 1) * 128], psb16.rearrange("p (c n) -> p c n", c=8), [psB[bank]], [B_xT],
                 eng=("vector" if j % 2 == 0 else "scalar"))
        if tb + 1 < nblk:
            for j in range(4):
                k.dma("gpsimd", xbf[j], x[t0 + TB + j * 128:t0 + TB + (j + 1) * 128, :], writes=[B_xbf[j]])
        for oc in range(8):
            bank = 2 + oc % 4
            for c in range(8):
                k.mm(ps[bank], winA[:, c, C_U + oc * 128:C_U + (oc + 1) * 128], xT[:, c, :], c == 0, c == 7,
                     [B_winA, B_xT], [psB[bank]], signal=(c == 7))
            k.act(uT[:, oc, :], ps[bank], AF.Gelu, [psB[bank], B_bcol], [B_uT], bias=bcol[:, oc:oc + 1])
        for (c_off, nch, dst, Bdst, eps_n, dst_d) in ((C_CQ, 3, cqs, B_cqs, 384, cqn_d), (C_CKV, 2, ckvs, B_ckvs, 256, ckvn_d)):
            for oc in range(nch):
                bank = 2 + oc % 4
                for c in range(8):
                    k.mm(ps[bank], winA[:, c, c_off + oc * 128:c_off + (oc + 1) * 128], xT[:, c, :], c == 0, c == 7,
                         [B_winA, B_xT], [psB[bank]], signal=(c == 7))
                k.act(latf[:, oc, :], ps[bank], AF.Identity, [psB[bank], B_bcol], [B_latf], bias=bcol[:, c_off // 128 + oc:c_off // 128 + oc + 1])
                k.act(latsq[:, oc, :], latf[:, oc, :], AF.Square, [B_latf], [B_latsq])
            bank = 6
            for oc in range(nch):
                k.mm(ps[bank], onesb, latsq[:, oc, :], oc == 0, oc == nch - 1, [B_onesb, B_latsq], [psB[bank]], signal=(oc == nch - 1))
            k.act(rstd_b, ps[bank], AF.Sqrt, [psB[bank], B_cst], [B_rstd], bias=(rc[:, 5:6] if eps_n == 384 else rc[:, 6:7]))
            S.op("vector", lambda e: e.reciprocal(rstd_b, rstd_b), [B_rstd], [B_rstd])
            for oc in range(nch):
                k.tt(dst[:, oc, :], latf[:, oc, :], rstd_b, ALU.mult, [B_latf, B_rstd], [Bdst])
            k.dma("sync", dst_d[:, t0:t0 + TB].rearrange("(c p) t -> p c t", p=128), dst, reads=[Bdst])
        for (wk, bank) in ((wkr, 6), (wkrs, 7)):
            for c in range(8):
                k.mm(ps[bank][0:96, :], wk[:, c, :], xT[:, c, :], c == 0, c == 7, [B_wkr, B_wkrs, B_xT], [psB[bank]], signal=(c == 7))
        k.stt(krt[R, 0:TB], ps[6][R, :], bkr[R, 0:1], cosT[R, t0:t0 + TB], ALU.add, ALU.mult, [psB[6], B_bkr, B_cos], [B_krt])
        k.stt(krt[R, TB:2 * TB], ps[7][R, :], bkr[R, 1:2], sinT[R, t0:t0 + TB], ALU.add, ALU.mult, [psB[7], B_bkr, B_sin], [B_krt])
        k.tt(krs[R, :], krt[R, 0:TB], krt[R, TB:2 * TB], ALU.add, [B_krt], [B_krs])
        k.dma("sync", kr_d[:, t0:t0 + TB], krs[R, :], reads=[B_krs])
        def v_mm(j):
            vt = vf[j % 2]; Bv = B_vf[j % 2]
            for half in range(2):
                bank = 2 + (2 * j + half) % 4
                for c in range(8):
                    k.mm(ps[bank], xT[:, c, j * 128:(j + 1) * 128], winA[:, c, C_V + half * 512:C_V + (half + 1) * 512], c == 0, c == 7,
                         [B_xT, B_winA], [psB[bank]], signal=(c == 7))
                k.tt(vt[:, half * 512:(half + 1) * 512], ps[bank], bv_b[:, half * 512:(half + 1) * 512], ALU.add, [psB[bank], B_bvb], [Bv])

        v_mm(0)
        for j in range(4):
            vt = vf[j % 2]; Bv = B_vf[j % 2]
            vb = vn[j % 2]; Bvn = B_vn[j % 2]
            k.act(vt, vt, AF.Gelu, [Bv], [Bv])
            for half in range(2):
                S.op("vector", lambda e, vt=vt, half=half: e.bn_stats(stats[:, half * 6:(half + 1) * 6], vt[:, half * 512:(half + 1) * 512]), [Bv], [B_st])
            S.op("vector", lambda e: e.bn_aggr(stats[:, 12:14], stats[:, 0:12]), [B_st], [B_st])
            k.act(stats[:, 14:15], stats[:, 13:14], AF.Sqrt, [B_st, B_cst], [B_st], bias=rc[:, 4:5])
            S.op("vector", lambda e: e.reciprocal(stats[:, 14:15], stats[:, 14:15]), [B_st], [B_st])
            k.ts(vb, vt, stats[:, 12:13], stats[:, 14:15], ALU.subtract, ALU.mult, [Bv, B_st], [Bvn])
            if j + 1 < 4:
                v_mm(j + 1)
            for gq in range(2):
                bank = 6 + gq
                for gg in range(4):
                    g = gq * 4 + gg
                    k.mm(ps[bank][:, gg * 128:(gg + 1) * 128], vb[:, g * 128:(g + 1) * 128], wsT[:, g * 128:(g + 1) * 128], True, True,
                         [Bvn, B_wsT], [psB[bank]], signal=(gg == 3))
                for gg in range(4):
                    g = gq * 4 + gg
                    k.stt(gtmp[:, gq * 512 + gg * 128:gq * 512 + (gg + 1) * 128], ps[bank][:, gg * 128:(gg + 1) * 128], lng_col[:, g:g + 1], BT[:, g * 128:(g + 1) * 128],
                          ALU.mult, ALU.add, [psB[bank], B_lncol, B_BT], [B_gtmp[gq]])
                k.tt(gT[:, gq * 4:(gq + 1) * 4, j * 128:(j + 1) * 128], gtmp[:, gq * 512:(gq + 1) * 512].rearrange("p (g t) -> p g t", g=4),
                     uT[:, gq * 4:(gq + 1) * 4, j * 128:(j + 1) * 128], ALU.mult, [B_gtmp[gq], B_uT], [B_gT], eng="gpsimd")
        for oc in range(8):
            bank = 2 + oc % 4
            for c in range(8):
                k.mm(ps[bank], wgo[:, c, oc * 128:(oc + 1) * 128], gT[:, c, :], c == 0, c == 7, [B_wgo, B_gT], [psB[bank]], signal=(c == 7))
            k.cp(ygT[:, oc, :], ps[bank], [psB[bank]], [B_ygT], eng=("vector" if oc % 2 == 0 else "scalar"))
        k.dma("sync", ygm_d[:, t0:t0 + TB].rearrange("(c p) t -> p c t", p=128), ygT, reads=[B_ygT])
        if debug:
            k.cp(dbgf.rearrange("p c n -> p (c n)"), gT.rearrange("p c n -> p (c n)"), [B_gT], [B_dbgf])
            k.dma("sync", dbg["d_gated"][:, t0:t0 + TB].rearrange("(c p) t -> p c t", p=128), dbgf, reads=[B_dbgf])


    def layer_norm_tile(h, Bh, outt, Bout, g_b, b_b, Bgb, st, Bst):
        for half in range(2):
            S.op("vector", lambda e, half=half: e.bn_stats(st[:, half * 6:(half + 1) * 6], h[:, half * 512:(half + 1) * 512]), [Bh], [Bst])
        S.op("vector", lambda e: e.bn_aggr(st[:, 12:14], st[:, 0:12]), [Bst], [Bst])
        k.act(st[:, 14:15], st[:, 13:14], AF.Sqrt, [Bst, B_cst], [Bst], bias=rc[:, 4:5])
        S.op("vector", lambda e: e.reciprocal(st[:, 14:15], st[:, 14:15]), [Bst], [Bst])
        k.ts(h, h, st[:, 12:13], st[:, 14:15], ALU.subtract, ALU.mult, [Bh, Bst], [Bh])
        k.tt(h, h, g_b, ALU.mult, [Bh, Bgb], [Bh])
        k.tt(outt, h, b_b, ALU.add, [Bh, Bgb], [Bout])

    _bc = {}

    def bcreg(e):
        if "r" not in _bc:
            _bc["r"] = e.to_reg(64 * 256 - 1)
        return _bc["r"]

    x1_d = nc.dram_tensor("x1_d", [S_TOK, D], F32, **dk).ap()
    x2_d = nc.dram_tensor("x2_d", [S_TOK, D], F32, **dk).ap()
    x2b_d = nc.dram_tensor("x2b_d", [S_TOK, D], BF16).ap()
    NSLOT = 64 * 256
    tokof_d = nc.dram_tensor("tokof_d", [NSLOT, 2], I32).ap()
    yd_d = nc.dram_tensor("yd_d", [NSLOT, D], BF16).ap()
    if stage >= 4:
        weg_b = nc.dram_tensor("weg_b", [64 * 128, 2048], BF16).ap()
        weu_b = nc.dram_tensor("weu_b", [64 * 128, 2048], BF16).ap()
        wed_b = nc.dram_tensor("wed_b", [64 * 128, 2048], BF16).ap()
        B_wcv = [[Buf(), Buf(), Buf()] for i in range(64)]

    def convert_expert(ex):
        k.dma("gpsimd", weg_b[ex * 128:(ex + 1) * 128, :].rearrange("p (c n) -> p c n", c=8),
              w_eg[ex * 1024:(ex + 1) * 1024, :].rearrange("(c p) n -> p c n", p=128), writes=[B_wcv[ex][0]])
        k.dma("gpsimd", weu_b[ex * 128:(ex + 1) * 128, :].rearrange("p (c n) -> p c n", c=8),
              w_eu[ex * 1024:(ex + 1) * 1024, :].rearrange("(c p) n -> p c n", p=128), writes=[B_wcv[ex][1]])
        k.dma("gpsimd", wed_b[ex * 128:(ex + 1) * 128, :].rearrange("p (c n) -> p c n", c=2),
              w_ed[ex * 256:(ex + 1) * 256, :].rearrange("(c p) n -> p c n", p=128), writes=[B_wcv[ex][2]])

    if stage >= 2:
        S.barrier()
        ar.top = persist_top
        wuq = ar.alloc(3 * 1536, BF16).rearrange("p (c n) -> p c n", c=3); B_wuq = Buf("wuq")
        wuqs = ar.alloc(3 * 1536, BF16).rearrange("p (c n) -> p c n", c=3); B_wuqs = Buf("wuqs")
        wuk = ar.alloc(2 * 1024, BF16).rearrange("p (c n) -> p c n", c=2); B_wuk = Buf("wuk")
        wuv = ar.alloc(2 * 1024, BF16).rearrange("p (c n) -> p c n", c=2); B_wuv = Buf("wuv")
        gcol = ar.alloc(8, F32); B_gcol = Buf("gcol")
        cqnT = ar.alloc(3 * S_TOK, BF16).rearrange("p (c n) -> p c n", c=3); B_cqn = Buf("cqn")
        ckvnT = ar.alloc(2 * S_TOK, BF16).rearrange("p (c n) -> p c n", c=2); B_ckvn = Buf("ckvn")
        KT = [ar.alloc(S_TOK, BF16) for _ in range(2)]; B_KT = [Buf("kt0"), Buf("kt1")]
        QT = [ar.alloc(S_TOK, BF16) for _ in range(2)]; B_QT = [Buf("qt0"), Buf("qt1")]
        VAf = [ar.alloc(NT * 65 + 64, BF16) for _ in range(2)]
        VA = [v_[:, 0:NT * 65].rearrange("p (t n) -> p t n", n=65) for v_ in VAf]; B_VA = [Buf("va0"), Buf("va1")]
        NP = 4
        pT = [ar.alloc(1024, BF16) for _ in range(NP)]; B_pT = [Buf(f"pT{i}") for i in range(NP)]
        rtmp = [ar.alloc(512, F32) for _ in range(2)]; B_rtmp = [Buf("rt0"), Buf("rt1")]
        rrec2 = [ar.alloc(512, F32) for _ in range(2)]; B_rrec2 = [Buf("rrec0"), Buf("rrec1")]
        bcs = ar.alloc(512, F32); B_bcs = Buf("bcs")
        onrm = [ar.alloc(512, BF16) for _ in range(2)]; B_onrm = [Buf("on0"), Buf("on1")]
        wst = ar.alloc(1536, F32); B_wst = Buf("wst")

        with nc.allow_non_contiguous_dma(reason="small parameter columns"):
            k.dma("sync", gcol[:, 0:3], q_norm_g[0, :].rearrange("(c p) -> p c", p=128), writes=[B_gcol], allow_slow_non_contiguous=True)
            k.dma("sync", gcol[:, 3:5], kv_norm_g[0, :].rearrange("(c p) -> p c", p=128), writes=[B_gcol], allow_slow_non_contiguous=True)
        k.ts(gcol[:, 0:3], gcol[:, 0:3], float(np.sqrt(384.0)), None, ALU.mult, None, [B_gcol], [B_gcol])
        k.ts(gcol[:, 3:5], gcol[:, 3:5], float(np.sqrt(256.0)), None, ALU.mult, None, [B_gcol], [B_gcol])
        for c in range(3):
            k.dma("sync", wst[:, 0:1536], w_uq[c * 128:(c + 1) * 128, :], writes=[B_wst])
            k.ts(wuq[:, c, :], wst[:, 0:1536], gcol[:, c:c + 1], None, ALU.mult, None, [B_wst, B_gcol], [B_wuq])
        for (wdst, Bw, wsrc) in ((wuk, B_wuk, w_uk), (wuv, B_wuv, w_uv)):
            for c in range(2):
                k.dma("sync", wst[:, 0:1024], wsrc[c * 128:(c + 1) * 128, :], writes=[B_wst])
                k.ts(wdst[:, c, :], wst[:, 0:1024], gcol[:, 3 + c:4 + c], None, ALU.mult, None, [B_wst, B_gcol], [Bw])
        k.memset(wuqs.rearrange("p c n -> p (c n)"), 0.0, [B_wuqs])
        for c in range(3):
            srcv = wuq[:, c, :].rearrange("p (h j) -> p h j", j=96)
            dstv = wuqs[:, c, :].rearrange("p (h j) -> p h j", j=96)
            k.cp(dstv[:, :, 64:80], srcv[:, :, 80:96], [B_wuq], [B_wuqs])
            k.cp(dstv[:, :, 80:96], srcv[:, :, 64:80], [B_wuq], [B_wuqs])
        k.dma("sync", cqnT, cqn_d.rearrange("(c p) t -> p c t", p=128), writes=[B_cqn])
        k.dma("sync", ckvnT, ckvn_d.rearrange("(c p) t -> p c t", p=128), writes=[B_ckvn])
        for b in range(2):
            k.memset(KT[b][64:128, :], 0.0, [B_KT[b]])
            k.memset(QT[b][64:128, :], 0.0, [B_QT[b]])
            k.dma("sync", KT[b][64:96, :], kr_d, writes=[B_KT[b]])
            k.memset(VAf[b][:, NT * 65:NT * 65 + 64], 0.0, [B_VA[b]])
            k.memset(VA[b][:, :, 64:65], 1.0, [B_VA[b]])
        maskD = ar.alloc(4 * 512, BF16); B_maskD = Buf("maskD")
        k.memset(maskD, 1.0, [B_maskD])
        for i_ in range(4):
            if i_ > 0:
                k.memset(maskD[:, i_ * 512:i_ * 512 + i_ * 128], 0.0, [B_maskD])
            k.cp(maskD[:, i_ * 512 + i_ * 128:i_ * 512 + (i_ + 1) * 128], trib, [B_trib, B_maskD], [B_maskD])
        SCALE = float(96.0 ** -0.5)
        NH = 16 if stage >= 2 else 0

        gcount = [0]
        SG = [k.psbig[1], k.psbig[2], k.psbig[3]]
        B_SG = [Buf("sg0"), Buf("sg1"), Buf("sg2")]

        busy = set()

        def next_group():
            for _ in range(3):
                g = gcount[0] % 3
                gcount[0] += 1
                if g not in busy:
                    return g
            raise AssertionError("no free PSUM group")

        def gen_chunks(h):
            b = h % 2
            chunks = []

            def q_chunk(qb):
                c0 = qb * 512
                g = next_group()
                G = SG[g]
                for c in range(3):
                    k.mm(G[0:96, 0:512], wuq[:, c, h * 96:(h + 1) * 96], cqnT[:, c, c0:c0 + 512], c == 0, c == 2, [B_wuq, B_cqn], [B_SG[g]])
                for c in range(3):
                    k.mm(G[0:96, 512:1024], wuqs[:, c, h * 96:(h + 1) * 96], cqnT[:, c, c0:c0 + 512], c == 0, c == 2, [B_wuqs, B_cqn], [B_SG[g]], signal=(c == 2))
                k.cp(QT[b][0:64, c0:c0 + 512], G[0:64, 0:512], [B_SG[g]], [B_QT[b]])
                k.tt(rtmp[0][R, :], G[R, 0:512], cosT[R, c0:c0 + 512], ALU.mult, [B_SG[g], B_cos], [B_rtmp[0]])
                k.tt(rtmp[1][R, :], G[R, 512:1024], sinT[R, c0:c0 + 512], ALU.mult, [B_SG[g], B_sin], [B_rtmp[1]])
                k.tt(QT[b][R, c0:c0 + 512], rtmp[0][R, :], rtmp[1][R, :], ALU.add, [B_rtmp[0], B_rtmp[1]], [B_QT[b]], eng="gpsimd")

            def k_chunk(qq):
                g = next_group()
                G = SG[g]
                for hf in range(2):
                    c0 = (qq * 2 + hf) * 512
                    for c in range(2):
                        k.mm(G[0:64, hf * 512:(hf + 1) * 512], wuk[:, c, h * 64:(h + 1) * 64], ckvnT[:, c, c0:c0 + 512], c == 0, c == 1, [B_wuk, B_ckvn], [B_SG[g]],
                             signal=(c == 1 and hf == 1))
                k.cp(KT[b][0:64, qq * 1024:(qq + 1) * 1024], G[0:64, :], [B_SG[g]], [B_KT[b]], eng="scalar")

            def v_chunk(tg):
                g = next_group()
                G = SG[g]
                for i in range(16):
                    t = tg * 16 + i
                    for c in range(2):
                        k.mm(G[:, i * 64:(i + 1) * 64], ckvnT[:, c, t * 128:(t + 1) * 128], wuv[:, c, h * 64:(h + 1) * 64], c == 0, c == 1,
                             [B_ckvn, B_wuv], [B_SG[g]], signal=(c == 1 and i == 15))
                k.cp(VA[b][:, tg * 16:(tg + 1) * 16, 0:64], G.rearrange("p (t n) -> p t n", n=64), [B_SG[g]], [B_VA[b]])

            for qb in range(8):
                chunks.append(lambda qb=qb: q_chunk(qb))
            for qq in range(4):
                chunks.append(lambda qq=qq: k_chunk(qq))
            for tg in range(2):
                chunks.append(lambda tg=tg: v_chunk(tg))
            return chunks

        pcount = [0]
        ocount = [0]

        convq = []

        def attn_head(h, pending):
            b = h % 2
            pairs = [(qb, p) for qb in range(8) for p in range(2 * qb + 2)]
            ob_of = {}
            for qb in range(8):
                ob_of[qb] = ocount[0] % 2
                ocount[0] += 1
            grp = {}

            def offs(qb, kt):
                return max(0, kt - 4 * qb) * 128

            def qk(i):
                qb, p = pairs[i]
                q0 = qb * 512
                g = next_group()
                busy.add(g)
                grp[i] = g
                for hf in range(2):
                    kt = 2 * p + hf
                    off = offs(qb, kt)
                    k.mm(SG[g][:, hf * 512 + off:hf * 512 + 512], KT[b][0:128, kt * 128:(kt + 1) * 128], QT[b][0:128, q0 + off:q0 + 512], True, True,
                         [B_KT[b], B_QT[b]], [B_SG[g]], signal=(hf == 1))

            def epi_a(qb):
                ob = ob_of[qb]
                rr = rrec2[qb % 2]
                S.op("vector", lambda e, ob=ob, rr=rr: e.reciprocal(rr[64:65, :], ps[ob][64:65, :]), [psB[ob]], [B_rrec2[qb % 2]])

            def epilogue(qb):
                ob = ob_of[qb]
                q0 = qb * 512
                rrec = rrec2[qb % 2]
                B_rrec = B_rrec2[qb % 2]
                g = next_group()
                k.mm(SG[g][0:64, 0:512], trif[64:65, 64:128], rrec[64:65, :], True, True, [B_cst, B_rrec], [B_SG[g]], signal=True)
                k.cp(bcs[0:64, :], SG[g][0:64, 0:512], [B_SG[g]], [B_bcs], eng="scalar")
                oj = ocount[0] % 2
                ocount[0] += 1
                k.tt(onrm[oj][0:64, :], ps[ob][0:64, :], bcs[0:64, :], ALU.mult, [psB[ob], B_bcs], [B_onrm[oj]])
                k.dma("sync", oT_d[h * 64:(h + 1) * 64, q0:q0 + 512], onrm[oj][0:64, :], reads=[B_onrm[oj]])

            due = []
            since = 0
            qk(0)
            qk(1)
            for i, (qb, p) in enumerate(pairs):
                if i + 2 < len(pairs):
                    qk(i + 2)
                nkt = 4 * qb + 4
                g = grp[i]
                pj = pcount[0] % NP
                pcount[0] += 1
                diag = (2 * p >= 4 * qb)
                if not diag:
                    k.act(pT[pj], SG[g], AF.Exp, [B_SG[g]], [B_pT[pj]], scale=SCALE)
                else:
                    for hf in range(2):
                        off = offs(qb, 2 * p + hf)
                        k.act(pT[pj][:, hf * 512 + off:hf * 512 + 512], SG[g][:, hf * 512 + off:hf * 512 + 512], AF.Exp, [B_SG[g]], [B_pT[pj]], scale=SCALE)
                        k.tt(pT[pj][:, hf * 512 + off:hf * 512 + off + 128], pT[pj][:, hf * 512 + off:hf * 512 + off + 128], trib, ALU.mult,
                             [B_pT[pj], B_trib], [B_pT[pj]], eng="gpsimd")
                ob = ob_of[qb]
                for hf in range(2):
                    kt = 2 * p + hf
                    off = offs(qb, kt)
                    k.mm(ps[ob][0:128, off:512], VAf[b][:, kt * 65:kt * 65 + 128], pT[pj][:, hf * 512 + off:hf * 512 + 512], kt == 0, kt == nkt - 1, [B_VA[b], B_pT[pj]], [psB[ob]],
                         signal=(kt == nkt - 1))
                busy.discard(g)
                if p == 2 * qb + 1:
                    epi_a(qb)
                    due.append((i + 4, qb))
                while due and due[0][0] <= i:
                    epilogue(due.pop(0)[1])
                if convq and i % 16 == 8:
                    convert_expert(convq.pop(0))
                since += 1
                if pending and since >= 5:
                    since = 0
                    pending.pop(0)()
            while due:
                epilogue(due.pop(0)[1])

        if NH:
            for ch in gen_chunks(0):
                ch()
        for h in range(NH):
            pending = gen_chunks(h + 1) if h + 1 < NH else []
            convq[:] = list(range(4 * h, 4 * h + 4)) if stage >= 4 else []
            attn_head(h, pending)
            while pending:
                pending.pop(0)()
            while convq:
                convert_expert(convq.pop(0))

    if stage >= 3:
        S.barrier()
        ar.top = persist_top
        wing = ar.alloc(8 * 2048, BF16).rearrange("p (c n) -> p c n", c=8); B_wing = Buf("wing")
        wml = ar.alloc(8 * 1024, BF16).rearrange("p (c n) -> p c n", c=8); B_wml = Buf("wml")
        wo_sb = ar.alloc(8 * 1024, BF16).rearrange("p (c n) -> p c n", c=8); B_wo = Buf("wo")
        bgcol = ar.alloc(16, F32); B_bgcol = Buf("bgcol")
        l1g = ar.alloc(1024, F32); l1b = ar.alloc(1024, F32); B_l1 = Buf("l1")
        xbf = [ar.alloc(1024, BF16) for _ in range(4)]; B_xbf = [Buf() for _ in range(4)]
        xT = ar.alloc(8 * 512, BF16).rearrange("p (c n) -> p c n", c=8); B_xT = Buf("xT")
        xf = [ar.alloc(1024, F32) for _ in range(4)]; B_xf = [Buf() for _ in range(4)]
        sgT = ar.alloc(8 * 512, BF16).rearrange("p (c n) -> p c n", c=8); B_sgT = Buf("sgT")
        smT = ar.alloc(8 * 512, BF16).rearrange("p (c n) -> p c n", c=8); B_smT = Buf("smT")
        ygb2 = [ar.alloc(8 * 512, BF16).rearrange("p (c n) -> p c n", c=8) for _ in range(2)]; B_ygb2 = [Buf(), Buf()]
        oTb2 = [ar.alloc(8 * 512, BF16).rearrange("p (c n) -> p c n", c=8) for _ in range(2)]; B_oTb2 = [Buf(), Buf()]
        mT = ar.alloc(8 * 512, BF16).rearrange("p (c n) -> p c n", c=8); B_mT = Buf("mT")
        t1 = [ar.alloc(512, F32) for _ in range(2)]; B_t1 = [Buf("t1a"), Buf("t1b")]
        t2 = [ar.alloc(512, F32) for _ in range(2)]; B_t2 = [Buf("t2a"), Buf("t2b")]
        h1 = [ar.alloc(1024, F32) for _ in range(2)]; B_h1 = [Buf("h1a"), Buf("h1b")]
        st3 = ar.alloc(16, F32); B_st3 = Buf("st3")
        with nc.allow_non_contiguous_dma(reason="small parameter columns"):
            for c0 in range(0, 2048, 512):
                k.dma("gpsimd", wing[:, :, c0:c0 + 512], w_in[:, C_GG + c0:C_GG + c0 + 512].rearrange("(c p) n -> p c n", p=128), writes=[B_wing])
            k.dma("gpsimd", wml, w_mla_out.rearrange("(c p) n -> p c n", p=128), writes=[B_wml])
            k.dma("gpsimd", wo_sb, w_o.rearrange("(c p) n -> p c n", p=128), writes=[B_wo])
            k.dma("sync", bgcol, b_in[0, C_GG:C_GG + 2048].rearrange("(c p) -> p c", p=128), writes=[B_bgcol], allow_slow_non_contiguous=True)
        k.dma("sync", l1g, ln1_g.partition_broadcast(128), writes=[B_l1])
        k.dma("sync", l1b, ln1_b.partition_broadcast(128), writes=[B_l1])
        def p3a_big_loads(tb):
            t0_ = tb * 512
            k.dma("sync", ygb2[tb % 2], ygm_d[:, t0_:t0_ + 512].rearrange("(c p) t -> p c t", p=128), writes=[B_ygb2[tb % 2]])
            k.dma("sync", oTb2[tb % 2], oT_d[:, t0_:t0_ + 512].rearrange("(c p) t -> p c t", p=128), writes=[B_oTb2[tb % 2]])

        p3a_big_loads(0)
        for j in range(4):
            k.dma("gpsimd", xbf[j], x[j * 128:(j + 1) * 128, :], writes=[B_xbf[j]])
        for tb in range(8):
            t0 = tb * 512
            ygb = ygb2[tb % 2]; B_ygb = B_ygb2[tb % 2]
            oTb = oTb2[tb % 2]; B_oTb = B_oTb2[tb % 2]
            if tb + 1 < 8:
                p3a_big_loads(tb + 1)
            for j in range(4):
                k.dma("sync", xf[j], x[t0 + j * 128:t0 + (j + 1) * 128, :], writes=[B_xf[j]])
            for j in range(4):
                xb = xbf[j]; Bx = B_xbf[j]
                bank = j % 2
                psb16 = ps[bank].bitcast(BF16)
                for c in range(8):
                    k.tr(psb16[:, c * 128:(c + 1) * 128], xb[:, c * 128:(c + 1) * 128], identb, [Bx, B_identb], [psB[bank]], signal=(c == 7))
                k.cp(xT[:, :, j * 128:(j + 1) * 128], psb16.rearrange("p (c n) -> p c n", c=8), [psB[bank]], [B_xT])
            if tb + 1 < 8:
                for j in range(4):
                    k.dma("gpsimd", xbf[j], x[t0 + 512 + j * 128:t0 + 512 + (j + 1) * 128, :], writes=[B_xbf[j]])
            for (dstT, Bd, coff) in ((sgT, B_sgT, 0), (smT, B_smT, 1024)):
                for oc in range(8):
                    bank = 2 + oc % 2
                    for c in range(8):
                        k.mm(ps[bank], wing[:, c, coff + oc * 128:coff + (oc + 1) * 128], xT[:, c, :], c == 0, c == 7, [B_wing, B_xT], [psB[bank]], signal=(c == 7))
                    k.act(dstT[:, oc, :], ps[bank], AF.Sigmoid, [psB[bank], B_bgcol], [Bd], bias=bgcol[:, coff // 128 + oc:coff // 128 + oc + 1])
            for oc in range(8):
                bank = 4 + oc % 2
                for c in range(8):
                    k.mm(ps[bank], wml[:, c, oc * 128:(oc + 1) * 128], oTb[:, c, :], c == 0, c == 7, [B_wml, B_oTb], [psB[bank]], signal=(c == 7))
                k.tt(t1[oc % 2], ps[bank], smT[:, oc, :], ALU.mult, [psB[bank], B_smT], [B_t1[oc % 2]])
                k.tt(t2[oc % 2], sgT[:, oc, :], ygb[:, oc, :], ALU.mult, [B_sgT, B_ygb], [B_t2[oc % 2]], eng="gpsimd")
                k.tt(mT[:, oc, :], t1[oc % 2], t2[oc % 2], ALU.add, [B_t1[oc % 2], B_t2[oc % 2]], [B_mT])
            for j in range(4):
                xt = xf[j]; Bxf = B_xf[j]
                hh = h1[j % 2]; Bh = B_h1[j % 2]
                for half in range(2):
                    bank = 6 + half
                    for c in range(8):
                        k.mm(ps[bank], mT[:, c, j * 128:(j + 1) * 128], wo_sb[:, c, half * 512:(half + 1) * 512], c == 0, c == 7, [B_mT, B_wo], [psB[bank]], signal=(c == 7))
                    k.stt(hh[:, half * 512:(half + 1) * 512], xt[:, half * 512:(half + 1) * 512], ALPHA, ps[bank], ALU.mult, ALU.add, [Bxf, psB[bank]], [Bh])
                layer_norm_tile(hh, Bh, hh, Bh, l1g, l1b, B_l1, st3, B_st3)
                k.dma("sync", x1_d[t0 + j * 128:t0 + (j + 1) * 128, :], hh, reads=[Bh])

    if stage >= 4:
        S.barrier()
        ar.top = persist_top
        wmq_sb = ar.alloc(8 * 1024, BF16).rearrange("p (c n) -> p c n", c=8); B_wmq = Buf("wmq")
        wmo_sb = ar.alloc(8 * 1024, BF16).rearrange("p (c n) -> p c n", c=8); B_wmo = Buf("wmo")
        wkv = ar.alloc(8 * 1024, BF16).rearrange("p (c n) -> p c n", c=8); B_wkv = Buf("wkv")
        memb = ar.alloc(2 * 1024, BF16).rearrange("p (c n) -> p c n", c=2); B_memb = Buf("memb")
        memT = ar.alloc(8 * 256, BF16).rearrange("p (c n) -> p c n", c=8); B_memT = Buf("memT")
        mKT = ar.alloc(8 * 256, BF16).rearrange("p (c n) -> p c n", c=8); B_mKT = Buf("mKT")
        mV = ar.alloc(2 * 1024, BF16).rearrange("p (c n) -> p c n", c=2); B_mV = Buf("mV")
        wr = ar.alloc(8 * 72, F32).rearrange("p (c n) -> p c n", c=8); B_wr = Buf("wr")
        br_b = ar.alloc(72, F32); B_brb = Buf("brb")
        l2g = ar.alloc(1024, F32); l2b = ar.alloc(1024, F32); B_l2 = Buf("l2")
        eidx = ar.alloc(64, F32); B_eidx = Buf("eidx")
        tokid = ar.alloc(NT * 2, I32); B_tokid = Buf("tokid")
        dest_all = ar.alloc(NT * 2, I32); B_dest = Buf("dest")
        w_all = ar.alloc(NT * 2, F32); B_wall = Buf("wall")
        moe_persist = ar.top
        xbf = [ar.alloc(1024, BF16) for _ in range(4)]; B_xbf = [Buf() for _ in range(4)]
        xT = ar.alloc(8 * 512, BF16).rearrange("p (c n) -> p c n", c=8); B_xT = Buf("xT")
        xf = [ar.alloc(1024, F32) for _ in range(4)]; B_xf = [Buf() for _ in range(4)]
        q2T = ar.alloc(8 * 512, BF16).rearrange("p (c n) -> p c n", c=8); B_q2T = Buf("q2T")
        p2T = [ar.alloc(512, BF16) for _ in range(2)]; B_p2T = [Buf("p2a"), Buf("p2b")]
        rb = ar.alloc(512, F32); B_rb = Buf("rb")
        o2T = ar.alloc(8 * 512, BF16).rearrange("p (c n) -> p c n", c=8); B_o2T = Buf("o2T")
        h2 = [ar.alloc(1024, F32) for _ in range(2)]; B_h2 = [Buf("h2a"), Buf("h2b")]
        x2b = [ar.alloc(1024, BF16) for _ in range(2)]; B_x2b = [Buf("x2ba"), Buf("x2bb")]
        x2T = ar.alloc(8 * 128, F32).rearrange("p (c n) -> p c n", c=8); B_x2T = Buf("x2T")
        st3 = ar.alloc(16, F32); B_st3 = Buf("st3")
        lg = ar.alloc(72, F32); B_lg = Buf("lg")
        rt = ar.alloc(256, F32); B_rt = Buf("rt")
        Mt = ar.alloc(64, BF16); B_Mt = Buf("Mt")
        M12 = ar.alloc(128, F32); B_M12 = Buf("M12")
        base = ar.alloc(64, F32); B_base = Buf("base")
        rank = ar.alloc(64, F32); B_rank = Buf("rank")
        tris = ar.alloc(128, BF16); B_tris = Buf("tris")
        zt = ar.alloc(256, I32); B_zt = Buf("zt")

        with nc.allow_non_contiguous_dma(reason="small parameter columns"):
            k.dma("gpsimd", wmq_sb, w_mq.rearrange("(c p) n -> p c n", p=128), writes=[B_wmq])
            k.dma("gpsimd", wmo_sb, w_mo.rearrange("(c p) n -> p c n", p=128), writes=[B_wmo])
            k.dma("gpsimd", wkv, w_mk.rearrange("(c p) n -> p c n", p=128), writes=[B_wkv])
            k.dma("gpsimd", memb, mem.rearrange("(c p) n -> p c n", p=128), writes=[B_memb])
            k.dma("sync", wr[:, :, 0:8], w_gr.rearrange("(c p) n -> p c n", p=128), writes=[B_wr], allow_slow_non_contiguous=True)
            k.dma("sync", wr[:, :, 8:72], w_er.rearrange("(c p) n -> p c n", p=128), writes=[B_wr], allow_slow_non_contiguous=True)
            k.dma("sync", br_b[:, 0:8], b_gr.partition_broadcast(128), writes=[B_brb])
            k.dma("sync", br_b[:, 8:72], b_er.partition_broadcast(128), writes=[B_brb])
        k.dma("sync", l2g, ln2_g.partition_broadcast(128), writes=[B_l2])
        k.dma("sync", l2b, ln2_b.partition_broadcast(128), writes=[B_l2])
        k.cp(eidx, cst_sb[:, 264:328], [B_cst], [B_eidx])
        k.cp(tris, cst_sb[:, 328:456], [B_cst], [B_tris])
        k.cp(tokid.rearrange("p (t o) -> p t o", o=2), cst_sb[:, 456:488].unsqueeze(2).to_broadcast([128, NT, 2]), [B_cst], [B_tokid])
        k.memset(base, 0.0, [B_base])
        k.memset(zt, 0, [B_zt])
        k.dma("sync", tokof_d.rearrange("(p n) o -> p (n o)", p=128), zt, reads=[B_zt])
        B_tokof = Buf("tokof")
        for mt in range(2):
            psb16 = ps[mt].bitcast(BF16)
            for c in range(8):
                k.tr(psb16[:, c * 128:(c + 1) * 128], memb[:, mt, c * 128:(c + 1) * 128], identb, [B_memb, B_identb], [psB[mt]], signal=(c == 7))
            k.cp(memT[:, :, mt * 128:(mt + 1) * 128], psb16.rearrange("p (c n) -> p c n", c=8), [psB[mt]], [B_memT])
        for oc in range(8):
            bank = 2 + oc % 2
            for c in range(8):
                k.mm(ps[bank][:, 0:256], wkv[:, c, oc * 128:(oc + 1) * 128], memT[:, c, :], c == 0, c == 7, [B_wkv, B_memT], [psB[bank]], signal=(c == 7))
            k.cp(mKT[:, oc, :], ps[bank][:, 0:256], [psB[bank]], [B_mKT])
        k.dma("gpsimd", wkv, w_mv.rearrange("(c p) n -> p c n", p=128), reads=[], writes=[B_wkv])
        for mt in range(2):
            for half in range(2):
                bank = 4 + half
                for c in range(8):
                    k.mm(ps[bank], memT[:, c, mt * 128:(mt + 1) * 128], wkv[:, c, half * 512:(half + 1) * 512], c == 0, c == 7, [B_memT, B_wkv], [psB[bank]], signal=(c == 7))
                k.cp(mV[:, mt, half * 512:(half + 1) * 512], ps[bank], [psB[bank]], [B_mV])

        for j in range(4):
            k.dma("gpsimd", xbf[j], x1_d[j * 128:(j + 1) * 128, :], writes=[B_xbf[j]])
        for tb in range(8):
            t0 = tb * 512
            for j in range(4):
                k.dma("sync", xf[j], x1_d[t0 + j * 128:t0 + (j + 1) * 128, :], writes=[B_xf[j]])
            for j in range(4):
                xb = xbf[j]; Bx = B_xbf[j]
                bank = j % 2
                psb16 = ps[bank].bitcast(BF16)
                for c in range(8):
                    k.tr(psb16[:, c * 128:(c + 1) * 128], xb[:, c * 128:(c + 1) * 128], identb, [Bx, B_identb], [psB[bank]], signal=(c == 7))
                k.cp(xT[:, :, j * 128:(j + 1) * 128], psb16.rearrange("p (c n) -> p c n", c=8), [psB[bank]], [B_xT])
            if tb + 1 < 8:
                for j in range(4):
                    k.dma("gpsimd", xbf[j], x1_d[t0 + 512 + j * 128:t0 + 512 + (j + 1) * 128, :], writes=[B_xbf[j]])
            for oc in range(8):
                bank = 2 + oc % 2
                for c in range(8):
                    k.mm(ps[bank], wmq_sb[:, c, oc * 128:(oc + 1) * 128], xT[:, c, :], c == 0, c == 7, [B_wmq, B_xT], [psB[bank]], signal=(c == 7))
                k.cp(q2T[:, oc, :], ps[bank], [psB[bank]], [B_q2T], eng="scalar")
            for hm in range(4):
                for mt in range(2):
                    for dc in range(2):
                        k.mm(ps[4 + mt], mKT[:, hm * 2 + dc, mt * 128:(mt + 1) * 128], q2T[:, hm * 2 + dc, :], dc == 0, dc == 1, [B_mKT, B_q2T], [psB[4 + mt]], signal=(dc == 1))
                    k.act(p2T[mt], ps[4 + mt], AF.Exp, [psB[4 + mt]], [B_p2T[mt]], scale=1.0 / 16.0)
                for mt in range(2):
                    k.mm(ps[6], onesb, p2T[mt], mt == 0, mt == 1, [B_onesb, B_p2T[mt]], [psB[6]], signal=(mt == 1))
                S.op("vector", lambda e: e.reciprocal(rb, ps[6]), [psB[6]], [B_rb])
                for dvc in range(2):
                    bank = 2 + dvc
                    for mt in range(2):
                        k.mm(ps[bank], mV[:, mt, hm * 256 + dvc * 128:hm * 256 + (dvc + 1) * 128], p2T[mt], mt == 0, mt == 1, [B_mV, B_p2T[mt]], [psB[bank]], signal=(mt == 1))
                    k.tt(o2T[:, hm * 2 + dvc, :], ps[bank], rb, ALU.mult, [psB[bank], B_rb], [B_o2T])
            for j in range(4):
                ti = tb * 4 + j
                xt = xf[j]; Bxf = B_xf[j]
                hh = h2[j % 2]; Bh = B_h2[j % 2]
                xb2 = x2b[j % 2]; Bxb2 = B_x2b[j % 2]
                for half in range(2):
                    bank = 0 + half
                    for c in range(8):
                        k.mm(ps[bank], o2T[:, c, j * 128:(j + 1) * 128], wmo_sb[:, c, half * 512:(half + 1) * 512], c == 0, c == 7, [B_o2T, B_wmo], [psB[bank]], signal=(c == 7))
                    k.stt(hh[:, half * 512:(half + 1) * 512], xt[:, half * 512:(half + 1) * 512], ALPHA, ps[bank], ALU.mult, ALU.add, [Bxf, psB[bank]], [Bh])
                layer_norm_tile(hh, Bh, hh, Bh, l2g, l2b, B_l2, st3, B_st3)
                k.dma("sync", x2_d[t0 + j * 128:t0 + (j + 1) * 128, :], hh, reads=[Bh])
                k.cp(xb2, hh, [Bh], [Bxb2])
                k.dma("sync", x2b_d[t0 + j * 128:t0 + (j + 1) * 128, :], xb2, reads=[Bxb2])
                for c in range(8):
                    k.tr(ps[7][:, (c % 4) * 128:(c % 4 + 1) * 128], hh[:, c * 128:(c + 1) * 128], identf, [Bh, B_cst], [psB[7]], signal=(c % 4 == 3))
                    if c % 4 == 3:
                        k.cp(x2T[:, c - 3:c + 1, :], ps[7].rearrange("p (c n) -> p c n", c=4), [psB[7]], [B_x2T])
                for c in range(8):
                    k.mm(ps[6][:, 0:72], x2T[:, c, :], wr[:, c, :], c == 0, c == 7, [B_x2T, B_wr], [psB[6]], signal=(c == 7))
                k.tt(lg, ps[6][:, 0:72], br_b, ALU.add, [psB[6], B_brb], [B_lg])
                V = lambda a, bb: rt[:, a:bb]
                RW = ([B_lg, B_rt, B_eidx], [B_rt])
                S.op("vector", lambda e: e.reduce_max(V(0, 1), lg[:, 0:8], AX.X), [B_lg], [B_rt])
                k.ts(V(8, 16), lg[:, 0:8], V(0, 1), None, ALU.subtract, None, *RW)
                k.act(V(16, 24), V(8, 16), AF.Exp, [B_rt], [B_rt], accum_out=V(1, 2))
                S.op("vector", lambda e: e.reciprocal(V(2, 3), V(1, 2)), [B_rt], [B_rt])
                k.ts(V(24, 32), V(8, 16), 0.0, None, ALU.is_equal, None, *RW)
                k.tt(V(64, 128).rearrange("p (g e) -> p g e", g=8), lg[:, 8:72].rearrange("p (g e) -> p g e", g=8),
                     V(24, 32).unsqueeze(2).to_broadcast([128, 8, 8]), ALU.mult, *RW)
                S.op("vector", lambda e: e.tensor_reduce(V(32, 40), V(64, 128).rearrange("p (g e) -> p e g", g=8), AX.X, ALU.add), [B_rt], [B_rt])
                S.op("vector", lambda e: e.reduce_max(V(3, 4), V(32, 40), AX.X), [B_rt], [B_rt])
                k.ts(V(40, 48), V(32, 40), V(3, 4), None, ALU.is_equal, None, *RW)
                k.stt(V(48, 56), V(40, 48), -1e30, V(32, 40), ALU.mult, ALU.add, *RW)
                S.op("vector", lambda e: e.reduce_max(V(4, 5), V(48, 56), AX.X), [B_rt], [B_rt])
                k.ts(V(56, 64), V(48, 56), V(4, 5), None, ALU.is_equal, None, *RW)
                k.tt(V(5, 6), V(4, 5), V(3, 4), ALU.subtract, *RW)
                k.act(V(6, 7), V(5, 6), AF.Exp, [B_rt], [B_rt])
                k.ts(V(7, 8), V(6, 7), 1.0, None, ALU.add, None, *RW)
                S.op("vector", lambda e: e.reciprocal(V(7, 8), V(7, 8)), [B_rt], [B_rt])
                k.tt(V(128, 129), V(7, 8), V(2, 3), ALU.mult, *RW)
                k.tt(V(129, 130), V(128, 129), V(6, 7), ALU.mult, *RW)
                for kk in range(2):
                    k.tt(M12[:, kk * 64:(kk + 1) * 64].rearrange("p (g e) -> p g e", g=8), V(24, 32).unsqueeze(2).to_broadcast([128, 8, 8]),
                         V(40 + 16 * kk, 48 + 16 * kk).unsqueeze(1).to_broadcast([128, 8, 8]), ALU.mult, [B_rt], [B_M12])
                k.tt(Mt, M12[:, 0:64], M12[:, 64:128], ALU.add, [B_M12], [B_Mt])
                k.mm(ps[6][:, 128:192], tris, Mt, True, True, [B_tris, B_Mt], [psB[6]], signal=True)
                k.tt(rank, ps[6][:, 128:192], base, ALU.add, [psB[6], B_base], [B_rank])
                k.mm(ps[6][:, 256:320], onesb, Mt, True, True, [B_onesb, B_Mt], [psB[6]], signal=True)
                k.tt(base, ps[6][:, 256:320], base, ALU.add, [psB[6], B_base], [B_base])
                for kk in range(2):
                    k.tt(V(130, 194), M12[:, kk * 64:(kk + 1) * 64], rank, ALU.mult, [B_M12, B_rank, B_rt], [B_rt])
                    S.op("vector", lambda e, kk=kk: e.reduce_sum(V(200 + kk, 201 + kk), V(130, 194), AX.X), [B_rt], [B_rt])
                    k.tt(V(130, 194), M12[:, kk * 64:(kk + 1) * 64], eidx, ALU.mult, [B_M12, B_eidx, B_rt], [B_rt])
                    S.op("vector", lambda e, kk=kk: e.reduce_sum(V(202 + kk, 203 + kk), V(130, 194), AX.X), [B_rt], [B_rt])
                    k.ts(V(204 + kk, 205 + kk), V(200 + kk, 201 + kk), 256.0, None, ALU.is_lt, None, *RW)
                    k.stt(V(206 + kk, 207 + kk), V(202 + kk, 203 + kk), 256.0, V(200 + kk, 201 + kk), ALU.mult, ALU.add, *RW)
                    k.ts(V(208 + kk, 209 + kk), V(204 + kk, 205 + kk), -1.0e6, 1.0e6, ALU.mult, ALU.add, *RW)
                    k.tt(V(206 + kk, 207 + kk), V(206 + kk, 207 + kk), V(208 + kk, 209 + kk), ALU.add, *RW)
                    k.cp(dest_all[:, ti * 2 + kk:ti * 2 + kk + 1], V(206 + kk, 207 + kk), [B_rt], [B_dest])
                    k.tt(w_all[:, ti * 2 + kk:ti * 2 + kk + 1], V(128 + kk, 129 + kk), V(204 + kk, 205 + kk), ALU.mult, [B_rt], [B_wall])
                    S.dma("gpsimd", None, None, reads=[B_dest, B_tokid, B_zt], writes=[B_tokof],
                          fn=lambda e, col=ti * 2 + kk, ti=ti: e.indirect_dma_start(
                              out=tokof_d, out_offset=bass.IndirectOffsetOnAxis(ap=dest_all[:, col:col + 1], axis=0),
                              in_=tokid[:, 2 * ti:2 * ti + 2], in_offset=None, bounds_check=bcreg(e), oob_is_err=False))

        S.barrier()
        ar.top = moe_persist
        l3g = ar.alloc(1024, F32); l3b = ar.alloc(1024, F32); B_l3 = Buf("l3")
        k.dma("sync", l3g, ln3_g.partition_broadcast(128), writes=[B_l3])
        k.dma("sync", l3b, ln3_b.partition_broadcast(128), writes=[B_l3])
        moe_work = ar.top
        NB = 4
        idx = [ar.alloc(2, I32) for _ in range(NB)]; B_idx = [Buf(f"idx{i}") for i in range(NB)]
        Xe = [ar.alloc(2 * 1024, BF16).rearrange("p (s n) -> p s n", s=2) for _ in range(NB)]; B_Xe = [Buf(f"Xe{i}") for i in range(NB)]
        XeT = [ar.alloc(8 * 256, BF16).rearrange("p (c n) -> p c n", c=8) for _ in range(NB)]; B_XeT = [Buf(f"XeT{i}") for i in range(NB)]
        wg = [ar.alloc(8 * 256, BF16).rearrange("p (c n) -> p c n", c=8) for _ in range(NB)]; B_wg = [Buf(f"wg{i}") for i in range(NB)]
        wu = [ar.alloc(8 * 256, BF16).rearrange("p (c n) -> p c n", c=8) for _ in range(NB)]; B_wu = [Buf(f"wu{i}") for i in range(NB)]
        wd = [ar.alloc(2 * 1024, BF16).rearrange("p (c n) -> p c n", c=2) for _ in range(NB)]; B_wd = [Buf(f"wd{i}") for i in range(NB)]
        sg = [ar.alloc(256, F32) for _ in range(2)]; B_sg = [Buf("sga"), Buf("sgb")]
        aT = [ar.alloc(2 * 256, BF16).rearrange("p (c n) -> p c n", c=2) for _ in range(NB)]; B_aT = [Buf(f"aT{i}") for i in range(NB)]
        yb = [ar.alloc(1024, BF16) for _ in range(2)]; B_yb = [Buf("yba"), Buf("ybb")]
        B_yd = Buf("yd")
        def moe_loads(ex):
            b = ex % NB
            for s_ in range(2):
                k.dma("sync", idx[b][:, s_:s_ + 1], tokof_d[ex * 256 + s_ * 128:ex * 256 + (s_ + 1) * 128, 0:1], reads=[B_tokof], writes=[B_idx[b]], allow_slow_non_contiguous=True)
            for s_ in range(2):
                S.dma("gpsimd", None, None, reads=[B_idx[b]], writes=[B_Xe[b]],
                      fn=lambda e, b=b, s_=s_: e.indirect_dma_start(
                          out=Xe[b][:, s_, :], out_offset=None, in_=x2b_d,
                          in_offset=bass.IndirectOffsetOnAxis(ap=idx[b][:, s_:s_ + 1], axis=0)))
            k.dma("sync", wg[b].rearrange("p c n -> p (c n)"), weg_b[ex * 128:(ex + 1) * 128, :], reads=[B_wcv[ex][0]], writes=[B_wg[b]])
            k.dma("sync", wu[b].rearrange("p c n -> p (c n)"), weu_b[ex * 128:(ex + 1) * 128, :], reads=[B_wcv[ex][1]], writes=[B_wu[b]])
            k.dma("sync", wd[b].rearrange("p c n -> p (c n)"), wed_b[ex * 128:(ex + 1) * 128, :], reads=[B_wcv[ex][2]], writes=[B_wd[b]])

        def moe_compute(ex):
            b = ex % NB
            for s_ in range(2):
                bank = s_
                psb16 = ps[bank].bitcast(BF16)
                for c in range(8):
                    k.tr(psb16[:, c * 128:(c + 1) * 128], Xe[b][:, s_, c * 128:(c + 1) * 128], identb, [B_Xe[b], B_identb], [psB[bank]], signal=(c == 7))
                k.cp(XeT[b][:, :, s_ * 128:(s_ + 1) * 128], psb16.rearrange("p (c n) -> p c n", c=8), [psB[bank]], [B_XeT[b]])
            for fc in range(2):
                for c in range(8):
                    k.mm(ps[2 + fc][:, 0:256], wg[b][:, c, fc * 128:(fc + 1) * 128], XeT[b][:, c, :], c == 0, c == 7, [B_wg[b], B_XeT[b]], [psB[2 + fc]], signal=(c == 7))
                for c in range(8):
                    k.mm(ps[2 + fc][:, 256:512], wu[b][:, c, fc * 128:(fc + 1) * 128], XeT[b][:, c, :], c == 0, c == 7, [B_wu[b], B_XeT[b]], [psB[2 + fc]], signal=(c == 7))
                k.act(sg[fc], ps[2 + fc][:, 0:256], AF.Silu, [psB[2 + fc]], [B_sg[fc]])
                k.tt(aT[b][:, fc, :], sg[fc], ps[2 + fc][:, 256:512], ALU.mult, [B_sg[fc], psB[2 + fc]], [B_aT[b]])
            for s_ in range(2):
                yy = yb[s_]
                for half in range(2):
                    bank = 4 + s_ * 2 + half
                    for fc in range(2):
                        k.mm(ps[bank], aT[b][:, fc, s_ * 128:(s_ + 1) * 128], wd[b][:, fc, half * 512:(half + 1) * 512], fc == 0, fc == 1, [B_aT[b], B_wd[b]], [psB[bank]], signal=(fc == 1))
                    k.cp(yy[:, half * 512:(half + 1) * 512], ps[bank], [psB[bank]], [B_yb[s_]], eng=("vector" if half == 0 else "scalar"))
                k.dma("sync", yd_d[ex * 256 + s_ * 128:ex * 256 + (s_ + 1) * 128, :], yy, reads=[B_yb[s_]])

        PF = 2
        for ex in range(PF):
            moe_loads(ex)
        for ex in range(64):
            if ex + PF < 64:
                moe_loads(ex + PF)
            moe_compute(ex)
        S.barrier()
        ar.top = moe_work
        NC4 = 4
        yg = [[ar.alloc(1024, BF16) for _ in range(2)] for _ in range(NC4)]; B_yg = [[Buf(), Buf()] for _ in range(NC4)]
        xf = [ar.alloc(1024, F32) for _ in range(NC4)]; B_xf = [Buf() for _ in range(NC4)]
        h3 = [ar.alloc(1024, F32) for _ in range(NC4)]; B_h3 = [Buf() for _ in range(NC4)]
        st4 = ar.alloc(16, F32); B_st4 = Buf("st4")
        def cmb_loads(ti):
            pb = ti % NC4
            k.dma("sync", xf[pb], x2_d[ti * 128:(ti + 1) * 128, :], writes=[B_xf[pb]])
            for kk in range(2):
                k.memset(yg[pb][kk], 0.0, [B_yg[pb][kk]], eng="gpsimd")
                S.dma("gpsimd", None, None, reads=[B_dest, B_yd], writes=[B_yg[pb][kk]],
                      fn=lambda e, pb=pb, kk=kk, col=ti * 2 + kk: e.indirect_dma_start(
                          out=yg[pb][kk], out_offset=None, in_=yd_d,
                          in_offset=bass.IndirectOffsetOnAxis(ap=dest_all[:, col:col + 1], axis=0),
                          bounds_check=bcreg(e), oob_is_err=False))

        def cmb_compute(ti):
            pb = ti % NC4
            hh = h3[pb]; Bh = B_h3[pb]
            k.ts(hh, yg[pb][0], w_all[:, ti * 2:ti * 2 + 1], None, ALU.mult, None, [B_yg[pb][0], B_wall], [Bh])
            k.stt(hh, yg[pb][1], w_all[:, ti * 2 + 1:ti * 2 + 2], hh, ALU.mult, ALU.add, [B_yg[pb][1], B_wall, Bh], [Bh])
            k.stt(hh, xf[pb], ALPHA, hh, ALU.mult, ALU.add, [B_xf[pb], Bh], [Bh])
            layer_norm_tile(hh, Bh, hh, Bh, l3g, l3b, B_l3, st4, B_st4)
            k.dma("sync", out[ti * 128:(ti + 1) * 128, :], hh, reads=[Bh])

        for ti in range(2):
            cmb_loads(ti)
        for ti in range(NT):
            if ti + 2 < NT:
                cmb_loads(ti + 2)
            cmb_compute(ti)

    S.barrier()
    S.emit()
    return nc, in_names


def make_consts():
    c = np.zeros((128, 512), np.float32)
    c[:, 0:128] = np.eye(128, dtype=np.float32)
    c[:, 128:256] = np.triu(np.ones((128, 128), np.float32))
    half = 16
    inv_freq = (10000.0 ** (-np.arange(half, dtype=np.float32) / half)).astype(np.float32)
    for p in range(64, 96):
        j = (p - 64) % 16
        c[p, 256] = inv_freq[j]
        first = (p - 64) < 16
        c[p, 257] = -1.0 if first else 1.0
    c[:, 264:328] = np.arange(64, dtype=np.float32)[None, :]
    c[:, 328:456] = np.triu(np.ones((128, 128), np.float32), k=1)
    c[:, 456:488] = (np.arange(32, dtype=np.float32)[None, :] * 128 + np.arange(128, dtype=np.float32)[:, None])
    c[:, 260] = LN_EPS
    c[:, 261] = 384 * RMS_EPS
    c[:, 262] = 256 * RMS_EPS
    return c


_CACHE = {}


def kernel(**inputs):
    n = 8
    if "nc" not in _CACHE:
        _CACHE["nc"] = build_nc(stage=4)[0]
    nc = _CACHE["nc"]
    consts = make_consts()
    shared = {}
    for kname, v in inputs.items():
        if kname in ("x", "mem", "positions"):
            continue
        a = np.ascontiguousarray(np.asarray(v)[0])
        if a.ndim == 1 or kname == "gm_b_s":
            a = a.reshape(1, -1)
        elif a.ndim == 3:
            a = a.reshape(-1, a.shape[-1])
        shared[kname] = a
    shared["consts"] = consts
    in_maps = []
    for b in range(n):
        m = dict(shared)
        m["x"] = np.ascontiguousarray(np.asarray(inputs["x"])[b])
        m["mem"] = np.ascontiguousarray(np.asarray(inputs["mem"])[b])
        m["positions"] = np.ascontiguousarray(np.asarray(inputs["positions"])[b]).reshape(1, -1).astype(np.int32)
        in_maps.append(m)
    res = run_bass_kernel_spmd(nc, in_maps, core_ids=list(range(n)))
    return np.stack([np.asarray(r["out"]) for r in res.results], axis=0).astype(np.float32)
```
